# Optimizing a Trainium2 kernel written in Bass

```python
import jax, jax.numpy as jnp
from jax import lax
import numpy as np

D_MODEL = 1024
BATCH = 32
SEQ = 256
DEPTH = 1
DEC_BATCH = 4
DEC_SEQ = 2048
PAST_LEN = 512

GRID_W = 64
N_HEADS = 8
HEAD_K = 128
HEAD_V = 128
QK_DIM = N_HEADS * HEAD_K
V_DIM = N_HEADS * HEAD_V
QKV_DIM = 2 * QK_DIM + V_DIM
CHUNK = 64
SHORT_CONV = 5
CONV_DIM = 512
CONV_WIDTH = 31
FF_DIM = ((8 * D_MODEL // 3 + 255) // 256) * 256
IN_DIM = QKV_DIM + V_DIM + 4 * N_HEADS + 2 * CONV_DIM + 2 * D_MODEL
SPLITS = (QKV_DIM, QKV_DIM + V_DIM, QKV_DIM + V_DIM + 4 * N_HEADS,
          QKV_DIM + V_DIM + 4 * N_HEADS + 2 * CONV_DIM)
EPS = 1e-6

kernel_name = "hybrid_deltanet_conformer_diffusion_step"


def rmsnorm(x, g):
    xf = x.astype(jnp.float32)
    y = xf * lax.rsqrt(jnp.mean(xf * xf, axis=-1, keepdims=True) + EPS)
    return (y * g.astype(jnp.float32)).astype(x.dtype)


def layernorm(x, g, b):
    xf = x.astype(jnp.float32)
    mu = jnp.mean(xf, axis=-1, keepdims=True)
    xc = xf - mu
    var = jnp.mean(xc * xc, axis=-1, keepdims=True)
    return (xc * lax.rsqrt(var + EPS) * g.astype(jnp.float32) + b.astype(jnp.float32)).astype(x.dtype)


def l2norm(x):
    xf = x.astype(jnp.float32)
    return xf * lax.rsqrt(jnp.sum(xf * xf, axis=-1, keepdims=True) + EPS)


def dwconv_seq(x, w):
    K, C = w.shape
    return lax.conv_general_dilated(
        x, w.reshape(K, 1, C).astype(x.dtype), window_strides=(1,),
        padding=[((K - 1) // 2, K // 2)], dimension_numbers=('NWC', 'WIO', 'NWC'),
        feature_group_count=C)


def dwconv_axial(x, w):
    B, L, C = x.shape
    rows = L // GRID_W
    K = w.shape[0]
    p = K // 2
    half = C // 2
    xg = x.reshape(B, rows, GRID_W, C)
    dn = ('NHWC', 'HWIO', 'NHWC')
    yw = lax.conv_general_dilated(
        xg[..., :half], w[:, :half].reshape(1, K, 1, half).astype(x.dtype), (1, 1),
        [(0, 0), (p, p)], dimension_numbers=dn, feature_group_count=half)
    yh = lax.conv_general_dilated(
        xg[..., half:], w[:, half:].reshape(K, 1, 1, C - half).astype(x.dtype), (1, 1),
        [(p, p), (0, 0)], dimension_numbers=dn, feature_group_count=C - half)
    return jnp.concatenate([yw, yh], axis=-1).reshape(B, L, C)


def chunk_gated_delta(q, k, v, g, beta, s0):
    B, L, H, DK = q.shape
    DV = v.shape[-1]
    n = L // CHUNK

    def blocks(t):
        t = t.reshape((B, n, CHUNK, H) + t.shape[3:])
        return jnp.moveaxis(t, 3, 1)

    q, k, v, g, beta = blocks(q), blocks(k), blocks(v), blocks(g), blocks(beta)
    gc = jnp.cumsum(g, axis=-1)
    idx = jnp.arange(CHUNK)
    incl = idx[:, None] >= idx[None, :]
    strict = idx[:, None] > idx[None, :]
    decay = jnp.exp(jnp.where(incl, gc[..., :, None] - gc[..., None, :], -jnp.inf))
    kb = k * beta[..., None]
    a_mat = jnp.where(strict, jnp.einsum('bhnik,bhnjk->bhnij', kb, k) * decay, 0.0) \
        + jnp.eye(CHUNK, dtype=jnp.float32)
    rhs = jnp.concatenate([v * beta[..., None], kb * jnp.exp(gc)[..., None]], axis=-1)
    sol = lax.linalg.triangular_solve(a_mat, rhs, left_side=True, lower=True, unit_diagonal=True)
    u, w = sol[..., :DV], sol[..., DV:]
    attn = jnp.einsum('bhnik,bhnjk->bhnij', q, k) * decay
    q_dec = q * jnp.exp(gc)[..., None]
    k_dec = k * jnp.exp(gc[..., -1:] - gc)[..., None]
    g_last = jnp.exp(gc[..., -1])
    xs = tuple(jnp.moveaxis(t, 2, 0) for t in (q_dec, k_dec, u, w, attn, g_last))

    def step(s, inp):
        qd, kd, un, wn, an, gl = inp
        v_new = un - jnp.einsum('bhck,bhkv->bhcv', wn, s)
        o = jnp.einsum('bhck,bhkv->bhcv', qd, s) + jnp.einsum('bhij,bhjv->bhiv', an, v_new)
        s = s * gl[..., None, None] + jnp.einsum('bhck,bhcv->bhkv', kd, v_new)
        return s, o

    s_fin, o = lax.scan(step, s0, xs)
    o = jnp.moveaxis(o, 0, 2)
    o = jnp.moveaxis(o, 1, 3).reshape(B, L, H, DV)
    return o, s_fin


def bidir_delta(q, k, v, g, beta, s0):
    B = q.shape[0]
    flip = lambda t: jnp.flip(t, axis=1)
    qq = jnp.concatenate([q, flip(q)], axis=0)
    kk = jnp.concatenate([k, flip(k)], axis=0)
    vv = jnp.concatenate([v, flip(v)], axis=0)
    gg = jnp.concatenate([g[:, :, 0], flip(g[:, :, 1])], axis=0)
    bb = jnp.concatenate([beta[:, :, 0], flip(beta[:, :, 1])], axis=0)
    ss = jnp.concatenate([s0[:, 0], s0[:, 1]], axis=0)
    o, s_fin = chunk_gated_delta(qq, kk, vv, gg, bb, ss)
    o = o[:B] + flip(o[B:])
    s_fin = jnp.stack([s_fin[:B], s_fin[B:]], axis=1)
    return o, s_fin


def adaln(cvec, w_ada, b_ada):
    m = jax.nn.silu(cvec) @ w_ada + b_ada
    return jnp.split(m[:, None, :], 6, axis=-1)


def trunk_layer(x, mod, s0, grid, norm1, w_in, conv_qkv, a_log, dt_bias, norm_o, w_a_out,
                conv_dw, b_dw, ln_g, ln_b, w_b_out, w_o, norm2, w_gu, w_down):
    sh1, sc1, g1, sh2, sc2, g2 = mod
    B, L, _ = x.shape
    h = rmsnorm(x, norm1) * (1 + sc1) + sh1
    p = h @ w_in
    qkv, z, ab, glu, gates = jnp.split(p, SPLITS, axis=-1)

    qkv = jax.nn.silu(dwconv_seq(qkv, conv_qkv))
    q, k, v = jnp.split(qkv, [QK_DIM, 2 * QK_DIM], axis=-1)
    q = l2norm(q.reshape(B, L, N_HEADS, HEAD_K)) * (HEAD_K ** -0.5)
    k = l2norm(k.reshape(B, L, N_HEADS, HEAD_K))
    v = v.reshape(B, L, N_HEADS, HEAD_V).astype(jnp.float32)
    ab = ab.astype(jnp.float32).reshape(B, L, 2, 2, N_HEADS)
    g = -jnp.exp(a_log.astype(jnp.float32)) * jax.nn.softplus(ab[:, :, 0] + dt_bias.astype(jnp.float32))
    beta = jax.nn.sigmoid(ab[:, :, 1])
    o, s_fin = bidir_delta(q, k, v, g, beta, s0)
    o = rmsnorm(o, norm_o) * jax.nn.silu(z.reshape(B, L, N_HEADS, HEAD_V).astype(jnp.float32))
    y_a = o.reshape(B, L, V_DIM).astype(x.dtype) @ w_a_out

    u = glu[..., :CONV_DIM] * jax.nn.sigmoid(glu[..., CONV_DIM:])
    u = (dwconv_axial(u, conv_dw) if grid else dwconv_seq(u, conv_dw)) + b_dw
    y_b = jax.nn.silu(layernorm(u, ln_g, ln_b)) @ w_b_out

    ga, gb = jnp.split(jax.nn.sigmoid(gates), 2, axis=-1)
    x = x + g1 * ((ga * y_a + gb * y_b) @ w_o)

    h2 = rmsnorm(x, norm2) * (1 + sc2) + sh2
    gu = h2 @ w_gu
    x = x + g2 * ((jax.nn.silu(gu[..., :FF_DIM]) * gu[..., FF_DIM:]) @ w_down)
    return x, s_fin


def setup_inputs(seed: int = 0) -> dict:
    key = jax.random.key(seed)
    ks = jax.random.split(key, 26)
    nrm = lambda k, shape, s: jax.random.normal(k, shape, jnp.float32) * s
    dt = jnp.exp(jax.random.uniform(ks[9], (DEPTH, 2, N_HEADS), jnp.float32,
                                    np.log(0.001), np.log(0.1)))
    return {
        "x_prompt": nrm(ks[0], (BATCH, SEQ, D_MODEL), 1.0),
        "x_sample": nrm(ks[1], (DEC_BATCH, DEC_SEQ, D_MODEL), 1.0),
        "state_delta": nrm(ks[2], (DEC_BATCH, DEPTH, 2, N_HEADS, HEAD_K, HEAD_V), 0.1),
        "c": nrm(ks[3], (DEC_BATCH, D_MODEL), 1.0),
        "c_ctx": nrm(ks[4], (D_MODEL,), 1.0),
        "w_ada": nrm(ks[5], (DEPTH, D_MODEL, 6 * D_MODEL), D_MODEL ** -0.5),
        "b_ada": nrm(ks[6], (DEPTH, 6 * D_MODEL), 0.01),
        "norm1": 1.0 + nrm(ks[7], (DEPTH, D_MODEL), 0.01),
        "w_in": nrm(ks[8], (DEPTH, D_MODEL, IN_DIM), D_MODEL ** -0.5),
        "conv_qkv": nrm(ks[10], (DEPTH, SHORT_CONV, QKV_DIM), SHORT_CONV ** -0.5),
        "a_log": jnp.log(jax.random.uniform(ks[11], (DEPTH, 2, N_HEADS), jnp.float32, 1.0, 16.0)),
        "dt_bias": dt + jnp.log(-jnp.expm1(-dt)),
        "norm_o": 1.0 + nrm(ks[12], (DEPTH, HEAD_V), 0.01),
        "w_a_out": nrm(ks[13], (DEPTH, V_DIM, D_MODEL), V_DIM ** -0.5),
        "conv_dw": nrm(ks[14], (DEPTH, CONV_WIDTH, CONV_DIM), CONV_WIDTH ** -0.5),
        "b_dw": nrm(ks[15], (DEPTH, CONV_DIM), 0.01),
        "ln_g": 1.0 + nrm(ks[16], (DEPTH, CONV_DIM), 0.01),
        "ln_b": nrm(ks[17], (DEPTH, CONV_DIM), 0.01),
        "w_b_out": nrm(ks[18], (DEPTH, CONV_DIM, D_MODEL), CONV_DIM ** -0.5),
        "w_o": nrm(ks[19], (DEPTH, D_MODEL, D_MODEL), D_MODEL ** -0.5),
        "norm2": 1.0 + nrm(ks[20], (DEPTH, D_MODEL), 0.01),
        "w_gu": nrm(ks[21], (DEPTH, D_MODEL, 2 * FF_DIM), D_MODEL ** -0.5),
        "w_down": nrm(ks[22], (DEPTH, FF_DIM, D_MODEL), FF_DIM ** -0.5),
        "norm_f": 1.0 + nrm(ks[23], (D_MODEL,), 0.01),
    }


def reference(x_prompt, x_sample, state_delta, c, c_ctx, w_ada, b_ada, norm1, w_in, conv_qkv,
              a_log, dt_bias, norm_o, w_a_out, conv_dw, b_dw, ln_g, ln_b, w_b_out, w_o, norm2,
              w_gu, w_down, norm_f):
    xp = x_prompt
    xs = x_sample
    s_zero = jnp.zeros((x_prompt.shape[0], 2, N_HEADS, HEAD_K, HEAD_V), jnp.float32)
    ctx_states = []
    for l in range(DEPTH):
        lw = (norm1[l], w_in[l], conv_qkv[l], a_log[l], dt_bias[l], norm_o[l], w_a_out[l],
              conv_dw[l], b_dw[l], ln_g[l], ln_b[l], w_b_out[l], w_o[l], norm2[l], w_gu[l],
              w_down[l])
        xp, s_ctx = trunk_layer(xp, adaln(c_ctx[None, :], w_ada[l], b_ada[l]), s_zero, False, *lw)
        ctx_states.append(s_ctx)
        xs, _ = trunk_layer(xs, adaln(c, w_ada[l], b_ada[l]),
                            state_delta[:, l].astype(jnp.float32), True, *lw)
    y_prompt = rmsnorm(xp, norm_f)
    y_sample = rmsnorm(xs, norm_f)
    new_state_delta = jnp.stack(ctx_states, axis=1)
    return (y_prompt, y_sample, new_state_delta)
```

```python
import numpy as np
from contextlib import ExitStack
import concourse.bass as bass
import concourse.mybir as mybir
from concourse.bass_utils import run_bass_kernel_spmd

F32 = mybir.dt.float32
BF16 = mybir.dt.bfloat16
ALU = mybir.AluOpType
AF = mybir.ActivationFunctionType
AX = mybir.AxisListType

ENGINES = ("pe", "act", "dve", "pool", "sp")
NDMA_SEMS = 8
EPS = 1e-6
FF = 2816
GROUPS = [[0, 1], [2, 3], [4, 5], [6, 7]]
NWT = 40
WT_KEYS = []


PSUM_NAMES = ("pA", "pB", "pC", "pC0", "pC1", "pS", "pT")


class StopBuild(Exception):
    pass


class Prog:
    def __init__(self, nc):
        self.nc = nc
        self.ops = {e: [] for e in ENGINES}
        self.last_writer = {}
        self.readers = {}
        self.ndma = {e: 0 for e in ENGINES}
        self.nbar = 0

    def op(self, eng, fn, reads=(), writes=(), dma=False, drain=False, selfsig=False):
        deps = set()
        for r in reads:
            lw = self.last_writer.get(r)
            if lw is not None:
                deps.add(lw)
            if r in PSUM_NAMES:
                for rd in self.readers.get(r, ()):
                    if rd[0] != eng:
                        deps.add(rd)
        for w in writes:
            lw = self.last_writer.get(w)
            if lw is not None:
                deps.add(lw)
            for rd in self.readers.get(w, ()):
                deps.add(rd)
        idx = len(self.ops[eng])
        me = (eng, idx)
        deps.discard(me)
        best = {}
        for (pe_, pi_) in deps:
            if self.ops[pe_][pi_]["dma"]:
                continue
            if pi_ > best.get(pe_, -1):
                best[pe_] = pi_
        deps = set(d_ for d_ in deps if self.ops[d_[0]][d_[1]]["dma"] or best[d_[0]] == d_[1])
        rec = dict(fn=fn, deps=deps, dma=dma, signal=selfsig, dma_idx=None, drain=drain,
                   ndma_before=self.ndma[eng], selfsig=selfsig)
        if dma:
            rec["dma_idx"] = self.ndma[eng]
            self.ndma[eng] += 1
        self.ops[eng].append(rec)
        for w in writes:
            self.last_writer[w] = me
            self.readers[w] = []
        for r in reads:
            self.readers.setdefault(r, []).append(me)
        return me

    def pe(self, fn, reads=(), writes=()):
        return self.op("pe", fn, reads, writes)

    def act(self, fn, reads=(), writes=()):
        return self.op("act", fn, reads, writes)

    def dve(self, fn, reads=(), writes=()):
        return self.op("dve", fn, reads, writes)

    def pool(self, fn, reads=(), writes=()):
        return self.op("pool", fn, reads, writes)

    def dma(self, fn, reads=(), writes=(), q="sp"):
        return self.op(q, fn, reads, writes, dma=True)

    def barrier(self):
        self.nbar += 1
        names = []
        for e in ENGINES:
            nm = "__bar%d_%s" % (self.nbar, e)
            last = len(self.ops[e]) - 1
            me = self.op(e, None, writes=[nm], drain=True, selfsig=True)
            if last >= 0:
                self.ops[e][me[1]]["deps"].add((e, last))
                self.ops[e][me[1]]["selfdep"] = True
            names.append(nm)
        for e in ENGINES:
            self.op(e, None, reads=names, selfsig=True)
        self.last_writer = {}
        self.readers = {}

    def emit(self, sems, dma_sems):
        for e in ENGINES:
            for rec in self.ops[e]:
                for (pe_, pi_) in rec["deps"]:
                    p = self.ops[pe_][pi_]
                    if p["dma"]:
                        continue
                    if pe_ == e and e == "pe" and not rec.get("selfdep"):
                        continue
                    p["signal"] = True
        cum = {}
        for e in ENGINES:
            c = 0
            arr = []
            for rec in self.ops[e]:
                if rec["signal"] and not rec["dma"]:
                    c += 1
                arr.append(c)
            cum[e] = arr
        prog = self

        def run_engine(e, eng):
            waited = {}
            nsig = [0]

            def wait(key, sem, val):
                if val <= 0 or waited.get(key, 0) >= val:
                    return
                waited[key] = val
                eng.wait_ge(sem, val)

            def drain_dmas(n):
                for k in range(min(n, NDMA_SEMS)):
                    cnt = (n - 1 - k) // NDMA_SEMS + 1
                    wait((e, k), dma_sems[e][k], 16 * cnt)

            for rec in prog.ops[e]:
                for (pe_, pi_) in sorted(rec["deps"]):
                    p = prog.ops[pe_][pi_]
                    if p["dma"]:
                        di = p["dma_idx"]
                        s = dma_sems[pe_][di % NDMA_SEMS]
                        wait((pe_, di % NDMA_SEMS), s, 16 * (di // NDMA_SEMS + 1))
                    else:
                        if pe_ == e and e == "pe" and not rec.get("selfdep"):
                            continue
                        wait(pe_, sems[pe_], cum[pe_][pi_])
                if e == "pool" and rec["signal"] and not rec["dma"] and nsig[0] > 0:
                    wait(e, sems[e], nsig[0])
                if rec["signal"] and not rec["dma"]:
                    nsig[0] += 1
                if rec["drain"]:
                    drain_dmas(rec["ndma_before"])
                if rec["dma"]:
                    di = rec["dma_idx"]
                    s = dma_sems[e][di % NDMA_SEMS]
                    if di >= NDMA_SEMS:
                        wait((e, di % NDMA_SEMS), s, 16 * (di // NDMA_SEMS))
                    rec["fn"](eng).then_inc(s, 16)
                elif rec["selfsig"]:
                    if rec["fn"] is not None:
                        rec["fn"](eng)
                    eng.nop(nofuse=True).then_inc(sems[e], 1)
                else:
                    ins = rec["fn"](eng)
                    if rec["signal"]:
                        ins.then_inc(sems[e], 1)
            drain_dmas(prog.ndma[e])

        return run_engine


def bc1(ap, n):
    return ap.unsqueeze(2).to_broadcast([ap.shape[0], ap.shape[1], n])


def bcm(ap, n):
    return ap.unsqueeze(1).to_broadcast([ap.shape[0], n, ap.shape[1]])


def v3(ap, a):
    return ap.rearrange("p (a b) -> p a b", a=a)


HEADS = None
SEQ_D4 = False
DORDER = (0, 1)


def build_program(enable_S=True, stop=None, stats=False):
    nc = bass.Bass("TRN2", target_bir_lowering=False)

    def din(name, shape):
        return nc.dram_tensor(name, shape, F32, kind="ExternalInput").ap()

    def dout(name, shape):
        return nc.dram_tensor(name, shape, F32, kind="ExternalOutput").ap()

    xs = din("xs", [2048, 1024])
    xp = din("xp", [1024, 1024])
    cT = din("cT", [128, 16])
    w_ada = din("w_ada", [1024, 6144])
    b_ada_fm = din("b_ada_fm", [128, 48])
    wpack = din("wpack", [NWT, 128, 4096])
    w_ab = din("w_ab", [2, 1024, 32])
    cw = din("cw", [2, 128, 120])
    gpar = din("gpar", [2, 2, 16])
    cdw = din("cdw", [2, 128, 124])
    cpar = din("cpar", [128, 12])
    npar = din("npar", [128, 16])
    norm_o = din("norm_o", [1, 128])
    norm_f = din("norm_f", [1, 1024])
    s0 = din("s0", [8, 128, 128])
    sel = din("sel", [128, 2])
    masks = din("masks", [5, 128, 128])
    yp = dout("yp", [1024, 1024])
    ys = dout("ys", [1024, 1024])
    st = dout("st", [4, 2, 8, 128, 128])
    cc_in = [nc.dram_tensor("cc_in%d" % h, [128, 128], F32).ap() for h in range(8)]
    cc_out = [nc.dram_tensor("cc_out%d" % h, [256, 128], F32).ap() for h in range(8)]

    P = Prog(nc)
    es = ExitStack()

    def sb(name, shape, dt=F32, stack=es):
        return stack.enter_context(nc.sbuf_tensor(name, shape, dt))

    def psum(name, shape, dt=F32):
        return es.enter_context(nc.psum_tensor(name, shape, dt))

    sems = {e: es.enter_context(nc.semaphore("s_" + e)) for e in ("pe", "act", "dve", "pool", "sp")}
    s_cc = es.enter_context(nc.semaphore("s_cc"))
    dma_sems = {"sp": [nc.alloc_semaphore(name="dq%d" % i) for i in range(NDMA_SEMS)],
                "pool": [nc.alloc_semaphore(name="dp%d" % i) for i in range(NDMA_SEMS)]}

    AR = 159 * 256
    arena_t = sb("arena", [128, AR], F32)

    class Arena:
        def __init__(self):
            self.top = 0

        def at(self, off, nelem, dt=F32):
            nfl = nelem if dt == F32 else (nelem + 1) // 2
            assert off + nfl <= AR, (off, nfl, AR)
            ap = arena_t[:, off:off + nfl]
            return ap.bitcast(BF16) if dt == BF16 else ap

        def alloc(self, nelem, dt=F32):
            nfl = nelem if dt == F32 else (nelem + 1) // 2
            off = self.top
            self.top += nfl
            return self.at(off, nelem, dt)

    AN = Arena()

    pA = psum("pA", [128, 1024])
    pB = psum("pB", [128, 1024])
    pC = psum("pC", [128, 1024])
    pS = psum("pS", [128, 512])
    pT = psum("pT", [128, 1024], BF16)

    identb = sb("identb", [128, 128], BF16)
    identf = sb("identf", [128, 128])
    onesf = sb("onesf", [128, 128])
    onesb = sb("onesb", [128, 128], BF16)
    triU = sb("triU", [128, 128])
    triL = sb("triL", [128, 128])
    sL = sb("sL", [128, 128])
    sU = sb("sU", [128, 128])
    epsb = sb("epsb", [128, 1])
    qbias = sb("qbias", [128, 1])
    zero1 = sb("zero1", [128, 1])

    def mk_mask(t, name, op, sgn=1):
        P.pool(lambda e: e.memset(t[:], 1.0), writes=[name])
        P.pool(lambda e: e.affine_select(out=t[:], in_=t[:], pattern=[[-sgn, 128]], compare_op=op, fill=0.0,
                                         base=0, channel_multiplier=sgn), reads=[name], writes=[name])

    mk_mask(identf, "identf", ALU.is_equal)
    mk_mask(triU, "triU", ALU.is_ge, -1)
    mk_mask(triL, "triL", ALU.is_ge, 1)
    mk_mask(sL, "sL", ALU.is_gt, 1)
    mk_mask(sU, "sU", ALU.is_gt, -1)
    P.pool(lambda e: e.memset(onesf[:], 1.0), writes=["onesf"])
    P.pool(lambda e: e.memset(onesb[:], 1.0), writes=["onesb"])
    P.pool(lambda e: e.memset(epsb[:], EPS), writes=["epsb"])
    P.pool(lambda e: e.memset(qbias[:], float(-0.5 * np.log(128.0))), writes=["qbias"])
    P.pool(lambda e: e.memset(zero1[:], 0.0), writes=["zero1"])
    P.pool(lambda e: e.tensor_copy(out=identb[:], in_=identf[:]), reads=["identf"], writes=["identb"])
    TRI = [triU, triL]
    TRIN = ["triU", "triL"]
    STR = [sL, sU]
    STRN = ["sL", "sU"]

    cTt = sb("cTt", [128, 16])
    scT = sb("scT", [128, 16])
    badaf = sb("badaf", [128, 48])
    modT = sb("modT", [128, 96])
    cwt = sb("cwt", [128, 240])
    cdwt = sb("cdwt", [128, 248])
    cpart = sb("cpart", [128, 12])
    npart = sb("npart", [128, 16])
    normo_bc = sb("normo_bc", [128, 128])
    gpt = sb("gpt", [128, 64])
    nea = sb("nea", [128, 32])
    selt = sb("selt", [128, 2])
    mkt = sb("mkt", [128, 5 * 128], BF16)
    mk = [mkt[:, i * 128:(i + 1) * 128] for i in range(5)]
    sc1t = sb("sc1t", [128, 16])
    sc2t = sb("sc2t", [128, 16])
    gbc = sb("gbc", [128, 4096])

    P.dma(lambda e: e.dma_start(out=cTt[:], in_=cT), writes=["cTt"])
    P.dma(lambda e: e.dma_start(out=badaf[:], in_=b_ada_fm), writes=["badaf"])
    P.dma(lambda e: e.dma_start(out=v3(cwt[:], 2), in_=cw.rearrange("j p f -> p j f")), writes=["cwt"])
    P.dma(lambda e: e.dma_start(out=v3(cdwt[:], 2), in_=cdw.rearrange("j p f -> p j f")), writes=["cdwt"])
    P.dma(lambda e: e.dma_start(out=cpart[:], in_=cpar), writes=["cpart"])
    P.dma(lambda e: e.dma_start(out=npart[:], in_=npar), writes=["npart"])
    P.dma(lambda e: e.dma_start(out=normo_bc[:], in_=norm_o.partition_broadcast(128)), writes=["normo_bc"])
    P.dma(lambda e: e.dma_start(out=gpt[:], in_=gpar.rearrange("j a b -> (j a b)").partition_broadcast(128)),
          writes=["gpt"])
    P.dma(lambda e: e.dma_start(out=selt[:], in_=sel), writes=["selt"])
    P.dma(lambda e: e.dma_start(out=v3(mkt[:], 5), in_=masks.rearrange("m p f -> p m f")), writes=["mkt"], q="pool")
    for j in range(2):
        P.act(lambda e, j=j: e.activation(out=nea[:, j * 16:(j + 1) * 16], in_=gpt[:, j * 32:j * 32 + 16], func=AF.Exp),
              reads=["gpt"], writes=["nea"])
    P.dve(lambda e: e.tensor_scalar(out=nea[:], in0=nea[:], scalar1=-1.0, scalar2=None, op0=ALU.mult),
          reads=["nea"], writes=["nea"])

    SKIP_ADA = (stop == 'const')
    NWB = 3
    wbuf = [sb("wb%d" % i, [128, 8 * 512], BF16) for i in range(NWB)]
    wstate = {"n": 0}

    WT = {}

    def wkey(name, r0, nr, c0, ncw):
        return (name, r0, nr, c0, ncw)

    def load_w(key, kc, ncols):
        if key not in WT:
            WT[key] = len(WT)
            assert len(WT) <= NWT
        assert key[2] == kc * 128 and key[4] == ncols
        src3 = wpack[WT[key]][:, 0:kc * ncols].rearrange("p (k n) -> p k n", k=kc)
        i = wstate["n"] % NWB
        wstate["n"] += 1
        nm = "wb%d" % i
        view = wbuf[i][:, 0:kc * ncols].rearrange("p (k n) -> p k n", k=kc)
        P.dma(lambda e: e.dma_start(out=view, in_=src3), writes=[nm], q="pool")
        return view, nm

    def wview(w, r0, nr, c0, ncw):
        return w[r0:r0 + nr, c0:c0 + ncw].rearrange("(k p) n -> p k n", p=128)

    if not SKIP_ADA:
        AN.top = 0
        wa = [AN.alloc(8 * 512) for i in range(2)]
        P.act(lambda e: e.activation(out=scT[:], in_=cTt[:], func=AF.Silu), reads=["cTt"], writes=["scT"])
        scT3 = v3(scT[:], 8)
        for g in range(12):
            wt = wa[g % 2]
            nm = "wa%d" % (g % 2)
            wt3 = wt[:].rearrange("p (k n) -> p k n", k=8)
            P.dma(lambda e, g=g, wt3=wt3: e.dma_start(out=wt3, in_=wview(w_ada, 0, 1024, g * 512, 512)), writes=[nm])
            for jj in range(4):
                j = g * 4 + jj
                for dc in range(8):
                    P.pe(lambda e, j=j, jj=jj, dc=dc, wt3=wt3: e.matmul(
                        pS[:, 2 * j:2 * j + 2], lhsT=wt3[:, dc, jj * 128:(jj + 1) * 128], rhs=scT3[:, dc, :],
                        start=(dc == 0), stop=(dc == 7)), reads=[nm, "scT"], writes=["pS"])
        P.dve(lambda e: e.tensor_tensor(out=v3(modT[:], 48), in0=v3(pS[:, 0:96], 48), in1=bc1(badaf[:], 2), op=ALU.add),
              reads=["pS", "badaf"], writes=["modT"])
        modT3 = v3(modT[:], 48)
        ADA1 = (stop == 'ada1')
        for job in range(0 if ADA1 else 2):
            for (dst, dn, c0, nrow) in ((sc1t, "sc1t", 8, 0), (sc2t, "sc2t", 32, 1)):
                P.dve(lambda e, dst=dst, c0=c0, nrow=nrow, job=job: e.scalar_tensor_tensor(
                    out=dst[:, job * 8:(job + 1) * 8], in0=modT3[:, c0:c0 + 8, job], scalar=1.0,
                    in1=npart[:, nrow * 8:(nrow + 1) * 8], op0=ALU.add, op1=ALU.mult),
                    reads=["modT", "npart"], writes=[dn])
        dg = AN.alloc(1024)
        for job in range(0 if (ADA1 or stop == 'ada2') else 2):
            for which, c0 in ((0, 16), (1, 40)):
                for c in range(8):
                    P.dve(lambda e, c=c, c0=c0, job=job: e.tensor_scalar(
                        out=dg[:, c * 128:(c + 1) * 128], in0=identf[:], scalar1=modT3[:, c0 + c, job:job + 1],
                        scalar2=None, op0=ALU.mult), reads=["identf", "modT"], writes=["dg"])
                for c in range(8):
                    P.pe(lambda e, c=c: e.matmul(pA[:, c * 128:(c + 1) * 128], lhsT=onesf[:], rhs=dg[:, c * 128:(c + 1) * 128],
                                                 start=True, stop=True), reads=["onesf", "dg"], writes=["pA"])
                off = (job * 2 + which) * 1024
                for hf in range(2):
                    P.act(lambda e, off=off, hf=hf: e.copy(out=gbc[:, off + hf * 512:off + (hf + 1) * 512], in_=pA[:, hf * 512:(hf + 1) * 512]), reads=["pA"], writes=["gbc"])
    P.barrier()

    def run_job(job):
        isS = (job == 1)
        xin = xs if isS else xp
        yout = ys if isS else yp
        nseq = 1 if isS else 4
        AN.top = 0
        xres = AN.alloc(8 * 1024)
        hT = AN.alloc(8 * 1024, BF16)
        hT3 = hT[:].rearrange("p (k n) -> p k n", k=8)
        mB_off = AN.top
        mB = AN.alloc(8 * 1024, BF16)
        mB3 = mB[:].rearrange("p (k n) -> p k n", k=8)
        ogT_off = AN.top
        ogT = AN.alloc(8 * 1024, BF16)
        ogT3 = ogT[:].rearrange("p (k n) -> p k n", k=8)
        stat = AN.alloc(64)
        hT2 = AN.alloc(16, BF16)
        T0 = AN.top
        xres3 = xres[:].rearrange("p (t n) -> p t n", t=8)
        g1bc = gbc[:, (job * 2) * 1024:(job * 2 + 1) * 1024]
        g2bc = gbc[:, (job * 2 + 1) * 1024:(job * 2 + 2) * 1024]

        def norm_transpose(src_ap, srcname, dst3, dstname, t, sct, shc0, ph, tag, part=0):
            junk = ph["junk"]
            xn = ph["xn"][t % 2]
            xnn = "xn%d" % (t % 2)
            tmpf = ph["tmpf"]
            col = stat[:, (t % 16) * 3:(t % 16) * 3 + 3]
            if part in (0, 1):
                P.act(lambda e: e.activation(out=junk[:], in_=src_ap, func=AF.Square, accum_out=col[:, 0:1]),
                      reads=[srcname], writes=["junk", "stat"])
                P.act(lambda e: e.activation(out=col[:, 1:2], in_=col[:, 0:1], func=AF.Ln, scale=1.0 / 1024, bias=epsb[:, 0:1]),
                      reads=["stat", "epsb"], writes=["stat"])
                P.act(lambda e: e.activation(out=col[:, 2:3], in_=col[:, 1:2], func=AF.Exp, scale=-0.5),
                      reads=["stat"], writes=["stat"])
                P.dve(lambda e: e.tensor_scalar(out=xn[:], in0=src_ap, scalar1=col[:, 2:3], scalar2=None, op0=ALU.mult),
                      reads=[srcname, "stat"], writes=[xnn])
            if part == 1:
                return
            for dc in range(8):
                P.pe(lambda e, dc=dc: e.transpose(pT[:, dc * 128:(dc + 1) * 128], xn[:, dc * 128:(dc + 1) * 128], identb[:]),
                     reads=[xnn, "identb"], writes=["pT"])
            P.dve(lambda e: e.tensor_tensor(out=v3(tmpf[:], 8), in0=v3(pT[:], 8), in1=bc1(sct, 128), op=ALU.mult),
                  reads=["pT", "sc1t", "sc2t"], writes=["tmpf"])
            (P.dve if tag == "f" else P.pool)(lambda e: e.tensor_tensor(out=dst3, in0=v3(tmpf[:], 8),
                                             in1=bc1(modT3[:, shc0:shc0 + 8, job], 128), op=ALU.add),
                   reads=["tmpf", "modT"], writes=[dstname])

        if True:
            AN.top = T0
            phtA = {"junk": AN.alloc(1024, BF16),
                   "xn": [AN.alloc(1024, BF16) for i in range(2)],
                   "tmpf": AN.alloc(1024)}
            xh = [AN.alloc(1024) for i in range(2)]
            hTh = AN.at(mB_off, 8 * 1024, BF16) if isS else None
            hTh3 = hTh[:].rearrange("p (k n) -> p k n", k=8) if isS else None
            for t in range(8):
                P.dma(lambda e, t=t: e.dma_start(out=xres3[:, t, :], in_=xin[t * 128:(t + 1) * 128, :]),
                      writes=["xres"])
                norm_transpose(xres3[:, t, :], "xres", hT3[:, :, t * 128:(t + 1) * 128], "hT", t,
                               sc1t[:, job * 8:(job + 1) * 8], 0, phtA, "a")
            if isS:
                for t in range(8):
                    xb_ = xh[t % 2]
                    xbn = "xh%d" % (t % 2)
                    P.dma(lambda e, t=t, xb_=xb_: e.dma_start(out=xb_[:], in_=xin[1024 + t * 128:1024 + (t + 1) * 128, :]),
                          writes=[xbn])
                    norm_transpose(xb_[:], xbn, hTh3[:, :, t * 128:(t + 1) * 128], "mB", 8 + t,
                                   sc1t[:, job * 8:(job + 1) * 8], 0, phtA, "h")
                P.pool(lambda e: e.tensor_copy(out=v3(hT2[:], 8), in_=hTh3[:, :, 0:2]), reads=["mB"], writes=["hT2"])
            else:
                P.pool(lambda e: e.memset(hT2[:], 0.0), writes=["hT2"])
            if stop == 'A':
                raise StopBuild()


            upads = [AN.alloc(2944), AN.alloc(2944)]
            cvo = AN.alloc(4 * 1024)
            cvo3 = cvo[:].rearrange("p (c n) -> p c n", c=4)
            ub = AN.at(ogT_off, 4 * 1024, BF16)
            ub3 = ub[:].rearrange("p (c n) -> p c n", c=4)
            sgtC = AN.alloc(1024)
            lnt = [AN.alloc(512) for i in range(4)]
            wa_v, wa_n = load_w(wkey("w_glu", 0, 1024, 0, 512), 8, 512)
            wb_v, wb_n = load_w(wkey("w_glu", 0, 1024, 512, 512), 8, 512)
            cdw3 = cdwt[:, job * 124:(job + 1) * 124].rearrange("p (c k) -> p c k", c=4)
            def gen_glu(cc):
                up = upads[cc % 2]
                upn = "upad%d" % (cc % 2)
                P.pool(lambda e, up=up: e.memset(up[:], 0.0), writes=[upn])
                passes = [(hT3, "hT", hf, False) for hf in range(2)]
                if isS and cc >= 2:
                    passes += [(hTh3, "mB", hf, True) for hf in range(2)]
                for (src3, srcn, hf, halo) in passes:
                    for dc in range(8):
                        P.pe(lambda e, dc=dc, src3=src3, hf=hf, cc=cc: e.matmul(
                            pA[:, 0:512], lhsT=wa_v[:, dc, cc * 128:(cc + 1) * 128], rhs=src3[:, dc, hf * 512:(hf + 1) * 512],
                            start=(dc == 0), stop=(dc == 7)), reads=[wa_n, srcn], writes=["pA"])
                    for dc in range(8):
                        P.pe(lambda e, dc=dc, src3=src3, hf=hf, cc=cc: e.matmul(
                            pB[:, 0:512], lhsT=wb_v[:, dc, cc * 128:(cc + 1) * 128], rhs=src3[:, dc, hf * 512:(hf + 1) * 512],
                            start=(dc == 0), stop=(dc == 7)), reads=[wb_n, srcn], writes=["pB"])
                    P.act(lambda e: e.activation(out=sgtC[:, 0:512], in_=pB[:, 0:512], func=AF.Sigmoid),
                          reads=["pB"], writes=["sgtC"])
                    if not isS:
                        dstv = up[:, 0:4 * 286].rearrange("p (s w) -> p s w", s=4)[:, 2 * hf:2 * hf + 2, 15:271]
                        inA = v3(pA[:, 0:512], 2)
                        inS = v3(sgtC[:, 0:512], 2)
                    elif cc < 2:
                        dstv = up[:, 0:16 * 94].rearrange("p (s w) -> p s w", s=16)[:, 8 * hf:8 * hf + 8, 15:79]
                        inA = v3(pA[:, 0:512], 8)
                        inS = v3(sgtC[:, 0:512], 8)
                    else:
                        r0 = (31 + 8 * hf) if halo else (15 + 8 * hf)
                        n = 448 if (halo and hf == 1) else 512
                        dstv = up[:, r0 * 64:r0 * 64 + n]
                        inA = pA[:, 0:n]
                        inS = sgtC[:, 0:n]
                    P.dve(lambda e, dstv=dstv, inA=inA, inS=inS: e.tensor_tensor(out=dstv, in0=inA, in1=inS, op=ALU.mult),
                          reads=["pA", "sgtC"], writes=[upn])
                    yield

            def gen_conv(cc):
                up = upads[cc % 2]
                upn = "upad%d" % (cc % 2)
                if not isS:
                    def iv(j, up=up):
                        return up[:, 0:4 * 286].rearrange("p (s w) -> p s w", s=4)[:, :, j:j + 256]
                    ov = v3(cvo3[:, cc, :], 4)
                elif cc < 2:
                    def iv(j, up=up):
                        return up[:, 0:16 * 94].rearrange("p (s w) -> p s w", s=16)[:, :, j:j + 64]
                    ov = v3(cvo3[:, cc, :], 16)
                else:
                    def iv(j, up=up):
                        return up[:, j * 64:j * 64 + 1024]
                    ov = cvo3[:, cc, :]
                P.dve(lambda e, ov=ov, iv=iv, cc=cc: e.tensor_scalar(
                    out=ov, in0=iv(0), scalar1=cdw3[:, cc, 0:1], scalar2=cpart[:, cc:cc + 1], op0=ALU.mult, op1=ALU.add),
                    reads=[upn, "cdwt", "cpart"], writes=["cvo"])
                for j in range(1, 31):
                    P.dve(lambda e, ov=ov, iv=iv, cc=cc, j=j: e.scalar_tensor_tensor(
                        out=ov, in0=iv(j), scalar=cdw3[:, cc, j:j + 1], in1=ov, op0=ALU.mult, op1=ALU.add),
                        reads=[upn, "cdwt", "cvo"], writes=["cvo"])
                    if j % 6 == 0:
                        yield

                yield

            def run_gens_c(gs):
                alive = [True] * len(gs)
                while any(alive):
                    for i_ in range(len(gs)):
                        if alive[i_]:
                            try:
                                next(gs[i_])
                            except StopIteration:
                                alive[i_] = False
            run_gens_c([gen_glu(0)])
            for cc in range(4):
                run_gens_c([gen_conv(cc)] + ([gen_glu(cc + 1)] if cc < 3 else []))
            for hf in range(2):
                sl = slice(hf * 512, (hf + 1) * 512)
                for cc in range(4):
                    P.pe(lambda e, cc=cc, sl=sl: e.matmul(pA[:, 0:512], lhsT=onesf[:], rhs=cvo3[:, cc, sl],
                                                          start=(cc == 0), stop=(cc == 3)), reads=["onesf", "cvo"], writes=["pA"])
                for cc in range(4):
                    P.pool(lambda e, cc=cc, sl=sl: e.tensor_tensor(out=lnt[cc % 2][:], in0=cvo3[:, cc, sl], in1=cvo3[:, cc, sl],
                                                                   op=ALU.mult), reads=["cvo"], writes=["lnt%d" % (cc % 2)])
                    P.pe(lambda e, cc=cc: e.matmul(pB[:, 0:512], lhsT=onesf[:], rhs=lnt[cc % 2][:],
                                                   start=(cc == 0), stop=(cc == 3)), reads=["onesf", "lnt%d" % (cc % 2)], writes=["pB"])
                mean, msq, var = lnt[2], lnt[3], lnt[0]
                P.dve(lambda e: e.tensor_scalar(out=mean[:], in0=pA[:, 0:512], scalar1=1.0 / 512, scalar2=None, op0=ALU.mult),
                      reads=["pA"], writes=["lnt2"])
                P.pool(lambda e: e.tensor_tensor(out=msq[:], in0=mean[:], in1=mean[:], op=ALU.mult), reads=["lnt2"], writes=["lnt3"])
                P.dve(lambda e: e.scalar_tensor_tensor(out=var[:], in0=pB[:, 0:512], scalar=1.0 / 512, in1=msq[:],
                                                       op0=ALU.mult, op1=ALU.subtract), reads=["pB", "lnt3"], writes=["lnt0"])
                P.act(lambda e: e.activation(out=var[:], in_=var[:], func=AF.Ln, bias=epsb[:, 0:1]), reads=["lnt0", "epsb"], writes=["lnt0"])
                P.act(lambda e: e.activation(out=var[:], in_=var[:], func=AF.Exp, scale=-0.5), reads=["lnt0"], writes=["lnt0"])
                for cc in range(4):
                    P.pool(lambda e, cc=cc, sl=sl: e.tensor_tensor(out=lnt[1][:], in0=cvo3[:, cc, sl], in1=mean[:], op=ALU.subtract),
                           reads=["cvo", "lnt2"], writes=["lnt1"])
                    P.pool(lambda e: e.tensor_tensor(out=lnt[1][:], in0=lnt[1][:], in1=var[:], op=ALU.mult),
                           reads=["lnt1", "lnt0"], writes=["lnt1"])
                    P.act(lambda e, cc=cc, sl=sl: e.activation(out=ub3[:, cc, sl], in_=lnt[1][:], func=AF.Silu,
                                                               scale=cpart[:, 4 + cc:5 + cc], bias=cpart[:, 8 + cc:9 + cc]),
                          reads=["lnt1", "cpart"], writes=["ogT"])
            for cn in range(2):
                wv, wn = load_w(wkey("w_b_out", 0, 512, cn * 512, 512), 4, 512)
                gv, gn = load_w(wkey("w_gate", 0, 1024, 1024 + cn * 512, 512), 8, 512)
                for jj in range(4):
                    j = cn * 4 + jj
                    for hf in range(2):
                        ui = j * 2 + hf
                        pX, pXn = [(pA, ["pA"]), (pB, ["pB"]), (pC, ["pC0", "pC1"])][ui % 3]
                        sg, sgn = [(sgtC[:, 0:512], "sgC0"), (sgtC[:, 512:1024], "sgC1"), (lnt[0][:], "lnt0")][ui % 3]
                        hs_ = slice(hf * 512, (hf + 1) * 512)
                        for kc in range(4):
                            P.pe(lambda e, kc=kc, jj=jj, hs_=hs_, wv=wv, pX=pX: e.matmul(
                                pX[:, 0:512], lhsT=wv[:, kc, jj * 128:(jj + 1) * 128],
                                rhs=ub3[:, kc, hs_], start=(kc == 0), stop=(kc == 3)),
                                reads=[wn, "ogT"], writes=pXn)
                        for dc in range(8):
                            P.pe(lambda e, dc=dc, jj=jj, hs_=hs_, gv=gv, pX=pX: e.matmul(
                                pX[:, 512:1024], lhsT=gv[:, dc, jj * 128:(jj + 1) * 128],
                                rhs=hT3[:, dc, hs_], start=(dc == 0), stop=(dc == 7)),
                                reads=[gn, "hT"], writes=pXn)
                        P.act(lambda e, pX=pX, sg=sg: e.activation(out=sg, in_=pX[:, 512:1024], func=AF.Sigmoid), reads=pXn, writes=[sgn])
                        P.dve(lambda e, j=j, hs_=hs_, pX=pX, sg=sg: e.tensor_tensor(out=mB3[:, j, hs_], in0=pX[:, 0:512], in1=sg, op=ALU.mult),
                              reads=pXn + [sgn], writes=["mB"])
        P.barrier()

        if stop == 'C':
            raise StopBuild()
        if True:
            AN.top = T0

            def t(name, shape, dt=F32):
                return AN.alloc(shape[1], dt)
            abt = t("abt", [128, 8 * 32])
            abt3 = v3(abt[:], 8)
            gt = t("gt", [128, 8 * 16])
            bt = t("bt", [128, 8 * 16])
            gc = t("gc", [128, 8 * 16])
            gtot = t("gtot", [128, 8 * 16])
            egc = t("egc", [128, 8 * 16])
            ekd = t("ekd", [128, 8 * 16])
            egt = t("egt", [128, 8 * 16])
            bgc = t("bgc", [128, 8 * 16])
            gt3, bt3, gc3, gtot3 = v3(gt[:], 8), v3(bt[:], 8), v3(gc[:], 8), v3(gtot[:], 8)
            egc3, ekd3, egt3, bgc3 = v3(egc[:], 8), v3(ekd[:], 8), v3(egt[:], 8), v3(bgc[:], 8)
            wabt = t("wabt", [128, 8 * 32], BF16)
            wab3 = v3(wabt[:], 8)
            P.dma(lambda e: e.dma_start(out=wab3, in_=w_ab[job].rearrange("(k p) n -> p k n", p=128)),
                  writes=["wabt"], q="pool")
            for tt in range(8):
                for dc in range(8):
                    P.pe(lambda e, tt=tt, dc=dc: e.matmul(pS[:, tt * 32:(tt + 1) * 32], lhsT=hT3[:, dc, tt * 128:(tt + 1) * 128],
                                                          rhs=wab3[:, dc, :], start=(dc == 0), stop=(dc == 7)),
                         reads=["hT", "wabt"], writes=["pS"])
            P.act(lambda e: e.copy(out=abt[:], in_=pS[:, 0:256]), reads=["pS"], writes=["abt"])
            gp3 = gpt[:, job * 32:(job + 1) * 32]
            P.dve(lambda e: e.tensor_tensor(out=gt3, in0=abt3[:, :, 0:16], in1=bcm(gp3[:, 16:32], 8), op=ALU.add),
                  reads=["abt", "gpt"], writes=["gt"])
            P.act(lambda e: e.activation(out=gt[:], in_=gt[:], func=AF.Exp), reads=["gt"], writes=["gt"])
            P.act(lambda e: e.activation(out=gt[:], in_=gt[:], func=AF.Ln, bias=onesf[:, 0:1]), reads=["gt", "onesf"], writes=["gt"])
            P.dve(lambda e: e.tensor_tensor(out=gt3, in0=gt3, in1=bcm(nea[:, job * 16:(job + 1) * 16], 8), op=ALU.mult),
                  reads=["gt", "nea"], writes=["gt"])
            P.act(lambda e: e.activation(out=bt3, in_=abt3[:, :, 16:32], func=AF.Sigmoid), reads=["abt"], writes=["bt"])
            for tt in range(8):
                for d in range(2):
                    P.pe(lambda e, tt=tt, d=d: e.matmul(pS[:, tt * 16 + d * 8:tt * 16 + d * 8 + 8], lhsT=TRI[d][:],
                                                        rhs=gt3[:, tt, d * 8:d * 8 + 8], start=True, stop=True),
                         reads=[TRIN[d], "gt"], writes=["pS"])
                P.pe(lambda e, tt=tt: e.matmul(pS[:, 128 + tt * 16:128 + (tt + 1) * 16], lhsT=onesf[:], rhs=gt3[:, tt, :],
                                               start=True, stop=True), reads=["onesf", "gt"], writes=["pS"])
            P.dve(lambda e: e.tensor_copy(out=gc[:], in_=pS[:, 0:128]), reads=["pS"], writes=["gc"])
            P.dve(lambda e: e.tensor_copy(out=gtot[:], in_=pS[:, 128:256]), reads=["pS"], writes=["gtot"])
            P.act(lambda e: e.activation(out=egc[:], in_=gc[:], func=AF.Exp), reads=["gc"], writes=["egc"])
            P.act(lambda e: e.activation(out=egt[:], in_=gtot[:], func=AF.Exp), reads=["gtot"], writes=["egt"])
            P.dve(lambda e: e.tensor_tensor(out=ekd[:], in0=gtot[:], in1=gc[:], op=ALU.subtract), reads=["gtot", "gc"], writes=["ekd"])
            P.act(lambda e: e.activation(out=ekd[:], in_=ekd[:], func=AF.Exp), reads=["ekd"], writes=["ekd"])
            P.dve(lambda e: e.tensor_tensor(out=bgc[:], in0=bt[:], in1=egc[:], op=ALU.mult), reads=["bt", "egc"], writes=["bgc"])

            AN.topA = 0

            def t2(n, dt=F32):
                nfl = n if dt == F32 else (n + 1) // 2
                if AN.topA + nfl <= 8192:
                    off = AN.topA
                    AN.topA += nfl
                    return AN.at(off, n, dt)
                return AN.alloc(n, dt)

            class BDir:
                pass
            BD = []
            for d_ in range(2):
                b_ = BDir()
                b_.X = [t2(1040 if (i_ == 0 or (d_ == 0 and i_ == 3)) else 1024) for i_ in range(4)]
                b_.Lb = [t2(1024, BF16) for _ in range(2)]
                b_.Nb = [t2(1024, BF16) for _ in range(2)]
                b_.Nfull = t2(1024, BF16)
                b_.Ttb, b_.attT, b_.qdT, b_.vbt, b_.kdt, b_.nwT = (t2(1024, BF16) for _ in range(6))
                b_.Sf = [t2(128) for _ in range(4)]
                b_.Sb = [t2(128, BF16) for _ in range(4)]
                b_.vnb = [t2(128, BF16) for _ in range(4)]
                BD.append(b_)

            def R0(nm):
                return nm + "_0"
            X1, X2, X3, X4 = BD[0].X
            raw, cvq, rq = X1, X2[:, 0:1024], X3[:, 0:1024]
            osq = X4[:, 0:1024]
            sq = t2(1024, BF16)
            ogt = t2(1024, BF16)
            fmT = [t2(1024, BF16) for i in range(3)]
            ktm = t2(1024, BF16)
            vtm = t2(1024, BF16)
            szt = t2(1024, BF16)
            osum = t2(1024)
            cw3 = cwt[:, job * 120:(job + 1) * 120].rearrange("p (c k) -> p c k", c=24)
            if stop == 'Dg':
                raise StopBuild()


            def d1_gen(h, wv, wn):
                def gen_ci(ci):
                    c24 = h * 3 + ci
                    pI, pIn = [(pA, ["pA"]), (pB, ["pB"]), (pC, ["pC0", "pC1"])][ci]
                    raw_, rawn = [(BD[0].X[0], "X1_0"), (BD[1].X[0], "X1_1"), (BD[0].X[3], "X4_0")][ci]
                    cvq_, cvqn = [(BD[0].X[1], "X2_0"), (BD[1].X[1], "X2_1"), (BD[1].X[3], "X4_1")][ci]
                    cvq_ = cvq_[:, 0:1024]
                    rq_, rqn = [(BD[0].X[2], "X3_0"), (BD[1].X[2], "X3_1"), (None, None)][ci]
                    if rq_ is not None:
                        rq_ = rq_[:, 0:1024]
                    for hf in range(2):
                        for dc in range(8):
                            P.pe(lambda e, dc=dc, hf=hf: e.matmul(
                                pI[:, hf * 512:(hf + 1) * 512], lhsT=wv[:, dc, ci * 128:(ci + 1) * 128],
                                rhs=hT3[:, dc, hf * 512:(hf + 1) * 512], start=(dc == 0), stop=(dc == 7)),
                                reads=[wn, "hT"], writes=pIn)
                    P.pool(lambda e: e.memset(raw_[:], 0.0), writes=[rawn])
                    if isS:
                        hv = v3(hT2[:], 8)
                        for dc in range(8):
                            P.pe(lambda e, dc=dc: e.matmul(
                                pS[:, 0:2], lhsT=wv[:, dc, ci * 128:(ci + 1) * 128], rhs=hv[:, dc, :],
                                start=(dc == 0), stop=(dc == 7)), reads=[wn, "hT2"], writes=["pS"])
                        P.act(lambda e: e.copy(out=raw_[:, 1026:1028], in_=pS[:, 0:2]), reads=["pS"], writes=[rawn])
                        P.act(lambda e: e.copy(out=raw_[:, 2:1026], in_=pI[:]), reads=pIn, writes=[rawn])

                        def iv(j):
                            return raw_[:, j:j + 1024]
                        ov = cvq_
                    else:
                        P.act(lambda e: e.copy(out=v3(raw_[:, 0:1040], 4)[:, :, 2:258], in_=v3(pI[:], 4)), reads=pIn, writes=[rawn])

                        def iv(j):
                            return v3(raw_[:, 0:1040], 4)[:, :, j:j + 256]
                        ov = v3(cvq_, 4)
                    yield
                    P.dve(lambda e: e.tensor_scalar(
                        out=ov, in0=iv(0), scalar1=cw3[:, c24, 0:1], scalar2=None, op0=ALU.mult),
                        reads=[rawn, "cwt"], writes=[cvqn])
                    for j in range(1, 5):
                        P.dve(lambda e, j=j: e.scalar_tensor_tensor(
                            out=ov, in0=iv(j), scalar=cw3[:, c24, j:j + 1], in1=ov, op0=ALU.mult, op1=ALU.add),
                            reads=[rawn, "cwt", cvqn], writes=[cvqn])
                    yield
                    if ci == 2:
                        P.act(lambda e: e.activation(out=fmT[2][:], in_=cvq_, func=AF.Silu), reads=[cvqn], writes=["fmT2"])
                    else:
                        P.act(lambda e: e.activation(out=cvq_, in_=cvq_, func=AF.Silu), reads=[cvqn], writes=[cvqn])
                        yield
                        P.act(lambda e: e.activation(out=sq[:], in_=cvq_, func=AF.Square), reads=[cvqn], writes=["sq"])
                        for hf in range(2):
                            P.pe(lambda e, hf=hf: e.matmul(pI[:, hf * 512:(hf + 1) * 512], lhsT=onesb[:], rhs=sq[:, hf * 512:(hf + 1) * 512],
                                                           start=True, stop=True), reads=["onesb", "sq"], writes=pIn)
                        P.act(lambda e: e.activation(out=rq_, in_=pI[:], func=AF.Ln, bias=epsb[:, 0:1]), reads=pIn + ["epsb"], writes=[rqn])
                        yield
                        P.act(lambda e: e.activation(out=rq_, in_=rq_, func=AF.Exp, scale=-0.5,
                                                     bias=(qbias[:, 0:1] if ci == 0 else zero1[:, 0:1])),
                              reads=[rqn, "qbias", "zero1"], writes=[rqn])
                        P.dve(lambda e: e.tensor_tensor(out=fmT[ci][:], in0=cvq_, in1=rq_, op=ALU.mult),
                              reads=[cvqn, rqn], writes=["fmT%d" % ci])
                    yield

                gl = [gen_ci(0), gen_ci(1), gen_ci(2)]
                alive = [True, True, True]
                while any(alive):
                    for i_ in range(3):
                        if alive[i_]:
                            try:
                                next(gl[i_])
                            except StopIteration:
                                alive[i_] = False
                    yield
                for (src, srcn, dst, dstn) in ((fmT[1], "fmT1", ktm, "ktm"), (fmT[2], "fmT2", vtm, "vtm")):
                    for tt in range(8):
                        P.pe(lambda e, tt=tt, src=src: e.transpose(pT[:, tt * 128:(tt + 1) * 128], src[:, tt * 128:(tt + 1) * 128], identb[:]),
                             reads=[srcn, "identb"], writes=["pT"])
                    P.act(lambda e, dst=dst: e.copy(out=dst[:], in_=pT[:]), reads=["pT"], writes=[dstn])
                    yield
                for tt in range(8):
                    for dc in range(8):
                        P.pe(lambda e, tt=tt, dc=dc: e.matmul(
                            pC[:, tt * 128:(tt + 1) * 128], lhsT=hT3[:, dc, tt * 128:(tt + 1) * 128], rhs=wv[:, dc, 384:512],
                            start=(dc == 0), stop=(dc == 7)), reads=[wn, "hT"], writes=["pC0", "pC1"])
                P.act(lambda e: e.activation(out=szt[:], in_=pC[:], func=AF.Silu), reads=["pC0", "pC1"], writes=["szt"])
                yield

            def d23(h):
                P.pool(lambda e: e.memset(osum[:], 0.0), writes=["osum"])
                def gen_dir(d):
                    bd = BD[d]

                    def R(nm):
                        return nm + "_%d" % d
                    X1, X2, X3, X4 = bd.X
                    gtri, dd, esym, eL = X1[:, 0:1024], X2[:, 0:1024], X3[:, 0:1024], X4[:, 0:1024]
                    ebc, Ttf, eA = X1[:, 0:1024], X2[:, 0:1024], X4[:, 0:1024]
                    cct = X3[:, 0:256]
                    Lb, Nb, Nfull = bd.Lb, bd.Nb, bd.Nfull
                    kbg = Lb[0]
                    Ttb, attT, qdT, vbt, kdt, nwT = bd.Ttb, bd.attT, bd.qdT, bd.vbt, bd.kdt, bd.nwT
                    Sf, Sb, vnb = bd.Sf, bd.Sb, bd.vnb

                    def init_state(ci_, d_):
                        if isS and d_ == 0:
                            P.dma(lambda e: e.dma_start(out=Sf[ci_][:], in_=s0[h]), writes=[R("Sf%d" % ci_)])
                        elif isS:
                            P.dma(lambda e: e.dma_start(out=v3(cct, 2), in_=cc_out[h].rearrange("(r p) f -> p r f", p=128)),
                                  reads=["cc_out%d" % h], writes=[R("X3")])
                            P.dve(lambda e: e.tensor_scalar(out=Sf[ci_][:], in0=cct[:, 0:128], scalar1=selt[:, 0:1], scalar2=None, op0=ALU.mult),
                                  reads=[R("X3"), "selt"], writes=[R("Sf%d" % ci_)])
                            P.dve(lambda e: e.scalar_tensor_tensor(out=Sf[ci_][:], in0=cct[:, 128:256], scalar=selt[:, 1:2], in1=Sf[ci_][:],
                                                                   op0=ALU.mult, op1=ALU.add), reads=[R("X3"), "selt", R("Sf%d" % ci_)], writes=[R("Sf%d" % ci_)])
                        else:
                            P.pool(lambda e: e.memset(Sf[ci_][:], 0.0), writes=[R("Sf%d" % ci_)])
                        P.act(lambda e: e.copy(out=Sb[ci_][:], in_=Sf[ci_][:]), reads=[R("Sf%d" % ci_)], writes=[R("Sb%d" % ci_)])

                    def run_chains(chs, d_):
                        col = d_ * 8 + h
                        nst = len(chs[0][1])
                        for ci_, (s_, order) in enumerate(chs):
                            init_state(ci_, d_)
                        for step in range(nst):
                            for ci_, (s_, order) in enumerate(chs):
                                c = order[step]
                                cs = slice(c * 128, (c + 1) * 128)
                                slot = ci_ % 2
                                pv = pC[:, slot * 512:(slot + 1) * 512]
                                pvn = "pC%d" % slot
                                Sfn, Sbn, vnn = R("Sf%d" % ci_), R("Sb%d" % ci_), R("vnb%d" % ci_)
                                P.pe(lambda e, cs=cs, pv=pv: e.matmul(pv[:, 0:128], lhsT=Ttb[:, cs], rhs=vbt[:, cs], start=True, stop=False),
                                     reads=[R("Ttb"), R("vbt")], writes=[pvn])
                                P.pe(lambda e, cs=cs, pv=pv, ci_=ci_: e.matmul(pv[:, 0:128], lhsT=nwT[:, cs], rhs=Sb[ci_][:], start=False, stop=True),
                                     reads=[R("nwT"), Sbn], writes=[pvn])
                                P.act(lambda e, pv=pv, ci_=ci_: e.copy(out=vnb[ci_][:], in_=pv[:, 0:128]), reads=[pvn], writes=[vnn])
                                P.pe(lambda e, cs=cs, pv=pv, ci_=ci_: e.matmul(pv[:, 128:256], lhsT=qdT[:, cs], rhs=Sb[ci_][:], start=True, stop=False),
                                     reads=[R("qdT"), Sbn], writes=[pvn])
                                P.pe(lambda e, cs=cs, pv=pv, ci_=ci_: e.matmul(pv[:, 128:256], lhsT=attT[:, cs], rhs=vnb[ci_][:], start=False, stop=True),
                                     reads=[R("attT"), vnn], writes=[pvn])
                                P.pe(lambda e, cs=cs, pv=pv, ci_=ci_: e.matmul(pv[:, 256:384], lhsT=kdt[:, cs], rhs=vnb[ci_][:], start=True, stop=True),
                                     reads=[R("kdt"), vnn], writes=[pvn])
                                P.dve(lambda e, cs=cs, pv=pv: e.tensor_tensor(out=osum[:, cs], in0=osum[:, cs], in1=pv[:, 128:256], op=ALU.add),
                                      reads=["osum", pvn], writes=["osum"])
                                P.dve(lambda e, pv=pv, ci_=ci_, c=c: e.scalar_tensor_tensor(
                                    out=Sf[ci_][:], in0=Sf[ci_][:], scalar=egt3[:, c, col:col + 1], in1=pv[:, 256:384], op0=ALU.mult, op1=ALU.add),
                                    reads=[Sfn, "egt", pvn], writes=[Sfn])
                                P.act(lambda e, ci_=ci_: e.copy(out=Sb[ci_][:], in_=Sf[ci_][:]), reads=[Sfn], writes=[Sbn])
                                yield

                    if stop == 'H1':
                        raise StopBuild()

                    pG, pGn = (pA, "pA") if d == 0 else (pB, "pB")
                    col = d * 8 + h
                    gcol = gt3[:, :, col]
                    P.dve(lambda e: e.tensor_tensor(out=v3(gtri, 8), in0=bcm(TRI[d][:], 8), in1=bc1(gcol, 128), op=ALU.mult),
                          reads=[TRIN[d], "gt"], writes=[R("X1")])
                    yield
                    for hf in range(2):
                        P.pe(lambda e, hf=hf: e.matmul(pG[:, hf * 512:(hf + 1) * 512], lhsT=onesf[:], rhs=gtri[:, hf * 512:(hf + 1) * 512],
                                                       start=True, stop=True), reads=["onesf", R("X1")], writes=[pGn])
                    P.dve(lambda e: e.tensor_tensor(out=v3(dd, 8), in0=v3(pG[:], 8), in1=bc1(gc3[:, :, col], 128), op=ALU.subtract),
                          reads=[pGn, "gc"], writes=[R("X2")])
                    P.act(lambda e: e.activation(out=ebc, in_=pG[:], func=AF.Exp), reads=[pGn], writes=[R("X1")])
                    P.dve(lambda e: e.scalar_tensor_tensor(out=dd, in0=dd, scalar=-1.0, in1=dd, op0=ALU.mult, op1=ALU.min), reads=[R("X2")], writes=[R("X2")])
                    P.act(lambda e: e.activation(out=esym, in_=dd, func=AF.Exp), reads=[R("X2")], writes=[R("X3")])
                    P.dve(lambda e: e.tensor_tensor(out=v3(eL, 8), in0=v3(esym, 8), in1=bcm(STR[d][:], 8), op=ALU.mult),
                           reads=[R("X3"), STRN[d]], writes=[R("X4")])
                    P.dve(lambda e: e.tensor_tensor(out=v3(eL, 8), in0=v3(eL, 8), in1=bc1(bt3[:, :, col], 128), op=ALU.mult),
                          reads=[R("X4"), "bt"], writes=[R("X4")])
                    P.dve(lambda e: e.tensor_tensor(out=qdT[:], in0=fmT[0][:], in1=ebc, op=ALU.mult),
                          reads=["fmT0", R("X1")], writes=[R("qdT")])
                    yield
                    for c in range(8):
                        cs = slice(c * 128, (c + 1) * 128)
                        P.pe(lambda e, cs=cs: e.matmul(pG[:, cs], lhsT=fmT[1][:, cs], rhs=fmT[1][:, cs], start=True, stop=True),
                             reads=["fmT1"], writes=[pGn])
                    P.dve(lambda e: e.tensor_tensor(out=Lb[0][:], in0=pG[:], in1=eL, op=ALU.mult), reads=[pGn, R("X4")], writes=[R("Lb0")])
                    P.dve(lambda e: e.tensor_tensor(out=v3(eA, 8), in0=v3(esym, 8), in1=bcm(TRI[d][:], 8), op=ALU.mult),
                          reads=[R("X3"), TRIN[d]], writes=[R("X4")])
                    yield
                    for c in range(8):
                        cs = slice(c * 128, (c + 1) * 128)
                        P.pe(lambda e, cs=cs: e.matmul(pG[:, cs], lhsT=fmT[1][:, cs], rhs=fmT[0][:, cs], start=True, stop=True),
                             reads=["fmT1", "fmT0"], writes=[pGn])
                    P.dve(lambda e: e.tensor_tensor(out=attT[:], in0=pG[:], in1=eA, op=ALU.mult),
                          reads=[pGn, R("X4")], writes=[R("attT")])
                    yield
                    for c in range(8):
                        cs = slice(c * 128, (c + 1) * 128)
                        P.pe(lambda e, cs=cs: e.transpose(pT[:, cs], Lb[0][:, cs], identb[:]), reads=[R("Lb0"), "identb"], writes=["pT"])
                    P.act(lambda e: e.copy(out=Nfull[:], in_=pT[:]), reads=["pT"], writes=[R("Nfull")])
                    P.dve(lambda e: e.tensor_tensor(out=v3(Lb[1][:], 8), in0=v3(Lb[0][:], 8), in1=bcm(mk[0], 8), op=ALU.mult),
                           reads=[R("Lb0"), "mkt"], writes=[R("Lb1")])
                    P.pool(lambda e: e.tensor_tensor(out=v3(Nb[1][:], 8), in0=v3(Nfull[:], 8), in1=bcm(mk[0], 8), op=ALU.mult),
                           reads=[R("Nfull"), "mkt"], writes=[R("Nb1")])
                    P.dve(lambda e: e.tensor_tensor(out=v3(Ttf, 8), in0=bcm(identf[:], 8), in1=v3(Nb[1][:], 8), op=ALU.subtract),
                          reads=["identf", R("Nb1")], writes=[R("X2")])
                    P.act(lambda e: e.copy(out=Ttb[:], in_=Ttf), reads=[R("X2")], writes=[R("Ttb")])
                    yield
                    cur = 1
                    for k in range(1, 3):
                        nxt = 1 - cur
                        for c in range(8):
                            cs = slice(c * 128, (c + 1) * 128)
                            P.pe(lambda e, cs=cs, cur=cur: e.matmul(pG[:, cs], lhsT=Nb[cur][:, cs], rhs=Lb[cur][:, cs], start=True, stop=True),
                                 reads=[R("Nb%d" % cur), R("Lb%d" % cur)], writes=[pGn])
                        P.act(lambda e, nxt=nxt: e.copy(out=Lb[nxt][:], in_=pG[:]), reads=[pGn], writes=[R("Lb%d" % nxt)])
                        yield
                        if k < 2:
                            for c in range(8):
                                cs = slice(c * 128, (c + 1) * 128)
                                P.pe(lambda e, cs=cs, cur=cur: e.matmul(pG[:, cs], lhsT=Lb[cur][:, cs], rhs=Nb[cur][:, cs], start=True, stop=True),
                                     reads=[R("Nb%d" % cur), R("Lb%d" % cur)], writes=[pGn])
                            P.dve(lambda e, nxt=nxt: e.tensor_copy(out=Nb[nxt][:], in_=pG[:]), reads=[pGn], writes=[R("Nb%d" % nxt)])
                            yield
                        for c in range(8):
                            cs = slice(c * 128, (c + 1) * 128)
                            P.pe(lambda e, cs=cs, nxt=nxt: e.matmul(pG[:, cs], lhsT=Lb[nxt][:, cs], rhs=Ttb[:, cs], start=True, stop=True),
                                 reads=[R("Lb%d" % nxt), R("Ttb")], writes=[pGn])
                        P.dve(lambda e: e.tensor_tensor(out=Ttb[:], in0=Ttf, in1=pG[:], op=ALU.add), reads=[R("X2"), pGn], writes=[R("Ttb")])
                        P.dve(lambda e: e.tensor_tensor(out=Ttf, in0=Ttf, in1=pG[:], op=ALU.add), reads=[R("X2"), pGn], writes=[R("X2")])
                        yield
                        cur = nxt
                    Tf, Tb, Ub, NOb = Ttf, Lb[1], Nb[0], Nb[1]
                    for c in range(8):
                        cs = slice(c * 128, (c + 1) * 128)
                        P.pe(lambda e, cs=cs: e.transpose(pT[:, cs], Ttb[:, cs], identb[:]), reads=[R("Ttb"), "identb"], writes=["pT"])
                    P.act(lambda e: e.copy(out=Tb[:], in_=pT[:]), reads=["pT"], writes=[R("Lb1")])
                    P.dve(lambda e: e.tensor_copy(out=Tf, in_=Tb[:]), reads=[R("Lb1")], writes=[R("X2")])
                    yield
                    NOs = [(Nb[1], R("Nb1")), (Lb[0], R("Lb0"))]

                    def mkmask(lvl):
                        nb_, nn_ = NOs[lvl % 2]
                        P.pool(lambda e: e.tensor_tensor(out=v3(nb_[:], 8), in0=v3(Nfull[:], 8), in1=bcm(mk[lvl], 8), op=ALU.mult),
                               reads=[R("Nfull"), "mkt"], writes=[nn_])
                    mkmask(1)
                    for lvl in range(1, 5):
                        NOb, NOn = NOs[lvl % 2]
                        for c in range(8):
                            cs = slice(c * 128, (c + 1) * 128)
                            P.pe(lambda e, cs=cs, NOb=NOb: e.matmul(pG[:, cs], lhsT=NOb[:, cs], rhs=Tb[:, cs], start=True, stop=True),
                                 reads=[NOn, R("Lb1")], writes=[pGn])
                        P.act(lambda e: e.copy(out=Ub[:], in_=pG[:]), reads=[pGn], writes=[R("Nb0")])
                        if lvl < 4:
                            mkmask(lvl + 1)
                        yield
                        for c in range(8):
                            cs = slice(c * 128, (c + 1) * 128)
                            P.pe(lambda e, cs=cs: e.matmul(pG[:, cs], lhsT=Ttb[:, cs], rhs=Ub[:, cs], start=True, stop=True),
                                 reads=[R("Ttb"), R("Nb0")], writes=[pGn])
                        P.dve(lambda e: e.tensor_tensor(out=Tb[:], in0=Tf, in1=pG[:], op=ALU.subtract), reads=[R("X2"), pGn], writes=[R("Lb1")])
                        if lvl < 4:
                            P.dve(lambda e: e.tensor_tensor(out=Tf, in0=Tf, in1=pG[:], op=ALU.subtract), reads=[R("X2"), pGn], writes=[R("X2")])
                        yield
                        for c in range(8):
                            cs = slice(c * 128, (c + 1) * 128)
                            P.pe(lambda e, cs=cs: e.transpose(pT[:, cs], Tb[:, cs], identb[:]), reads=[R("Lb1"), "identb"], writes=["pT"])
                        P.act(lambda e: e.copy(out=Ttb[:], in_=pT[:]), reads=["pT"], writes=[R("Ttb")])
                        yield
                    P.dve(lambda e: e.tensor_tensor(out=v3(vbt[:], 8), in0=v3(vtm[:], 8), in1=bc1(bt3[:, :, col], 128), op=ALU.mult),
                          reads=["vtm", "bt"], writes=[R("vbt")])
                    P.dve(lambda e: e.tensor_tensor(out=v3(kbg[:], 8), in0=v3(ktm[:], 8), in1=bc1(bgc3[:, :, col], 128), op=ALU.mult),
                          reads=["ktm", "bgc"], writes=[R("Lb0")])
                    P.pool(lambda e: e.tensor_tensor(out=v3(kdt[:], 8), in0=v3(ktm[:], 8), in1=bc1(ekd3[:, :, col], 128), op=ALU.mult),
                           reads=["ktm", "ekd"], writes=[R("kdt")])
                    yield
                    for c in range(8):
                        cs = slice(c * 128, (c + 1) * 128)
                        P.pe(lambda e, cs=cs: e.matmul(pG[:, cs], lhsT=kbg[:, cs], rhs=Ttb[:, cs], start=True, stop=True),
                             reads=[R("Lb0"), R("Ttb")], writes=[pGn])
                    P.act(lambda e: e.mul(out=nwT[:], in_=pG[:], mul=-1.0), reads=[pGn], writes=[R("nwT")])
                    yield

                    if isS:
                        if d == 1:
                            yield "WAIT0"
                            P.dma(lambda e: e.dma_start(out=cc_in[h], in_=BD[0].Sf[0][:]), reads=["Sf0_0"], writes=["cc_in%d" % h], q="pool")

                            def ccfn(e):
                                e.collective_compute("AllGather", ALU.bypass, replica_groups=GROUPS,
                                                     ins=[cc_in[h]], outs=[cc_out[h]]).then_inc(s_cc, 1)
                                e.wait_ge(s_cc, h + 1)
                            P.op("pool", ccfn, reads=["cc_in%d" % h], writes=["cc_out%d" % h], selfsig=True)
                        yield from run_chains([(0, list(range(8)) if d == 0 else list(range(7, -1, -1)))], d)
                    else:
                        chs = [(s_, ([2 * s_, 2 * s_ + 1] if d == 0 else [2 * s_ + 1, 2 * s_])) for s_ in range(4)]
                        yield from run_chains(chs, d)
                        for ci_, (s_, order) in enumerate(chs):
                            P.dma(lambda e, ci_=ci_, s_=s_, d=d: e.dma_start(out=st[s_, d, h], in_=Sf[ci_][:]), reads=[R("Sf%d" % ci_)])

                    if stop == 'X%d_%d' % (h, d):
                        raise StopBuild()


                gens = [gen_dir(0), gen_dir(1)]
                done = [False, False]

                def step(i_):
                    if done[i_]:
                        return None
                    try:
                        return next(gens[i_])
                    except StopIteration:
                        done[i_] = True
                        return None
                while not (done[0] and done[1]):
                    step(0)
                    r_ = step(1)
                    if r_ == "WAIT0":
                        while not done[0]:
                            step(0)

            def d4_gen(h):
                for tt in range(8):
                    P.act(lambda e, tt=tt: e.activation(out=ogt[:, tt * 128:(tt + 1) * 128], in_=osum[:, tt * 128:(tt + 1) * 128],
                                                        func=AF.Square, accum_out=stat[:, 48 + tt:49 + tt]),
                          reads=["osum"], writes=["ogt", "stat"])
                P.act(lambda e: e.activation(out=stat[:, 48:56], in_=stat[:, 48:56], func=AF.Ln, scale=1.0 / 128, bias=epsb[:, 0:1]),
                      reads=["stat", "epsb"], writes=["stat"])
                P.act(lambda e: e.activation(out=stat[:, 48:56], in_=stat[:, 48:56], func=AF.Exp, scale=-0.5), reads=["stat"], writes=["stat"])
                yield
                P.dve(lambda e: e.tensor_tensor(out=v3(osum[:], 8), in0=v3(osum[:], 8), in1=bc1(stat[:, 48:56], 128), op=ALU.mult),
                      reads=["osum", "stat"], writes=["osum"])
                P.dve(lambda e: e.tensor_tensor(out=v3(osum[:], 8), in0=v3(osum[:], 8), in1=bcm(normo_bc[:], 8), op=ALU.mult),
                      reads=["osum", "normo_bc"], writes=["osum"])
                yield
                P.dve(lambda e: e.tensor_tensor(out=ogt[:], in0=osum[:], in1=szt[:], op=ALU.mult), reads=["osum", "szt"], writes=["ogt"])
                yield
                for tt in range(8):
                    P.pe(lambda e, tt=tt: e.transpose(pT[:, tt * 128:(tt + 1) * 128], ogt[:, tt * 128:(tt + 1) * 128], identb[:]),
                         reads=["ogt", "identb"], writes=["pT"])
                P.act(lambda e: e.copy(out=ogT3[:, h, :], in_=pT[:]), reads=["pT"], writes=["ogT"])
                yield

            def run_gens(gs):
                alive = [True] * len(gs)
                while any(alive):
                    for i_ in range(len(gs)):
                        if alive[i_]:
                            try:
                                next(gs[i_])
                            except StopIteration:
                                alive[i_] = False

            heads = list(HEADS if HEADS else range(8))
            wl = {heads[0]: load_w(wkey("w_in_h", 0, 1024, heads[0] * 512, 512), 8, 512)}
            run_gens([d1_gen(heads[0], *wl[heads[0]])])
            for hi, h in enumerate(heads):
                nh = heads[hi + 1] if hi + 1 < len(heads) else None
                if nh is not None:
                    wl[nh] = load_w(wkey("w_in_h", 0, 1024, nh * 512, 512), 8, 512)
                d23(h)
                if SEQ_D4:
                    run_gens([d4_gen(h)])
                    if nh is not None:
                        run_gens([d1_gen(nh, *wl[nh])])
                else:
                    run_gens([d4_gen(h)] + ([d1_gen(nh, *wl[nh])] if nh is not None else []))
                if stop == 'D%d' % h:
                    raise StopBuild()
        P.barrier()

        if stop == 'D':
            raise StopBuild()
        if True:
            AN.top = T0
            sgtE = AN.alloc(1024)
            tmpeE = AN.alloc(1024)
            actb = AN.at(mB_off, 22 * 1024, BF16)
            act3 = actb[:].rearrange("p (k n) -> p k n", k=22)
            phtF = {"junk": AN.alloc(1024, BF16),
                   "xn": [AN.alloc(1024, BF16) for i in range(2)],
                   "tmpf": AN.alloc(1024)}
            for tt in range(8):
                P.dma(lambda e, tt=tt: e.dma_start(out=xres3[:, tt, :], in_=xin[tt * 128:(tt + 1) * 128, :]), writes=["xres"])
            for cn in range(2):
                wv, wn = load_w(wkey("w_a_out", 0, 1024, cn * 512, 512), 8, 512)
                gv, gn = load_w(wkey("w_gate", 0, 1024, cn * 512, 512), 8, 512)
                for jj in range(4):
                    j = cn * 4 + jj
                    for hf in range(2):
                        ui = j * 2 + hf
                        pX, pXn = [(pA, ["pA"]), (pB, ["pB"]), (pC, ["pC0", "pC1"])][ui % 3]
                        sg, sgn = [(sgtE[:, 0:512], "sgE0"), (sgtE[:, 512:1024], "sgE1"), (phtF["tmpf"][:, 0:512], "tmpf")][ui % 3]
                        hs_ = slice(hf * 512, (hf + 1) * 512)
                        for kc in range(8):
                            P.pe(lambda e, kc=kc, jj=jj, hs_=hs_, wv=wv, pX=pX: e.matmul(
                                pX[:, 0:512], lhsT=wv[:, kc, jj * 128:(jj + 1) * 128],
                                rhs=ogT3[:, kc, hs_], start=(kc == 0), stop=(kc == 7)),
                                reads=[wn, "ogT"], writes=pXn)
                        for dc in range(8):
                            P.pe(lambda e, dc=dc, jj=jj, hs_=hs_, gv=gv, pX=pX: e.matmul(
                                pX[:, 512:1024], lhsT=gv[:, dc, jj * 128:(jj + 1) * 128],
                                rhs=hT3[:, dc, hs_], start=(dc == 0), stop=(dc == 7)),
                                reads=[gn, "hT"], writes=pXn)
                        P.act(lambda e, pX=pX, sg=sg: e.activation(out=sg, in_=pX[:, 512:1024], func=AF.Sigmoid), reads=pXn, writes=[sgn])
                        P.dve(lambda e, pX=pX, sg=sg, hs_=hs_: e.tensor_tensor(out=tmpeE[:, hs_], in0=pX[:, 0:512], in1=sg, op=ALU.mult),
                              reads=pXn + [sgn], writes=["tmpeE"])
                        P.dve(lambda e, j=j, hs_=hs_: e.tensor_tensor(out=mB3[:, j, hs_], in0=tmpeE[:, hs_], in1=mB3[:, j, hs_], op=ALU.add),
                              reads=["tmpeE", "mB"], writes=["mB"])
            if stop == 'E':
                raise StopBuild()
            wo = [load_w(wkey("w_o", 0, 1024, cn * 512, 512), 8, 512) for cn in range(2)]
            for tt in range(8):
                for cn in range(2):
                    pv, pvn = (pA, "pA") if cn == 0 else (pB, "pB")
                    for dc in range(8):
                        P.pe(lambda e, dc=dc, tt=tt, cn=cn, pv=pv: e.matmul(
                            pv[:, 0:512], lhsT=mB3[:, dc, tt * 128:(tt + 1) * 128], rhs=wo[cn][0][:, dc, :],
                            start=(dc == 0), stop=(dc == 7)), reads=["mB", wo[cn][1]], writes=[pvn])
                    P.dve(lambda e, cn=cn, pv=pv: e.tensor_tensor(out=tmpeE[:, cn * 512:(cn + 1) * 512], in0=pv[:, 0:512],
                                                                  in1=g1bc[:, cn * 512:(cn + 1) * 512], op=ALU.mult),
                          reads=[pvn, "gbc"], writes=["tmpeE"])
                P.dve(lambda e, tt=tt: e.tensor_tensor(out=xres3[:, tt, :], in0=xres3[:, tt, :], in1=tmpeE[:], op=ALU.add),
                       reads=["xres", "tmpeE"], writes=["xres"])
                norm_transpose(xres3[:, tt, :], "xres", hT3[:, :, tt * 128:(tt + 1) * 128], "hT", tt,
                               sc2t[:, job * 8:(job + 1) * 8], 24, phtF, "f", part=1)
                if tt >= 1:
                    norm_transpose(xres3[:, tt - 1, :], "xres", hT3[:, :, (tt - 1) * 128:tt * 128], "hT", tt - 1,
                                   sc2t[:, job * 8:(job + 1) * 8], 24, phtF, "f", part=2)
            norm_transpose(xres3[:, 7, :], "xres", hT3[:, :, 7 * 128:8 * 128], "hT", 7,
                           sc2t[:, job * 8:(job + 1) * 8], 24, phtF, "f", part=2)
            if stop == 'F':
                raise StopBuild()
            P.barrier()
            AN.top = mB_off + 11 * 1024
            sgtG = AN.alloc(1024)
            tmpeG = AN.alloc(1024)
            phtG = {"junk": AN.alloc(1024, BF16)}
            yt = [AN.alloc(1024) for i in range(2)]
            normf_bc = AN.alloc(1024)
            stg = [AN.alloc(8 * 512) for _ in range(2)]
            stgn = [0]

            def load_w_hw(key, kc, ncols):
                if key not in WT:
                    WT[key] = len(WT)
                    assert len(WT) <= NWT
                src3 = wpack[WT[key]][:, 0:kc * ncols].rearrange("p (k n) -> p k n", k=kc)
                k_ = stgn[0] % 2
                stgn[0] += 1
                sv = stg[k_][:, 0:kc * ncols].rearrange("p (k n) -> p k n", k=kc)
                P.dma(lambda e: e.dma_start(out=sv, in_=src3), writes=["stg%d" % k_])
                i = wstate["n"] % NWB
                wstate["n"] += 1
                nm = "wb%d" % i
                view = wbuf[i][:, 0:kc * ncols].rearrange("p (k n) -> p k n", k=kc)
                P.dve(lambda e: e.tensor_copy(out=view, in_=sv), reads=["stg%d" % k_], writes=[nm])
                return view, nm
            P.dma(lambda e: e.dma_start(out=normf_bc[:], in_=norm_f.partition_broadcast(128)), writes=["normf_bc"])
            for j in range(22):
                if j % 4 == 0:
                    ncol = min(512, FF - j * 128)
                    wg = load_w(wkey("w_gu", 0, 1024, j * 128, ncol), 8, ncol)
                    wu = load_w_hw(wkey("w_gu", 0, 1024, FF + j * 128, ncol), 8, ncol)
                jj = j % 4
                for hf in range(2):
                    ui = j * 2 + hf
                    pX, pXn = [(pA, ["pA"]), (pB, ["pB"]), (pC, ["pC0", "pC1"])][ui % 3]
                    sg, sgn = [(sgtG[:, 0:512], "sgS0"), (sgtG[:, 512:1024], "sgS1"), (yt[0][:, 0:512], "yt0")][ui % 3]
                    for dc in range(8):
                        P.pe(lambda e, dc=dc, jj=jj, hf=hf, wg=wg, pX=pX: e.matmul(
                            pX[:, 0:512], lhsT=wg[0][:, dc, jj * 128:(jj + 1) * 128],
                            rhs=hT3[:, dc, hf * 512:(hf + 1) * 512], start=(dc == 0), stop=(dc == 7)),
                            reads=[wg[1], "hT"], writes=pXn)
                    for dc in range(8):
                        P.pe(lambda e, dc=dc, jj=jj, hf=hf, wu=wu, pX=pX: e.matmul(
                            pX[:, 512:1024], lhsT=wu[0][:, dc, jj * 128:(jj + 1) * 128],
                            rhs=hT3[:, dc, hf * 512:(hf + 1) * 512], start=(dc == 0), stop=(dc == 7)),
                            reads=[wu[1], "hT"], writes=pXn)
                    P.act(lambda e, pX=pX, sg=sg: e.activation(out=sg, in_=pX[:, 0:512], func=AF.Silu), reads=pXn, writes=[sgn])
                    P.dve(lambda e, j=j, hf=hf, pX=pX, sg=sg: e.tensor_tensor(out=act3[:, j, hf * 512:(hf + 1) * 512], in0=pX[:, 512:1024], in1=sg, op=ALU.mult),
                          reads=pXn + [sgn], writes=["actb"])
            for cn in range(2):
                wd = [load_w(wkey("w_down", k0 * 128, nk * 128, cn * 512, 512), nk, 512)
                      for (k0, nk) in ((0, 8), (8, 8), (16, 6))]
                for tt in range(8):
                    pv, pvn = (pA, "pA") if tt % 2 == 0 else (pB, "pB")
                    for j in range(22):
                        wdv, wdn = wd[j // 8]
                        P.pe(lambda e, j=j, tt=tt, wdv=wdv, pv=pv: e.matmul(
                            pv[:, 0:512], lhsT=act3[:, j, tt * 128:(tt + 1) * 128], rhs=wdv[:, j % 8, :],
                            start=(j == 0), stop=(j == 21)), reads=["actb", wdn], writes=[pvn])
                    P.dve(lambda e, cn=cn, pv=pv: e.tensor_tensor(out=tmpeG[:, 0:512], in0=pv[:, 0:512],
                                                                  in1=g2bc[:, cn * 512:(cn + 1) * 512], op=ALU.mult),
                          reads=[pvn, "gbc"], writes=["tmpeG"])
                    P.dve(lambda e, tt=tt, cn=cn: e.tensor_tensor(out=xres3[:, tt, cn * 512:(cn + 1) * 512],
                                                                   in0=xres3[:, tt, cn * 512:(cn + 1) * 512], in1=tmpeG[:, 0:512], op=ALU.add),
                           reads=["xres", "tmpeG"], writes=["xres"])
            for tt in range(8):
                col = stat[:, tt * 3:tt * 3 + 3]
                y = yt[tt % 2]
                yn = "yt%d" % (tt % 2)
                P.act(lambda e, tt=tt, col=col: e.activation(out=phtG["junk"][:], in_=xres3[:, tt, :], func=AF.Square, accum_out=col[:, 0:1]),
                      reads=["xres"], writes=["junk", "stat"])
                P.act(lambda e, col=col: e.activation(out=col[:, 1:2], in_=col[:, 0:1], func=AF.Ln, scale=1.0 / 1024, bias=epsb[:, 0:1]),
                      reads=["stat", "epsb"], writes=["stat"])
                P.act(lambda e, col=col: e.activation(out=col[:, 2:3], in_=col[:, 1:2], func=AF.Exp, scale=-0.5), reads=["stat"], writes=["stat"])
                P.dve(lambda e, tt=tt, col=col, y=y: e.scalar_tensor_tensor(out=y[:], in0=xres3[:, tt, :], scalar=col[:, 2:3], in1=normf_bc[:],
                                                                            op0=ALU.mult, op1=ALU.mult),
                      reads=["xres", "stat", "normf_bc"], writes=[yn])
                P.dma(lambda e, tt=tt, y=y: e.dma_start(out=yout[tt * 128:(tt + 1) * 128, :], in_=y[:]), reads=[yn])
        P.barrier()

    try:
        if stop in ('ada', 'const', 'ada1', 'ada2'):
            raise StopBuild()
        run_job(0)
        if stop == 'P':
            raise StopBuild()
        if enable_S:
            run_job(1)
    except StopBuild:
        pass

    WT_KEYS[:] = sorted(WT, key=WT.get)
    run = P.emit(sems, dma_sems)
    if stats:
        for e_ in ENGINES:
            print(e_, 'ops', len(P.ops[e_]), 'signals', sum(1 for r_ in P.ops[e_] if r_['signal']), 'dmas', P.ndma[e_])
    with nc.Block() as block:
        @block.sync
        def _(eng):
            run("sp", eng)

        @block.tensor
        def _(eng):
            run("pe", eng)

        @block.scalar
        def _(eng):
            run("act", eng)

        @block.vector
        def _(eng):
            run("dve", eng)

        @block.gpsimd
        def _(eng):
            run("pool", eng)
    es.close()
    return nc


_NC_CACHE = {}


def _prep_inputs(inp):
    f = lambda a: np.ascontiguousarray(np.asarray(a, dtype=np.float32))
    x_prompt, x_sample = f(inp["x_prompt"]), f(inp["x_sample"])
    state_delta, c, c_ctx = f(inp["state_delta"]), f(inp["c"]), f(inp["c_ctx"])
    w_in = f(inp["w_in"])[0]
    cols = []
    for h in range(8):
        for base in (0, 1024, 2048, 3072):
            cols.append(np.arange(base + h * 128, base + (h + 1) * 128))
    cols = np.concatenate(cols)
    w_in_h = np.ascontiguousarray(w_in[:, cols])
    w_ab_n = w_in[:, 4096:4128]
    w_ab_sw = w_ab_n.reshape(1024, 2, 2, 8)[:, :, ::-1, :].reshape(1024, 32)
    w_glu = np.ascontiguousarray(w_in[:, 4128:5152])
    w_gate = np.ascontiguousarray(w_in[:, 5152:7200])
    conv_qkv = f(inp["conv_qkv"])[0]
    ccols = []
    for h in range(8):
        for base in (0, 1024, 2048):
            ccols.append(base + h * 128)

    def mk_cw(cq):
        out = np.zeros((128, 24, 5), np.float32)
        for i, c0 in enumerate(ccols):
            out[:, i, :] = cq[:, c0:c0 + 128].T
        return out.reshape(128, 120)
    cw_n, cw_f = mk_cw(conv_qkv), mk_cw(conv_qkv[::-1])
    a_log, dt_bias = f(inp["a_log"])[0], f(inp["dt_bias"])[0]
    gp_n = np.stack([a_log.reshape(16), dt_bias.reshape(16)])
    gp_s = np.stack([a_log[::-1].reshape(16), dt_bias[::-1].reshape(16)])
    conv_dw = f(inp["conv_dw"])[0]

    def mk_cdw(cd):
        return np.ascontiguousarray(cd.T.reshape(4, 128, 31).transpose(1, 0, 2)).reshape(128, 124)
    cdw_n, cdw_f = mk_cdw(conv_dw), mk_cdw(conv_dw[::-1])
    fm = lambda v, k: np.ascontiguousarray(f(v).reshape(k, 128).T)
    cpar = np.concatenate([fm(inp["b_dw"][0], 4), fm(inp["ln_g"][0], 4), fm(inp["ln_b"][0], 4)], axis=1)
    npar = np.concatenate([fm(inp["norm1"][0], 8), fm(inp["norm2"][0], 8)], axis=1)
    common = {
        "w_ada": f(inp["w_ada"])[0], "b_ada_fm": fm(inp["b_ada"][0], 48),
        "cpar": np.ascontiguousarray(cpar), "npar": np.ascontiguousarray(npar),
        "norm_o": f(inp["norm_o"]).reshape(1, 128), "norm_f": f(inp["norm_f"]).reshape(1, 1024),
    }
    if not WT_KEYS:
        _NC_CACHE["nc"] = build_program(True)
    Wd = {"w_in_h": w_in_h, "w_glu": w_glu, "w_gate": w_gate, "w_a_out": f(inp["w_a_out"])[0],
          "w_b_out": f(inp["w_b_out"])[0], "w_o": f(inp["w_o"])[0], "w_gu": f(inp["w_gu"])[0], "w_down": f(inp["w_down"])[0]}
    wpack = np.zeros((NWT, 128, 4096), np.float32)
    for ti, (wn_, r0, nr, c0, ncw) in enumerate(WT_KEYS):
        kc_ = nr // 128
        wpack[ti, :, :kc_ * ncw] = Wd[wn_][r0:r0 + nr, c0:c0 + ncw].reshape(kc_, 128, ncw).transpose(1, 0, 2).reshape(128, kc_ * ncw)
    common["wpack"] = wpack
    ii = np.arange(128)
    mlist = [(ii[:, None] // 8 == ii[None, :] // 8)]
    for s_ in (8, 16, 32, 64):
        mlist.append((ii[:, None] // (2 * s_) == ii[None, :] // (2 * s_)) & (ii[:, None] // s_ != ii[None, :] // s_))
    common["masks"] = np.ascontiguousarray(np.stack(mlist).astype(np.float32))
    in_maps = []
    for core in range(8):
        b, r = core // 2, core % 2
        m = dict(common)
        xsb = x_sample[b]
        m["xs"] = np.ascontiguousarray(xsb if r == 0 else xsb[::-1])
        m["xp"] = np.ascontiguousarray(x_prompt[4 * core:4 * core + 4].reshape(1024, 1024))
        cv = np.stack([c_ctx, c[b]], axis=0)
        m["cT"] = np.ascontiguousarray(cv.reshape(2, 8, 128).transpose(2, 1, 0)).reshape(128, 16)
        m["w_ab"] = np.ascontiguousarray(np.stack([w_ab_n, w_ab_n if r == 0 else w_ab_sw]))
        m["cw"] = np.ascontiguousarray(np.stack([cw_n, cw_n if r == 0 else cw_f]))
        m["gpar"] = np.ascontiguousarray(np.stack([gp_n, gp_n if r == 0 else gp_s]))
        m["cdw"] = np.ascontiguousarray(np.stack([cdw_n, cdw_n if r == 0 else cdw_f]))
        m["s0"] = np.ascontiguousarray(state_delta[b, 0, r])
        sel = np.zeros((128, 2), np.float32)
        sel[:, 1 - r] = 1.0
        m["sel"] = sel
        in_maps.append(m)
    return in_maps


def kernel(**inputs):
    in_maps = _prep_inputs(inputs)
    if "nc" not in _NC_CACHE:
        _NC_CACHE["nc"] = build_program(True)
    res = run_bass_kernel_spmd(_NC_CACHE["nc"], in_maps, core_ids=list(range(8)))
    y_prompt = np.zeros((32, 256, 1024), np.float32)
    y_sample = np.zeros((4, 2048, 1024), np.float32)
    new_state = np.zeros((32, 1, 2, 8, 128, 128), np.float32)
    for core in range(8):
        r_ = res.results[core]
        b, r = core // 2, core % 2
        y_prompt[4 * core:4 * core + 4] = r_["yp"].reshape(4, 256, 1024)
        if r == 0:
            y_sample[b, 0:1024] = r_["ys"]
        else:
            y_sample[b, 1024:2048] = r_["ys"][::-1]
        new_state[4 * core:4 * core + 4, 0] = r_["st"]
    return (y_prompt, y_sample, new_state)
```

```python
import numpy as np
from contextlib import ExitStack
import concourse.bass as bass
import concourse.mybir as mybir
from concourse.bass_utils import run_bass_kernel_spmd

F32 = mybir.dt.float32
BF16 = mybir.dt.bfloat16
ALU = mybir.AluOpType
AF = mybir.ActivationFunctionType
AX = mybir.AxisListType

ENGINES = ("pe", "act", "dve", "pool", "sp")
NDMA_SEMS = 8
EPS = 1e-6
FF = 2816
GROUPS = [[0, 1], [2, 3], [4, 5], [6, 7]]
NWT = 40
WT_KEYS = []


PSUM_NAMES = ("pA", "pB", "pC", "pC0", "pC1", "pS", "pT")


class StopBuild(Exception):
    pass


class Prog:
    def __init__(self, nc):
        self.nc = nc
        self.ops = {e: [] for e in ENGINES}
        self.last_writer = {}
        self.readers = {}
        self.ndma = {e: 0 for e in ENGINES}
        self.nbar = 0

    def op(self, eng, fn, reads=(), writes=(), dma=False, drain=False, selfsig=False):
        deps = set()
        for r in reads:
            lw = self.last_writer.get(r)
            if lw is not None:
                deps.add(lw)
            if r in PSUM_NAMES:
                for rd in self.readers.get(r, ()):
                    if rd[0] != eng:
                        deps.add(rd)
        for w in writes:
            lw = self.last_writer.get(w)
            if lw is not None:
                deps.add(lw)
            for rd in self.readers.get(w, ()):
                deps.add(rd)
        idx = len(self.ops[eng])
        me = (eng, idx)
        deps.discard(me)
        best = {}
        for (pe_, pi_) in deps:
            if self.ops[pe_][pi_]["dma"]:
                continue
            if pi_ > best.get(pe_, -1):
                best[pe_] = pi_
        deps = set(d_ for d_ in deps if self.ops[d_[0]][d_[1]]["dma"] or best[d_[0]] == d_[1])
        rec = dict(fn=fn, deps=deps, dma=dma, signal=selfsig, dma_idx=None, drain=drain,
                   ndma_before=self.ndma[eng], selfsig=selfsig)
        if dma:
            rec["dma_idx"] = self.ndma[eng]
            self.ndma[eng] += 1
        self.ops[eng].append(rec)
        for w in writes:
            self.last_writer[w] = me
            self.readers[w] = []
        for r in reads:
            self.readers.setdefault(r, []).append(me)
        return me

    def pe(self, fn, reads=(), writes=()):
        return self.op("pe", fn, reads, writes)

    def act(self, fn, reads=(), writes=()):
        return self.op("act", fn, reads, writes)

    def dve(self, fn, reads=(), writes=()):
        return self.op("dve", fn, reads, writes)

    def pool(self, fn, reads=(), writes=()):
        return self.op("pool", fn, reads, writes)

    def dma(self, fn, reads=(), writes=(), q="sp"):
        return self.op(q, fn, reads, writes, dma=True)

    def barrier(self):
        self.nbar += 1
        names = []
        for e in ENGINES:
            nm = "__bar%d_%s" % (self.nbar, e)
            last = len(self.ops[e]) - 1
            me = self.op(e, None, writes=[nm], drain=True, selfsig=True)
            if last >= 0:
                self.ops[e][me[1]]["deps"].add((e, last))
                self.ops[e][me[1]]["selfdep"] = True
            names.append(nm)
        for e in ENGINES:
            self.op(e, None, reads=names, selfsig=True)
        self.last_writer = {}
        self.readers = {}

    def emit(self, sems, dma_sems):
        for e in ENGINES:
            for rec in self.ops[e]:
                for (pe_, pi_) in rec["deps"]:
                    p = self.ops[pe_][pi_]
                    if p["dma"]:
                        continue
                    if pe_ == e and e == "pe" and not rec.get("selfdep"):
                        continue
                    p["signal"] = True
        cum = {}
        for e in ENGINES:
            c = 0
            arr = []
            for rec in self.ops[e]:
                if rec["signal"] and not rec["dma"]:
                    c += 1
                arr.append(c)
            cum[e] = arr
        prog = self

        def run_engine(e, eng):
            waited = {}
            nsig = [0]

            def wait(key, sem, val):
                if val <= 0 or waited.get(key, 0) >= val:
                    return
                waited[key] = val
                eng.wait_ge(sem, val)

            def drain_dmas(n):
                for k in range(min(n, NDMA_SEMS)):
                    cnt = (n - 1 - k) // NDMA_SEMS + 1
                    wait((e, k), dma_sems[e][k], 16 * cnt)

            for rec in prog.ops[e]:
                for (pe_, pi_) in sorted(rec["deps"]):
                    p = prog.ops[pe_][pi_]
                    if p["dma"]:
                        di = p["dma_idx"]
                        s = dma_sems[pe_][di % NDMA_SEMS]
                        wait((pe_, di % NDMA_SEMS), s, 16 * (di // NDMA_SEMS + 1))
                    else:
                        if pe_ == e and e == "pe" and not rec.get("selfdep"):
                            continue
                        wait(pe_, sems[pe_], cum[pe_][pi_])
                if e == "pool" and rec["signal"] and not rec["dma"] and nsig[0] > 0:
                    wait(e, sems[e], nsig[0])
                if rec["signal"] and not rec["dma"]:
                    nsig[0] += 1
                if rec["drain"]:
                    drain_dmas(rec["ndma_before"])
                if rec["dma"]:
                    di = rec["dma_idx"]
                    s = dma_sems[e][di % NDMA_SEMS]
                    if di >= NDMA_SEMS:
                        wait((e, di % NDMA_SEMS), s, 16 * (di // NDMA_SEMS))
                    rec["fn"](eng).then_inc(s, 16)
                elif rec["selfsig"]:
                    if rec["fn"] is not None:
                        rec["fn"](eng)
                    eng.nop(nofuse=True).then_inc(sems[e], 1)
                else:
                    ins = rec["fn"](eng)
                    if rec["signal"]:
                        ins.then_inc(sems[e], 1)
            drain_dmas(prog.ndma[e])

        return run_engine


def bc1(ap, n):
    return ap.unsqueeze(2).to_broadcast([ap.shape[0], ap.shape[1], n])


def bcm(ap, n):
    return ap.unsqueeze(1).to_broadcast([ap.shape[0], n, ap.shape[1]])


def v3(ap, a):
    return ap.rearrange("p (a b) -> p a b", a=a)


HEADS = None
SEQ_D4 = False
DORDER = (0, 1)


def build_program(enable_S=True, stop=None, stats=False):
    nc = bass.Bass("TRN2", target_bir_lowering=False)

    def din(name, shape):
        return nc.dram_tensor(name, shape, F32, kind="ExternalInput").ap()

    def dout(name, shape):
        return nc.dram_tensor(name, shape, F32, kind="ExternalOutput").ap()

    xs = din("xs", [2048, 1024])
    xp = din("xp", [1024, 1024])
    cT = din("cT", [128, 16])
    w_ada = din("w_ada", [1024, 6144])
    b_ada_fm = din("b_ada_fm", [128, 48])
    wpack = din("wpack", [NWT, 128, 4096])
    w_ab = din("w_ab", [2, 1024, 32])
    cw = din("cw", [2, 128, 120])
    gpar = din("gpar", [2, 2, 16])
    cdw = din("cdw", [2, 128, 124])
    cpar = din("cpar", [128, 12])
    npar = din("npar", [128, 16])
    norm_o = din("norm_o", [1, 128])
    norm_f = din("norm_f", [1, 1024])
    s0 = din("s0", [8, 128, 128])
    sel = din("sel", [128, 2])
    masks = din("masks", [5, 128, 128])
    yp = dout("yp", [1024, 1024])
    ys = dout("ys", [1024, 1024])
    st = dout("st", [4, 2, 8, 128, 128])
    cc_in = [nc.dram_tensor("cc_in%d" % h, [128, 128], F32).ap() for h in range(8)]
    cc_out = [nc.dram_tensor("cc_out%d" % h, [256, 128], F32).ap() for h in range(8)]

    P = Prog(nc)
    es = ExitStack()

    def sb(name, shape, dt=F32, stack=es):
        return stack.enter_context(nc.sbuf_tensor(name, shape, dt))

    def psum(name, shape, dt=F32):
        return es.enter_context(nc.psum_tensor(name, shape, dt))

    sems = {e: es.enter_context(nc.semaphore("s_" + e)) for e in ("pe", "act", "dve", "pool", "sp")}
    s_cc = es.enter_context(nc.semaphore("s_cc"))
    dma_sems = {"sp": [nc.alloc_semaphore(name="dq%d" % i) for i in range(NDMA_SEMS)],
                "pool": [nc.alloc_semaphore(name="dp%d" % i) for i in range(NDMA_SEMS)]}

    AR = 159 * 256
    arena_t = sb("arena", [128, AR], F32)

    class Arena:
        def __init__(self):
            self.top = 0

        def at(self, off, nelem, dt=F32):
            nfl = nelem if dt == F32 else (nelem + 1) // 2
            assert off + nfl <= AR, (off, nfl, AR)
            ap = arena_t[:, off:off + nfl]
            return ap.bitcast(BF16) if dt == BF16 else ap

        def alloc(self, nelem, dt=F32):
            nfl = nelem if dt == F32 else (nelem + 1) // 2
            off = self.top
            self.top += nfl
            return self.at(off, nelem, dt)

    AN = Arena()

    pA = psum("pA", [128, 1024])
    pB = psum("pB", [128, 1024])
    pC = psum("pC", [128, 1024])
    pS = psum("pS", [128, 512])
    pT = psum("pT", [128, 1024], BF16)

    identb = sb("identb", [128, 128], BF16)
    identf = sb("identf", [128, 128])
    onesf = sb("onesf", [128, 128])
    onesb = sb("onesb", [128, 128], BF16)
    triU = sb("triU", [128, 128])
    triL = sb("triL", [128, 128])
    sL = sb("sL", [128, 128])
    sU = sb("sU", [128, 128])
    epsb = sb("epsb", [128, 1])
    qbias = sb("qbias", [128, 1])
    zero1 = sb("zero1", [128, 1])

    def mk_mask(t, name, op, sgn=1):
        P.pool(lambda e: e.memset(t[:], 1.0), writes=[name])
        P.pool(lambda e: e.affine_select(out=t[:], in_=t[:], pattern=[[-sgn, 128]], compare_op=op, fill=0.0,
                                         base=0, channel_multiplier=sgn), reads=[name], writes=[name])

    mk_mask(identf, "identf", ALU.is_equal)
    mk_mask(triU, "triU", ALU.is_ge, -1)
    mk_mask(triL, "triL", ALU.is_ge, 1)
    mk_mask(sL, "sL", ALU.is_gt, 1)
    mk_mask(sU, "sU", ALU.is_gt, -1)
    P.pool(lambda e: e.memset(onesf[:], 1.0), writes=["onesf"])
    P.pool(lambda e: e.memset(onesb[:], 1.0), writes=["onesb"])
    P.pool(lambda e: e.memset(epsb[:], EPS), writes=["epsb"])
    P.pool(lambda e: e.memset(qbias[:], float(-0.5 * np.log(128.0))), writes=["qbias"])
    P.pool(lambda e: e.memset(zero1[:], 0.0), writes=["zero1"])
    P.pool(lambda e: e.tensor_copy(out=identb[:], in_=identf[:]), reads=["identf"], writes=["identb"])
    TRI = [triU, triL]
    TRIN = ["triU", "triL"]
    STR = [sL, sU]
    STRN = ["sL", "sU"]

    cTt = sb("cTt", [128, 16])
    scT = sb("scT", [128, 16])
    badaf = sb("badaf", [128, 48])
    modT = sb("modT", [128, 96])
    cwt = sb("cwt", [128, 240])
    cdwt = sb("cdwt", [128, 248])
    cpart = sb("cpart", [128, 12])
    npart = sb("npart", [128, 16])
    normo_bc = sb("normo_bc", [128, 128])
    gpt = sb("gpt", [128, 64])
    nea = sb("nea", [128, 32])
    selt = sb("selt", [128, 2])
    mkt = sb("mkt", [128, 5 * 128], BF16)
    mk = [mkt[:, i * 128:(i + 1) * 128] for i in range(5)]
    sc1t = sb("sc1t", [128, 16])
    sc2t = sb("sc2t", [128, 16])
    gbc = sb("gbc", [128, 4096])

    P.dma(lambda e: e.dma_start(out=cTt[:], in_=cT), writes=["cTt"])
    P.dma(lambda e: e.dma_start(out=badaf[:], in_=b_ada_fm), writes=["badaf"])
    P.dma(lambda e: e.dma_start(out=v3(cwt[:], 2), in_=cw.rearrange("j p f -> p j f")), writes=["cwt"])
    P.dma(lambda e: e.dma_start(out=v3(cdwt[:], 2), in_=cdw.rearrange("j p f -> p j f")), writes=["cdwt"])
    P.dma(lambda e: e.dma_start(out=cpart[:], in_=cpar), writes=["cpart"])
    P.dma(lambda e: e.dma_start(out=npart[:], in_=npar), writes=["npart"])
    P.dma(lambda e: e.dma_start(out=normo_bc[:], in_=norm_o.partition_broadcast(128)), writes=["normo_bc"])
    P.dma(lambda e: e.dma_start(out=gpt[:], in_=gpar.rearrange("j a b -> (j a b)").partition_broadcast(128)),
          writes=["gpt"])
    P.dma(lambda e: e.dma_start(out=selt[:], in_=sel), writes=["selt"])
    P.dma(lambda e: e.dma_start(out=v3(mkt[:], 5), in_=masks.rearrange("m p f -> p m f")), writes=["mkt"], q="pool")
    for j in range(2):
        P.act(lambda e, j=j: e.activation(out=nea[:, j * 16:(j + 1) * 16], in_=gpt[:, j * 32:j * 32 + 16], func=AF.Exp),
              reads=["gpt"], writes=["nea"])
    P.dve(lambda e: e.tensor_scalar(out=nea[:], in0=nea[:], scalar1=-1.0, scalar2=None, op0=ALU.mult),
          reads=["nea"], writes=["nea"])

    SKIP_ADA = (stop == 'const')
    NWB = 3
    wbuf = [sb("wb%d" % i, [128, 8 * 512], BF16) for i in range(NWB)]
    wstate = {"n": 0}

    WT = {}

    def wkey(name, r0, nr, c0, ncw):
        return (name, r0, nr, c0, ncw)

    def load_w(key, kc, ncols):
        if key not in WT:
            WT[key] = len(WT)
            assert len(WT) <= NWT
        assert key[2] == kc * 128 and key[4] == ncols
        src3 = wpack[WT[key]][:, 0:kc * ncols].rearrange("p (k n) -> p k n", k=kc)
        i = wstate["n"] % NWB
        wstate["n"] += 1
        nm = "wb%d" % i
        view = wbuf[i][:, 0:kc * ncols].rearrange("p (k n) -> p k n", k=kc)
        P.dma(lambda e: e.dma_start(out=view, in_=src3), writes=[nm], q="pool")
        return view, nm

    def wview(w, r0, nr, c0, ncw):
        return w[r0:r0 + nr, c0:c0 + ncw].rearrange("(k p) n -> p k n", p=128)

    if not SKIP_ADA:
        AN.top = 0
        wa = [AN.alloc(8 * 512) for i in range(2)]
        P.act(lambda e: e.activation(out=scT[:], in_=cTt[:], func=AF.Silu), reads=["cTt"], writes=["scT"])
        scT3 = v3(scT[:], 8)
        for g in range(12):
            wt = wa[g % 2]
            nm = "wa%d" % (g % 2)
            wt3 = wt[:].rearrange("p (k n) -> p k n", k=8)
            P.dma(lambda e, g=g, wt3=wt3: e.dma_start(out=wt3, in_=wview(w_ada, 0, 1024, g * 512, 512)), writes=[nm])
            for jj in range(4):
                j = g * 4 + jj
                for dc in range(8):
                    P.pe(lambda e, j=j, jj=jj, dc=dc, wt3=wt3: e.matmul(
                        pS[:, 2 * j:2 * j + 2], lhsT=wt3[:, dc, jj * 128:(jj + 1) * 128], rhs=scT3[:, dc, :],
                        start=(dc == 0), stop=(dc == 7)), reads=[nm, "scT"], writes=["pS"])
        P.dve(lambda e: e.tensor_tensor(out=v3(modT[:], 48), in0=v3(pS[:, 0:96], 48), in1=bc1(badaf[:], 2), op=ALU.add),
              reads=["pS", "badaf"], writes=["modT"])
        modT3 = v3(modT[:], 48)
        ADA1 = (stop == 'ada1')
        for job in range(0 if ADA1 else 2):
            for (dst, dn, c0, nrow) in ((sc1t, "sc1t", 8, 0), (sc2t, "sc2t", 32, 1)):
                P.dve(lambda e, dst=dst, c0=c0, nrow=nrow, job=job: e.scalar_tensor_tensor(
                    out=dst[:, job * 8:(job + 1) * 8], in0=modT3[:, c0:c0 + 8, job], scalar=1.0,
                    in1=npart[:, nrow * 8:(nrow + 1) * 8], op0=ALU.add, op1=ALU.mult),
                    reads=["modT", "npart"], writes=[dn])
        dg = AN.alloc(1024)
        for job in range(0 if (ADA1 or stop == 'ada2') else 2):
            for which, c0 in ((0, 16), (1, 40)):
                for c in range(8):
                    P.dve(lambda e, c=c, c0=c0, job=job: e.tensor_scalar(
                        out=dg[:, c * 128:(c + 1) * 128], in0=identf[:], scalar1=modT3[:, c0 + c, job:job + 1],
                        scalar2=None, op0=ALU.mult), reads=["identf", "modT"], writes=["dg"])
                for c in range(8):
                    P.pe(lambda e, c=c: e.matmul(pA[:, c * 128:(c + 1) * 128], lhsT=onesf[:], rhs=dg[:, c * 128:(c + 1) * 128],
                                                 start=True, stop=True), reads=["onesf", "dg"], writes=["pA"])
                off = (job * 2 + which) * 1024
                for hf in range(2):
                    P.act(lambda e, off=off, hf=hf: e.copy(out=gbc[:, off + hf * 512:off + (hf + 1) * 512], in_=pA[:, hf * 512:(hf + 1) * 512]), reads=["pA"], writes=["gbc"])
    P.barrier()

    def run_job(job):
        isS = (job == 1)
        xin = xs if isS else xp
        yout = ys if isS else yp
        nseq = 1 if isS else 4
        AN.top = 0
        xres = AN.alloc(8 * 1024)
        hT = AN.alloc(8 * 1024, BF16)
        hT3 = hT[:].rearrange("p (k n) -> p k n", k=8)
        mB_off = AN.top
        mB = AN.alloc(8 * 1024, BF16)
        mB3 = mB[:].rearrange("p (k n) -> p k n", k=8)
        ogT_off = AN.top
        ogT = AN.alloc(8 * 1024, BF16)
        ogT3 = ogT[:].rearrange("p (k n) -> p k n", k=8)
        stat = AN.alloc(64)
        hT2 = AN.alloc(16, BF16)
        T0 = AN.top
        xres3 = xres[:].rearrange("p (t n) -> p t n", t=8)
        g1bc = gbc[:, (job * 2) * 1024:(job * 2 + 1) * 1024]
        g2bc = gbc[:, (job * 2 + 1) * 1024:(job * 2 + 2) * 1024]

        def norm_transpose(src_ap, srcname, dst3, dstname, t, sct, shc0, ph, tag, part=0):
            junk = ph["junk"]
            xn = ph["xn"][t % 2]
            xnn = "xn%d" % (t % 2)
            tmpf = ph["tmpf"]
            col = stat[:, (t % 16) * 3:(t % 16) * 3 + 3]
            if part in (0, 1):
                P.act(lambda e: e.activation(out=junk[:], in_=src_ap, func=AF.Square, accum_out=col[:, 0:1]),
                      reads=[srcname], writes=["junk", "stat"])
                P.act(lambda e: e.activation(out=col[:, 1:2], in_=col[:, 0:1], func=AF.Ln, scale=1.0 / 1024, bias=epsb[:, 0:1]),
                      reads=["stat", "epsb"], writes=["stat"])
                P.act(lambda e: e.activation(out=col[:, 2:3], in_=col[:, 1:2], func=AF.Exp, scale=-0.5),
                      reads=["stat"], writes=["stat"])
                P.dve(lambda e: e.tensor_scalar(out=xn[:], in0=src_ap, scalar1=col[:, 2:3], scalar2=None, op0=ALU.mult),
                      reads=[srcname, "stat"], writes=[xnn])
            if part == 1:
                return
            for dc in range(8):
                P.pe(lambda e, dc=dc: e.transpose(pT[:, dc * 128:(dc + 1) * 128], xn[:, dc * 128:(dc + 1) * 128], identb[:]),
                     reads=[xnn, "identb"], writes=["pT"])
            P.dve(lambda e: e.tensor_tensor(out=v3(tmpf[:], 8), in0=v3(pT[:], 8), in1=bc1(sct, 128), op=ALU.mult),
                  reads=["pT", "sc1t", "sc2t"], writes=["tmpf"])
            (P.dve if tag == "f" else P.pool)(lambda e: e.tensor_tensor(out=dst3, in0=v3(tmpf[:], 8),
                                             in1=bc1(modT3[:, shc0:shc0 + 8, job], 128), op=ALU.add),
                   reads=["tmpf", "modT"], writes=[dstname])

        if True:
            AN.top = T0
            phtA = {"junk": AN.alloc(1024, BF16),
                   "xn": [AN.alloc(1024, BF16) for i in range(2)],
                   "tmpf": AN.alloc(1024)}
            xh = [AN.alloc(1024) for i in range(2)]
            hTh = AN.at(mB_off, 8 * 1024, BF16) if isS else None
            hTh3 = hTh[:].rearrange("p (k n) -> p k n", k=8) if isS else None
            for t in range(8):
                P.dma(lambda e, t=t: e.dma_start(out=xres3[:, t, :], in_=xin[t * 128:(t + 1) * 128, :]),
                      writes=["xres"])
                norm_transpose(xres3[:, t, :], "xres", hT3[:, :, t * 128:(t + 1) * 128], "hT", t,
                               sc1t[:, job * 8:(job + 1) * 8], 0, phtA, "a")
            if isS:
                for t in range(8):
                    xb_ = xh[t % 2]
                    xbn = "xh%d" % (t % 2)
                    P.dma(lambda e, t=t, xb_=xb_: e.dma_start(out=xb_[:], in_=xin[1024 + t * 128:1024 + (t + 1) * 128, :]),
                          writes=[xbn])
                    norm_transpose(xb_[:], xbn, hTh3[:, :, t * 128:(t + 1) * 128], "mB", 8 + t,
                                   sc1t[:, job * 8:(job + 1) * 8], 0, phtA, "h")
                P.pool(lambda e: e.tensor_copy(out=v3(hT2[:], 8), in_=hTh3[:, :, 0:2]), reads=["mB"], writes=["hT2"])
            else:
                P.pool(lambda e: e.memset(hT2[:], 0.0), writes=["hT2"])
            if stop == 'A':
                raise StopBuild()


            upads = [AN.alloc(2944), AN.alloc(2944)]
            cvo = AN.alloc(4 * 1024)
            cvo3 = cvo[:].rearrange("p (c n) -> p c n", c=4)
            ub = AN.at(ogT_off, 4 * 1024, BF16)
            ub3 = ub[:].rearrange("p (c n) -> p c n", c=4)
            sgtC = AN.alloc(1024)
            lnt = [AN.alloc(512) for i in range(4)]
            wa_v, wa_n = load_w(wkey("w_glu", 0, 1024, 0, 512), 8, 512)
            wb_v, wb_n = load_w(wkey("w_glu", 0, 1024, 512, 512), 8, 512)
            cdw3 = cdwt[:, job * 124:(job + 1) * 124].rearrange("p (c k) -> p c k", c=4)
            def gen_glu(cc):
                up = upads[cc % 2]
                upn = "upad%d" % (cc % 2)
                P.pool(lambda e, up=up: e.memset(up[:], 0.0), writes=[upn])
                passes = [(hT3, "hT", hf, False) for hf in range(2)]
                if isS and cc >= 2:
                    passes += [(hTh3, "mB", hf, True) for hf in range(2)]
                for (src3, srcn, hf, halo) in passes:
                    for dc in range(8):
                        P.pe(lambda e, dc=dc, src3=src3, hf=hf, cc=cc: e.matmul(
                            pA[:, 0:512], lhsT=wa_v[:, dc, cc * 128:(cc + 1) * 128], rhs=src3[:, dc, hf * 512:(hf + 1) * 512],
                            start=(dc == 0), stop=(dc == 7)), reads=[wa_n, srcn], writes=["pA"])
                    for dc in range(8):
                        P.pe(lambda e, dc=dc, src3=src3, hf=hf, cc=cc: e.matmul(
                            pB[:, 0:512], lhsT=wb_v[:, dc, cc * 128:(cc + 1) * 128], rhs=src3[:, dc, hf * 512:(hf + 1) * 512],
                            start=(dc == 0), stop=(dc == 7)), reads=[wb_n, srcn], writes=["pB"])
                    P.act(lambda e: e.activation(out=sgtC[:, 0:512], in_=pB[:, 0:512], func=AF.Sigmoid),
                          reads=["pB"], writes=["sgtC"])
                    if not isS:
                        dstv = up[:, 0:4 * 286].rearrange("p (s w) -> p s w", s=4)[:, 2 * hf:2 * hf + 2, 15:271]
                        inA = v3(pA[:, 0:512], 2)
                        inS = v3(sgtC[:, 0:512], 2)
                    elif cc < 2:
                        dstv = up[:, 0:16 * 94].rearrange("p (s w) -> p s w", s=16)[:, 8 * hf:8 * hf + 8, 15:79]
                        inA = v3(pA[:, 0:512], 8)
                        inS = v3(sgtC[:, 0:512], 8)
                    else:
                        r0 = (31 + 8 * hf) if halo else (15 + 8 * hf)
                        n = 448 if (halo and hf == 1) else 512
                        dstv = up[:, r0 * 64:r0 * 64 + n]
                        inA = pA[:, 0:n]
                        inS = sgtC[:, 0:n]
                    P.dve(lambda e, dstv=dstv, inA=inA, inS=inS: e.tensor_tensor(out=dstv, in0=inA, in1=inS, op=ALU.mult),
                          reads=["pA", "sgtC"], writes=[upn])
                    yield

            def gen_conv(cc):
                up = upads[cc % 2]
                upn = "upad%d" % (cc % 2)
                if not isS:
                    def iv(j, up=up):
                        return up[:, 0:4 * 286].rearrange("p (s w) -> p s w", s=4)[:, :, j:j + 256]
                    ov = v3(cvo3[:, cc, :], 4)
                elif cc < 2:
                    def iv(j, up=up):
                        return up[:, 0:16 * 94].rearrange("p (s w) -> p s w", s=16)[:, :, j:j + 64]
                    ov = v3(cvo3[:, cc, :], 16)
                else:
                    def iv(j, up=up):
                        return up[:, j * 64:j * 64 + 1024]
                    ov = cvo3[:, cc, :]
                P.dve(lambda e, ov=ov, iv=iv, cc=cc: e.tensor_scalar(
                    out=ov, in0=iv(0), scalar1=cdw3[:, cc, 0:1], scalar2=cpart[:, cc:cc + 1], op0=ALU.mult, op1=ALU.add),
                    reads=[upn, "cdwt", "cpart"], writes=["cvo"])
                for j in range(1, 31):
                    P.dve(lambda e, ov=ov, iv=iv, cc=cc, j=j: e.scalar_tensor_tensor(
                        out=ov, in0=iv(j), scalar=cdw3[:, cc, j:j + 1], in1=ov, op0=ALU.mult, op1=ALU.add),
                        reads=[upn, "cdwt", "cvo"], writes=["cvo"])
                    if j % 6 == 0:
                        yield

                yield

            def run_gens_c(gs):
                alive = [True] * len(gs)
                while any(alive):
                    for i_ in range(len(gs)):
                        if alive[i_]:
                            try:
                                next(gs[i_])
                            except StopIteration:
                                alive[i_] = False
            run_gens_c([gen_glu(0)])
            for cc in range(4):
                run_gens_c([gen_conv(cc)] + ([gen_glu(cc + 1)] if cc < 3 else []))
            for hf in range(2):
                sl = slice(hf * 512, (hf + 1) * 512)
                for cc in range(4):
                    P.pe(lambda e, cc=cc, sl=sl: e.matmul(pA[:, 0:512], lhsT=onesf[:], rhs=cvo3[:, cc, sl],
                                                          start=(cc == 0), stop=(cc == 3)), reads=["onesf", "cvo"], writes=["pA"])
                for cc in range(4):
                    P.pool(lambda e, cc=cc, sl=sl: e.tensor_tensor(out=lnt[cc % 2][:], in0=cvo3[:, cc, sl], in1=cvo3[:, cc, sl],
                                                                   op=ALU.mult), reads=["cvo"], writes=["lnt%d" % (cc % 2)])
                    P.pe(lambda e, cc=cc: e.matmul(pB[:, 0:512], lhsT=onesf[:], rhs=lnt[cc % 2][:],
                                                   start=(cc == 0), stop=(cc == 3)), reads=["onesf", "lnt%d" % (cc % 2)], writes=["pB"])
                mean, msq, var = lnt[2], lnt[3], lnt[0]
                P.dve(lambda e: e.tensor_scalar(out=mean[:], in0=pA[:, 0:512], scalar1=1.0 / 512, scalar2=None, op0=ALU.mult),
                      reads=["pA"], writes=["lnt2"])
                P.pool(lambda e: e.tensor_tensor(out=msq[:], in0=mean[:], in1=mean[:], op=ALU.mult), reads=["lnt2"], writes=["lnt3"])
                P.dve(lambda e: e.scalar_tensor_tensor(out=var[:], in0=pB[:, 0:512], scalar=1.0 / 512, in1=msq[:],
                                                       op0=ALU.mult, op1=ALU.subtract), reads=["pB", "lnt3"], writes=["lnt0"])
                P.act(lambda e: e.activation(out=var[:], in_=var[:], func=AF.Ln, bias=epsb[:, 0:1]), reads=["lnt0", "epsb"], writes=["lnt0"])
                P.act(lambda e: e.activation(out=var[:], in_=var[:], func=AF.Exp, scale=-0.5), reads=["lnt0"], writes=["lnt0"])
                for cc in range(4):
                    P.pool(lambda e, cc=cc, sl=sl: e.tensor_tensor(out=lnt[1][:], in0=cvo3[:, cc, sl], in1=mean[:], op=ALU.subtract),
                           reads=["cvo", "lnt2"], writes=["lnt1"])
                    P.pool(lambda e: e.tensor_tensor(out=lnt[1][:], in0=lnt[1][:], in1=var[:], op=ALU.mult),
                           reads=["lnt1", "lnt0"], writes=["lnt1"])
                    P.act(lambda e, cc=cc, sl=sl: e.activation(out=ub3[:, cc, sl], in_=lnt[1][:], func=AF.Silu,
                                                               scale=cpart[:, 4 + cc:5 + cc], bias=cpart[:, 8 + cc:9 + cc]),
                          reads=["lnt1", "cpart"], writes=["ogT"])
            for cn in range(2):
                wv, wn = load_w(wkey("w_b_out", 0, 512, cn * 512, 512), 4, 512)
                gv, gn = load_w(wkey("w_gate", 0, 1024, 1024 + cn * 512, 512), 8, 512)
                for jj in range(4):
                    j = cn * 4 + jj
                    for hf in range(2):
                        ui = j * 2 + hf
                        pX, pXn = [(pA, ["pA"]), (pB, ["pB"]), (pC, ["pC0", "pC1"])][ui % 3]
                        sg, sgn = [(sgtC[:, 0:512], "sgC0"), (sgtC[:, 512:1024], "sgC1"), (lnt[0][:], "lnt0")][ui % 3]
                        hs_ = slice(hf * 512, (hf + 1) * 512)
                        for kc in range(4):
                            P.pe(lambda e, kc=kc, jj=jj, hs_=hs_, wv=wv, pX=pX: e.matmul(
                                pX[:, 0:512], lhsT=wv[:, kc, jj * 128:(jj + 1) * 128],
                                rhs=ub3[:, kc, hs_], start=(kc == 0), stop=(kc == 3)),
                                reads=[wn, "ogT"], writes=pXn)
                        for dc in range(8):
                            P.pe(lambda e, dc=dc, jj=jj, hs_=hs_, gv=gv, pX=pX: e.matmul(
                                pX[:, 512:1024], lhsT=gv[:, dc, jj * 128:(jj + 1) * 128],
                                rhs=hT3[:, dc, hs_], start=(dc == 0), stop=(dc == 7)),
                                reads=[gn, "hT"], writes=pXn)
                        P.act(lambda e, pX=pX, sg=sg: e.activation(out=sg, in_=pX[:, 512:1024], func=AF.Sigmoid), reads=pXn, writes=[sgn])
                        P.dve(lambda e, j=j, hs_=hs_, pX=pX, sg=sg: e.tensor_tensor(out=mB3[:, j, hs_], in0=pX[:, 0:512], in1=sg, op=ALU.mult),
                              reads=pXn + [sgn], writes=["mB"])
        P.barrier()

        if stop == 'C':
            raise StopBuild()
        if True:
            AN.top = T0

            def t(name, shape, dt=F32):
                return AN.alloc(shape[1], dt)
            abt = t("abt", [128, 8 * 32])
            abt3 = v3(abt[:], 8)
            gt = t("gt", [128, 8 * 16])
            bt = t("bt", [128, 8 * 16])
            gc = t("gc", [128, 8 * 16])
            gtot = t("gtot", [128, 8 * 16])
            egc = t("egc", [128, 8 * 16])
            ekd = t("ekd", [128, 8 * 16])
            egt = t("egt", [128, 8 * 16])
            bgc = t("bgc", [128, 8 * 16])
            gt3, bt3, gc3, gtot3 = v3(gt[:], 8), v3(bt[:], 8), v3(gc[:], 8), v3(gtot[:], 8)
            egc3, ekd3, egt3, bgc3 = v3(egc[:], 8), v3(ekd[:], 8), v3(egt[:], 8), v3(bgc[:], 8)
            wabt = t("wabt", [128, 8 * 32], BF16)
            wab3 = v3(wabt[:], 8)
            P.dma(lambda e: e.dma_start(out=wab3, in_=w_ab[job].rearrange("(k p) n -> p k n", p=128)),
                  writes=["wabt"], q="pool")
            for tt in range(8):
                for dc in range(8):
                    P.pe(lambda e, tt=tt, dc=dc: e.matmul(pS[:, tt * 32:(tt + 1) * 32], lhsT=hT3[:, dc, tt * 128:(tt + 1) * 128],
                                                          rhs=wab3[:, dc, :], start=(dc == 0), stop=(dc == 7)),
                         reads=["hT", "wabt"], writes=["pS"])
            P.act(lambda e: e.copy(out=abt[:], in_=pS[:, 0:256]), reads=["pS"], writes=["abt"])
            gp3 = gpt[:, job * 32:(job + 1) * 32]
            P.dve(lambda e: e.tensor_tensor(out=gt3, in0=abt3[:, :, 0:16], in1=bcm(gp3[:, 16:32], 8), op=ALU.add),
                  reads=["abt", "gpt"], writes=["gt"])
            P.act(lambda e: e.activation(out=gt[:], in_=gt[:], func=AF.Exp), reads=["gt"], writes=["gt"])
            P.act(lambda e: e.activation(out=gt[:], in_=gt[:], func=AF.Ln, bias=onesf[:, 0:1]), reads=["gt", "onesf"], writes=["gt"])
            P.dve(lambda e: e.tensor_tensor(out=gt3, in0=gt3, in1=bcm(nea[:, job * 16:(job + 1) * 16], 8), op=ALU.mult),
                  reads=["gt", "nea"], writes=["gt"])
            P.act(lambda e: e.activation(out=bt3, in_=abt3[:, :, 16:32], func=AF.Sigmoid), reads=["abt"], writes=["bt"])
            for tt in range(8):
                for d in range(2):
                    P.pe(lambda e, tt=tt, d=d: e.matmul(pS[:, tt * 16 + d * 8:tt * 16 + d * 8 + 8], lhsT=TRI[d][:],
                                                        rhs=gt3[:, tt, d * 8:d * 8 + 8], start=True, stop=True),
                         reads=[TRIN[d], "gt"], writes=["pS"])
                P.pe(lambda e, tt=tt: e.matmul(pS[:, 128 + tt * 16:128 + (tt + 1) * 16], lhsT=onesf[:], rhs=gt3[:, tt, :],
                                               start=True, stop=True), reads=["onesf", "gt"], writes=["pS"])
            P.dve(lambda e: e.tensor_copy(out=gc[:], in_=pS[:, 0:128]), reads=["pS"], writes=["gc"])
            P.dve(lambda e: e.tensor_copy(out=gtot[:], in_=pS[:, 128:256]), reads=["pS"], writes=["gtot"])
            P.act(lambda e: e.activation(out=egc[:], in_=gc[:], func=AF.Exp), reads=["gc"], writes=["egc"])
            P.act(lambda e: e.activation(out=egt[:], in_=gtot[:], func=AF.Exp), reads=["gtot"], writes=["egt"])
            P.dve(lambda e: e.tensor_tensor(out=ekd[:], in0=gtot[:], in1=gc[:], op=ALU.subtract), reads=["gtot", "gc"], writes=["ekd"])
            P.act(lambda e: e.activation(out=ekd[:], in_=ekd[:], func=AF.Exp), reads=["ekd"], writes=["ekd"])
            P.dve(lambda e: e.tensor_tensor(out=bgc[:], in0=bt[:], in1=egc[:], op=ALU.mult), reads=["bt", "egc"], writes=["bgc"])

            AN.topA = 0

            def t2(n, dt=F32):
                nfl = n if dt == F32 else (n + 1) // 2
                if AN.topA + nfl <= 8192:
                    off = AN.topA
                    AN.topA += nfl
                    return AN.at(off, n, dt)
                return AN.alloc(n, dt)

            class BDir:
                pass
            BD = []
            for d_ in range(2):
                b_ = BDir()
                b_.X = [t2(1040 if (i_ == 0 or (d_ == 0 and i_ == 3)) else 1024) for i_ in range(4)]
                b_.Lb = [t2(1024, BF16) for _ in range(2)]
                b_.Nb = [t2(1024, BF16) for _ in range(2)]
                b_.Nfull = t2(1024, BF16)
                b_.Ttb, b_.attT, b_.qdT, b_.vbt, b_.kdt, b_.nwT = (t2(1024, BF16) for _ in range(6))
                b_.Sf = [t2(128) for _ in range(4)]
                b_.Sb = [t2(128, BF16) for _ in range(4)]
                b_.vnb = [t2(128, BF16) for _ in range(4)]
                BD.append(b_)

            def R0(nm):
                return nm + "_0"
            X1, X2, X3, X4 = BD[0].X
            raw, cvq, rq = X1, X2[:, 0:1024], X3[:, 0:1024]
            osq = X4[:, 0:1024]
            sq = t2(1024, BF16)
            ogt = t2(1024, BF16)
            fmT = [t2(1024, BF16) for i in range(3)]
            ktm = t2(1024, BF16)
            vtm = t2(1024, BF16)
            szt = t2(1024, BF16)
            osum = t2(1024)
            cw3 = cwt[:, job * 120:(job + 1) * 120].rearrange("p (c k) -> p c k", c=24)
            if stop == 'Dg':
                raise StopBuild()


            def d1_gen(h, wv, wn):
                def gen_ci(ci):
                    c24 = h * 3 + ci
                    pI, pIn = [(pA, ["pA"]), (pB, ["pB"]), (pC, ["pC0", "pC1"])][ci]
                    raw_, rawn = [(BD[0].X[0], "X1_0"), (BD[1].X[0], "X1_1"), (BD[0].X[3], "X4_0")][ci]
                    cvq_, cvqn = [(BD[0].X[1], "X2_0"), (BD[1].X[1], "X2_1"), (BD[1].X[3], "X4_1")][ci]
                    cvq_ = cvq_[:, 0:1024]
                    rq_, rqn = [(BD[0].X[2], "X3_0"), (BD[1].X[2], "X3_1"), (None, None)][ci]
                    if rq_ is not None:
                        rq_ = rq_[:, 0:1024]
                    for hf in range(2):
                        for dc in range(8):
                            P.pe(lambda e, dc=dc, hf=hf: e.matmul(
                                pI[:, hf * 512:(hf + 1) * 512], lhsT=wv[:, dc, ci * 128:(ci + 1) * 128],
                                rhs=hT3[:, dc, hf * 512:(hf + 1) * 512], start=(dc == 0), stop=(dc == 7)),
                                reads=[wn, "hT"], writes=pIn)
                    P.pool(lambda e: e.memset(raw_[:], 0.0), writes=[rawn])
                    if isS:
                        hv = v3(hT2[:], 8)
                        for dc in range(8):
                            P.pe(lambda e, dc=dc: e.matmul(
                                pS[:, 0:2], lhsT=wv[:, dc, ci * 128:(ci + 1) * 128], rhs=hv[:, dc, :],
                                start=(dc == 0), stop=(dc == 7)), reads=[wn, "hT2"], writes=["pS"])
                        P.act(lambda e: e.copy(out=raw_[:, 1026:1028], in_=pS[:, 0:2]), reads=["pS"], writes=[rawn])
                        P.act(lambda e: e.copy(out=raw_[:, 2:1026], in_=pI[:]), reads=pIn, writes=[rawn])

                        def iv(j):
                            return raw_[:, j:j + 1024]
                        ov = cvq_
                    else:
                        P.act(lambda e: e.copy(out=v3(raw_[:, 0:1040], 4)[:, :, 2:258], in_=v3(pI[:], 4)), reads=pIn, writes=[rawn])

                        def iv(j):
                            return v3(raw_[:, 0:1040], 4)[:, :, j:j + 256]
                        ov = v3(cvq_, 4)
                    yield
                    P.dve(lambda e: e.tensor_scalar(
                        out=ov, in0=iv(0), scalar1=cw3[:, c24, 0:1], scalar2=None, op0=ALU.mult),
                        reads=[rawn, "cwt"], writes=[cvqn])
                    for j in range(1, 5):
                        P.dve(lambda e, j=j: e.scalar_tensor_tensor(
                            out=ov, in0=iv(j), scalar=cw3[:, c24, j:j + 1], in1=ov, op0=ALU.mult, op1=ALU.add),
                            reads=[rawn, "cwt", cvqn], writes=[cvqn])
                    yield
                    if ci == 2:
                        P.act(lambda e: e.activation(out=fmT[2][:], in_=cvq_, func=AF.Silu), reads=[cvqn], writes=["fmT2"])
                    else:
                        P.act(lambda e: e.activation(out=cvq_, in_=cvq_, func=AF.Silu), reads=[cvqn], writes=[cvqn])
                        yield
                        P.act(lambda e: e.activation(out=sq[:], in_=cvq_, func=AF.Square), reads=[cvqn], writes=["sq"])
                        for hf in range(2):
                            P.pe(lambda e, hf=hf: e.matmul(pI[:, hf * 512:(hf + 1) * 512], lhsT=onesb[:], rhs=sq[:, hf * 512:(hf + 1) * 512],
                                                           start=True, stop=True), reads=["onesb", "sq"], writes=pIn)
                        P.act(lambda e: e.activation(out=rq_, in_=pI[:], func=AF.Ln, bias=epsb[:, 0:1]), reads=pIn + ["epsb"], writes=[rqn])
                        yield
                        P.act(lambda e: e.activation(out=rq_, in_=rq_, func=AF.Exp, scale=-0.5,
                                                     bias=(qbias[:, 0:1] if ci == 0 else zero1[:, 0:1])),
                              reads=[rqn, "qbias", "zero1"], writes=[rqn])
                        P.dve(lambda e: e.tensor_tensor(out=fmT[ci][:], in0=cvq_, in1=rq_, op=ALU.mult),
                              reads=[cvqn, rqn], writes=["fmT%d" % ci])
                    yield

                gl = [gen_ci(0), gen_ci(1), gen_ci(2)]
                alive = [True, True, True]
                while any(alive):
                    for i_ in range(3):
                        if alive[i_]:
                            try:
                                next(gl[i_])
                            except StopIteration:
                                alive[i_] = False
                    yield
                for (src, srcn, dst, dstn) in ((fmT[1], "fmT1", ktm, "ktm"), (fmT[2], "fmT2", vtm, "vtm")):
                    for tt in range(8):
                        P.pe(lambda e, tt=tt, src=src: e.transpose(pT[:, tt * 128:(tt + 1) * 128], src[:, tt * 128:(tt + 1) * 128], identb[:]),
                             reads=[srcn, "identb"], writes=["pT"])
                    P.act(lambda e, dst=dst: e.copy(out=dst[:], in_=pT[:]), reads=["pT"], writes=[dstn])
                    yield
                for tt in range(8):
                    for dc in range(8):
                        P.pe(lambda e, tt=tt, dc=dc: e.matmul(
                            pC[:, tt * 128:(tt + 1) * 128], lhsT=hT3[:, dc, tt * 128:(tt + 1) * 128], rhs=wv[:, dc, 384:512],
                            start=(dc == 0), stop=(dc == 7)), reads=[wn, "hT"], writes=["pC0", "pC1"])
                P.act(lambda e: e.activation(out=szt[:], in_=pC[:], func=AF.Silu), reads=["pC0", "pC1"], writes=["szt"])
                yield

            def d23(h):
                P.pool(lambda e: e.memset(osum[:], 0.0), writes=["osum"])
                def gen_dir(d):
                    bd = BD[d]

                    def R(nm):
                        return nm + "_%d" % d
                    X1, X2, X3, X4 = bd.X
                    gtri, dd, esym, eL = X1[:, 0:1024], X2[:, 0:1024], X3[:, 0:1024], X4[:, 0:1024]
                    ebc, Ttf, eA = X1[:, 0:1024], X2[:, 0:1024], X4[:, 0:1024]
                    cct = X3[:, 0:256]
                    Lb, Nb, Nfull = bd.Lb, bd.Nb, bd.Nfull
                    kbg = Lb[0]
                    Ttb, attT, qdT, vbt, kdt, nwT = bd.Ttb, bd.attT, bd.qdT, bd.vbt, bd.kdt, bd.nwT
                    Sf, Sb, vnb = bd.Sf, bd.Sb, bd.vnb

                    def init_state(ci_, d_):
                        if isS and d_ == 0:
                            P.dma(lambda e: e.dma_start(out=Sf[ci_][:], in_=s0[h]), writes=[R("Sf%d" % ci_)])
                        elif isS:
                            P.dma(lambda e: e.dma_start(out=v3(cct, 2), in_=cc_out[h].rearrange("(r p) f -> p r f", p=128)),
                                  reads=["cc_out%d" % h], writes=[R("X3")])
                            P.dve(lambda e: e.tensor_scalar(out=Sf[ci_][:], in0=cct[:, 0:128], scalar1=selt[:, 0:1], scalar2=None, op0=ALU.mult),
                                  reads=[R("X3"), "selt"], writes=[R("Sf%d" % ci_)])
                            P.dve(lambda e: e.scalar_tensor_tensor(out=Sf[ci_][:], in0=cct[:, 128:256], scalar=selt[:, 1:2], in1=Sf[ci_][:],
                                                                   op0=ALU.mult, op1=ALU.add), reads=[R("X3"), "selt", R("Sf%d" % ci_)], writes=[R("Sf%d" % ci_)])
                        else:
                            P.pool(lambda e: e.memset(Sf[ci_][:], 0.0), writes=[R("Sf%d" % ci_)])
                        P.act(lambda e: e.copy(out=Sb[ci_][:], in_=Sf[ci_][:]), reads=[R("Sf%d" % ci_)], writes=[R("Sb%d" % ci_)])

                    def run_chains(chs, d_):
                        col = d_ * 8 + h
                        nst = len(chs[0][1])
                        for ci_, (s_, order) in enumerate(chs):
                            init_state(ci_, d_)
                        for step in range(nst):
                            for ci_, (s_, order) in enumerate(chs):
                                c = order[step]
                                cs = slice(c * 128, (c + 1) * 128)
                                slot = ci_ % 2
                                pv = pC[:, slot * 512:(slot + 1) * 512]
                                pvn = "pC%d" % slot
                                Sfn, Sbn, vnn = R("Sf%d" % ci_), R("Sb%d" % ci_), R("vnb%d" % ci_)
                                P.pe(lambda e, cs=cs, pv=pv: e.matmul(pv[:, 0:128], lhsT=Ttb[:, cs], rhs=vbt[:, cs], start=True, stop=False),
                                     reads=[R("Ttb"), R("vbt")], writes=[pvn])
                                P.pe(lambda e, cs=cs, pv=pv, ci_=ci_: e.matmul(pv[:, 0:128], lhsT=nwT[:, cs], rhs=Sb[ci_][:], start=False, stop=True),
                                     reads=[R("nwT"), Sbn], writes=[pvn])
                                P.act(lambda e, pv=pv, ci_=ci_: e.copy(out=vnb[ci_][:], in_=pv[:, 0:128]), reads=[pvn], writes=[vnn])
                                P.pe(lambda e, cs=cs, pv=pv, ci_=ci_: e.matmul(pv[:, 128:256], lhsT=qdT[:, cs], rhs=Sb[ci_][:], start=True, stop=False),
                                     reads=[R("qdT"), Sbn], writes=[pvn])
                                P.pe(lambda e, cs=cs, pv=pv, ci_=ci_: e.matmul(pv[:, 128:256], lhsT=attT[:, cs], rhs=vnb[ci_][:], start=False, stop=True),
                                     reads=[R("attT"), vnn], writes=[pvn])
                                P.pe(lambda e, cs=cs, pv=pv, ci_=ci_: e.matmul(pv[:, 256:384], lhsT=kdt[:, cs], rhs=vnb[ci_][:], start=True, stop=True),
                                     reads=[R("kdt"), vnn], writes=[pvn])
                                P.dve(lambda e, cs=cs, pv=pv: e.tensor_tensor(out=osum[:, cs], in0=osum[:, cs], in1=pv[:, 128:256], op=ALU.add),
                                      reads=["osum", pvn], writes=["osum"])
                                P.dve(lambda e, pv=pv, ci_=ci_, c=c: e.scalar_tensor_tensor(
                                    out=Sf[ci_][:], in0=Sf[ci_][:], scalar=egt3[:, c, col:col + 1], in1=pv[:, 256:384], op0=ALU.mult, op1=ALU.add),
                                    reads=[Sfn, "egt", pvn], writes=[Sfn])
                                P.act(lambda e, ci_=ci_: e.copy(out=Sb[ci_][:], in_=Sf[ci_][:]), reads=[Sfn], writes=[Sbn])
                                yield

                    if stop == 'H1':
                        raise StopBuild()

                    pG, pGn = (pA, "pA") if d == 0 else (pB, "pB")
                    col = d * 8 + h
                    gcol = gt3[:, :, col]
                    P.dve(lambda e: e.tensor_tensor(out=v3(gtri, 8), in0=bcm(TRI[d][:], 8), in1=bc1(gcol, 128), op=ALU.mult),
                          reads=[TRIN[d], "gt"], writes=[R("X1")])
                    yield
                    for hf in range(2):
                        P.pe(lambda e, hf=hf: e.matmul(pG[:, hf * 512:(hf + 1) * 512], lhsT=onesf[:], rhs=gtri[:, hf * 512:(hf + 1) * 512],
                                                       start=True, stop=True), reads=["onesf", R("X1")], writes=[pGn])
                    P.dve(lambda e: e.tensor_tensor(out=v3(dd, 8), in0=v3(pG[:], 8), in1=bc1(gc3[:, :, col], 128), op=ALU.subtract),
                          reads=[pGn, "gc"], writes=[R("X2")])
                    P.act(lambda e: e.activation(out=ebc, in_=pG[:], func=AF.Exp), reads=[pGn], writes=[R("X1")])
                    P.dve(lambda e: e.scalar_tensor_tensor(out=dd, in0=dd, scalar=-1.0, in1=dd, op0=ALU.mult, op1=ALU.min), reads=[R("X2")], writes=[R("X2")])
                    P.act(lambda e: e.activation(out=esym, in_=dd, func=AF.Exp), reads=[R("X2")], writes=[R("X3")])
                    P.dve(lambda e: e.tensor_tensor(out=v3(eL, 8), in0=v3(esym, 8), in1=bcm(STR[d][:], 8), op=ALU.mult),
                           reads=[R("X3"), STRN[d]], writes=[R("X4")])
                    P.dve(lambda e: e.tensor_tensor(out=v3(eL, 8), in0=v3(eL, 8), in1=bc1(bt3[:, :, col], 128), op=ALU.mult),
                          reads=[R("X4"), "bt"], writes=[R("X4")])
                    P.dve(lambda e: e.tensor_tensor(out=qdT[:], in0=fmT[0][:], in1=ebc, op=ALU.mult),
                          reads=["fmT0", R("X1")], writes=[R("qdT")])
                    yield
                    for c in range(8):
                        cs = slice(c * 128, (c + 1) * 128)
                        P.pe(lambda e, cs=cs: e.matmul(pG[:, cs], lhsT=fmT[1][:, cs], rhs=fmT[1][:, cs], start=True, stop=True),
                             reads=["fmT1"], writes=[pGn])
                    P.dve(lambda e: e.tensor_tensor(out=Lb[0][:], in0=pG[:], in1=eL, op=ALU.mult), reads=[pGn, R("X4")], writes=[R("Lb0")])
                    P.pool(lambda e: e.tensor_tensor(out=v3(eA, 8), in0=v3(esym, 8), in1=bcm(TRI[d][:], 8), op=ALU.mult),
                           reads=[R("X3"), TRIN[d]], writes=[R("X4")])
                    yield
                    for c in range(8):
                        cs = slice(c * 128, (c + 1) * 128)
                        P.pe(lambda e, cs=cs: e.matmul(pG[:, cs], lhsT=fmT[1][:, cs], rhs=fmT[0][:, cs], start=True, stop=True),
                             reads=["fmT1", "fmT0"], writes=[pGn])
                    P.dve(lambda e: e.tensor_tensor(out=attT[:], in0=pG[:], in1=eA, op=ALU.mult),
                          reads=[pGn, R("X4")], writes=[R("attT")])
                    yield
                    for c in range(8):
                        cs = slice(c * 128, (c + 1) * 128)
                        P.pe(lambda e, cs=cs: e.transpose(pT[:, cs], Lb[0][:, cs], identb[:]), reads=[R("Lb0"), "identb"], writes=["pT"])
                    P.act(lambda e: e.copy(out=Nfull[:], in_=pT[:]), reads=["pT"], writes=[R("Nfull")])
                    P.dve(lambda e: e.tensor_tensor(out=v3(Lb[1][:], 8), in0=v3(Lb[0][:], 8), in1=bcm(mk[0], 8), op=ALU.mult),
                           reads=[R("Lb0"), "mkt"], writes=[R("Lb1")])
                    P.pool(lambda e: e.tensor_tensor(out=v3(Nb[1][:], 8), in0=v3(Nfull[:], 8), in1=bcm(mk[0], 8), op=ALU.mult),
                           reads=[R("Nfull"), "mkt"], writes=[R("Nb1")])
                    P.dve(lambda e: e.tensor_tensor(out=v3(Ttf, 8), in0=bcm(identf[:], 8), in1=v3(Nb[1][:], 8), op=ALU.subtract),
                          reads=["identf", R("Nb1")], writes=[R("X2")])
                    P.act(lambda e: e.copy(out=Ttb[:], in_=Ttf), reads=[R("X2")], writes=[R("Ttb")])
                    yield
                    cur = 1
                    for k in range(1, 3):
                        nxt = 1 - cur
                        for c in range(8):
                            cs = slice(c * 128, (c + 1) * 128)
                            P.pe(lambda e, cs=cs, cur=cur: e.matmul(pG[:, cs], lhsT=Nb[cur][:, cs], rhs=Lb[cur][:, cs], start=True, stop=True),
                                 reads=[R("Nb%d" % cur), R("Lb%d" % cur)], writes=[pGn])
                        P.act(lambda e, nxt=nxt: e.copy(out=Lb[nxt][:], in_=pG[:]), reads=[pGn], writes=[R("Lb%d" % nxt)])
                        yield
                        if k < 2:
                            for c in range(8):
                                cs = slice(c * 128, (c + 1) * 128)
                                P.pe(lambda e, cs=cs, cur=cur: e.matmul(pG[:, cs], lhsT=Lb[cur][:, cs], rhs=Nb[cur][:, cs], start=True, stop=True),
                                     reads=[R("Nb%d" % cur), R("Lb%d" % cur)], writes=[pGn])
                            P.dve(lambda e, nxt=nxt: e.tensor_copy(out=Nb[nxt][:], in_=pG[:]), reads=[pGn], writes=[R("Nb%d" % nxt)])
                            yield
                        for c in range(8):
                            cs = slice(c * 128, (c + 1) * 128)
                            P.pe(lambda e, cs=cs, nxt=nxt: e.matmul(pG[:, cs], lhsT=Lb[nxt][:, cs], rhs=Ttb[:, cs], start=True, stop=True),
                                 reads=[R("Lb%d" % nxt), R("Ttb")], writes=[pGn])
                        P.dve(lambda e: e.tensor_tensor(out=Ttb[:], in0=Ttf, in1=pG[:], op=ALU.add), reads=[R("X2"), pGn], writes=[R("Ttb")])
                        P.dve(lambda e: e.tensor_tensor(out=Ttf, in0=Ttf, in1=pG[:], op=ALU.add), reads=[R("X2"), pGn], writes=[R("X2")])
                        yield
                        cur = nxt
                    Tf, Tb, Ub, NOb = Ttf, Lb[1], Nb[0], Nb[1]
                    for c in range(8):
                        cs = slice(c * 128, (c + 1) * 128)
                        P.pe(lambda e, cs=cs: e.transpose(pT[:, cs], Ttb[:, cs], identb[:]), reads=[R("Ttb"), "identb"], writes=["pT"])
                    P.act(lambda e: e.copy(out=Tb[:], in_=pT[:]), reads=["pT"], writes=[R("Lb1")])
                    P.dve(lambda e: e.tensor_copy(out=Tf, in_=Tb[:]), reads=[R("Lb1")], writes=[R("X2")])
                    yield
                    NOs = [(Nb[1], R("Nb1")), (Lb[0], R("Lb0"))]

                    def mkmask(lvl):
                        nb_, nn_ = NOs[lvl % 2]
                        P.pool(lambda e: e.tensor_tensor(out=v3(nb_[:], 8), in0=v3(Nfull[:], 8), in1=bcm(mk[lvl], 8), op=ALU.mult),
                               reads=[R("Nfull"), "mkt"], writes=[nn_])
                    mkmask(1)
                    for lvl in range(1, 5):
                        NOb, NOn = NOs[lvl % 2]
                        for c in range(8):
                            cs = slice(c * 128, (c + 1) * 128)
                            P.pe(lambda e, cs=cs, NOb=NOb: e.matmul(pG[:, cs], lhsT=NOb[:, cs], rhs=Tb[:, cs], start=True, stop=True),
                                 reads=[NOn, R("Lb1")], writes=[pGn])
                        P.act(lambda e: e.copy(out=Ub[:], in_=pG[:]), reads=[pGn], writes=[R("Nb0")])
                        if lvl < 4:
                            mkmask(lvl + 1)
                        yield
                        for c in range(8):
                            cs = slice(c * 128, (c + 1) * 128)
                            P.pe(lambda e, cs=cs: e.matmul(pG[:, cs], lhsT=Ttb[:, cs], rhs=Ub[:, cs], start=True, stop=True),
                                 reads=[R("Ttb"), R("Nb0")], writes=[pGn])
                        P.dve(lambda e: e.tensor_tensor(out=Tb[:], in0=Tf, in1=pG[:], op=ALU.subtract), reads=[R("X2"), pGn], writes=[R("Lb1")])
                        if lvl < 4:
                            P.dve(lambda e: e.tensor_tensor(out=Tf, in0=Tf, in1=pG[:], op=ALU.subtract), reads=[R("X2"), pGn], writes=[R("X2")])
                        yield
                        for c in range(8):
                            cs = slice(c * 128, (c + 1) * 128)
                            P.pe(lambda e, cs=cs: e.transpose(pT[:, cs], Tb[:, cs], identb[:]), reads=[R("Lb1"), "identb"], writes=["pT"])
                        P.act(lambda e: e.copy(out=Ttb[:], in_=pT[:]), reads=["pT"], writes=[R("Ttb")])
                        yield
                    P.dve(lambda e: e.tensor_tensor(out=v3(vbt[:], 8), in0=v3(vtm[:], 8), in1=bc1(bt3[:, :, col], 128), op=ALU.mult),
                          reads=["vtm", "bt"], writes=[R("vbt")])
                    P.dve(lambda e: e.tensor_tensor(out=v3(kbg[:], 8), in0=v3(ktm[:], 8), in1=bc1(bgc3[:, :, col], 128), op=ALU.mult),
                          reads=["ktm", "bgc"], writes=[R("Lb0")])
                    P.pool(lambda e: e.tensor_tensor(out=v3(kdt[:], 8), in0=v3(ktm[:], 8), in1=bc1(ekd3[:, :, col], 128), op=ALU.mult),
                           reads=["ktm", "ekd"], writes=[R("kdt")])
                    yield
                    for c in range(8):
                        cs = slice(c * 128, (c + 1) * 128)
                        P.pe(lambda e, cs=cs: e.matmul(pG[:, cs], lhsT=kbg[:, cs], rhs=Ttb[:, cs], start=True, stop=True),
                             reads=[R("Lb0"), R("Ttb")], writes=[pGn])
                    P.act(lambda e: e.mul(out=nwT[:], in_=pG[:], mul=-1.0), reads=[pGn], writes=[R("nwT")])
                    yield

                    if isS:
                        if d == 1:
                            yield "WAIT0"
                            P.dma(lambda e: e.dma_start(out=cc_in[h], in_=BD[0].Sf[0][:]), reads=["Sf0_0"], writes=["cc_in%d" % h], q="pool")

                            def ccfn(e):
                                e.collective_compute("AllGather", ALU.bypass, replica_groups=GROUPS,
                                                     ins=[cc_in[h]], outs=[cc_out[h]]).then_inc(s_cc, 1)
                                e.wait_ge(s_cc, h + 1)
                            P.op("pool", ccfn, reads=["cc_in%d" % h], writes=["cc_out%d" % h], selfsig=True)
                        yield from run_chains([(0, list(range(8)) if d == 0 else list(range(7, -1, -1)))], d)
                    else:
                        chs = [(s_, ([2 * s_, 2 * s_ + 1] if d == 0 else [2 * s_ + 1, 2 * s_])) for s_ in range(4)]
                        yield from run_chains(chs, d)
                        for ci_, (s_, order) in enumerate(chs):
                            P.dma(lambda e, ci_=ci_, s_=s_, d=d: e.dma_start(out=st[s_, d, h], in_=Sf[ci_][:]), reads=[R("Sf%d" % ci_)])

                    if stop == 'X%d_%d' % (h, d):
                        raise StopBuild()


                gens = [gen_dir(0), gen_dir(1)]
                done = [False, False]

                def step(i_):
                    if done[i_]:
                        return None
                    try:
                        return next(gens[i_])
                    except StopIteration:
                        done[i_] = True
                        return None
                while not (done[0] and done[1]):
                    step(0)
                    r_ = step(1)
                    if r_ == "WAIT0":
                        while not done[0]:
                            step(0)

            def d4_gen(h):
                for tt in range(8):
                    P.act(lambda e, tt=tt: e.activation(out=ogt[:, tt * 128:(tt + 1) * 128], in_=osum[:, tt * 128:(tt + 1) * 128],
                                                        func=AF.Square, accum_out=stat[:, 48 + tt:49 + tt]),
                          reads=["osum"], writes=["ogt", "stat"])
                P.act(lambda e: e.activation(out=stat[:, 48:56], in_=stat[:, 48:56], func=AF.Ln, scale=1.0 / 128, bias=epsb[:, 0:1]),
                      reads=["stat", "epsb"], writes=["stat"])
                P.act(lambda e: e.activation(out=stat[:, 48:56], in_=stat[:, 48:56], func=AF.Exp, scale=-0.5), reads=["stat"], writes=["stat"])
                yield
                P.dve(lambda e: e.tensor_tensor(out=v3(osum[:], 8), in0=v3(osum[:], 8), in1=bc1(stat[:, 48:56], 128), op=ALU.mult),
                      reads=["osum", "stat"], writes=["osum"])
                P.dve(lambda e: e.tensor_tensor(out=v3(osum[:], 8), in0=v3(osum[:], 8), in1=bcm(normo_bc[:], 8), op=ALU.mult),
                      reads=["osum", "normo_bc"], writes=["osum"])
                yield
                P.dve(lambda e: e.tensor_tensor(out=ogt[:], in0=osum[:], in1=szt[:], op=ALU.mult), reads=["osum", "szt"], writes=["ogt"])
                yield
                for tt in range(8):
                    P.pe(lambda e, tt=tt: e.transpose(pT[:, tt * 128:(tt + 1) * 128], ogt[:, tt * 128:(tt + 1) * 128], identb[:]),
                         reads=["ogt", "identb"], writes=["pT"])
                P.act(lambda e: e.copy(out=ogT3[:, h, :], in_=pT[:]), reads=["pT"], writes=["ogT"])
                yield

            def run_gens(gs):
                alive = [True] * len(gs)
                while any(alive):
                    for i_ in range(len(gs)):
                        if alive[i_]:
                            try:
                                next(gs[i_])
                            except StopIteration:
                                alive[i_] = False

            heads = list(HEADS if HEADS else range(8))
            wl = {heads[0]: load_w(wkey("w_in_h", 0, 1024, heads[0] * 512, 512), 8, 512)}
            run_gens([d1_gen(heads[0], *wl[heads[0]])])
            for hi, h in enumerate(heads):
                nh = heads[hi + 1] if hi + 1 < len(heads) else None
                if nh is not None:
                    wl[nh] = load_w(wkey("w_in_h", 0, 1024, nh * 512, 512), 8, 512)
                d23(h)
                if SEQ_D4:
                    run_gens([d4_gen(h)])
                    if nh is not None:
                        run_gens([d1_gen(nh, *wl[nh])])
                else:
                    run_gens([d4_gen(h)] + ([d1_gen(nh, *wl[nh])] if nh is not None else []))
                if stop == 'D%d' % h:
                    raise StopBuild()
        P.barrier()

        if stop == 'D':
            raise StopBuild()
        if True:
            AN.top = T0
            sgtE = AN.alloc(1024)
            tmpeE = AN.alloc(1024)
            actb = AN.at(mB_off, 22 * 1024, BF16)
            act3 = actb[:].rearrange("p (k n) -> p k n", k=22)
            phtF = {"junk": AN.alloc(1024, BF16),
                   "xn": [AN.alloc(1024, BF16) for i in range(2)],
                   "tmpf": AN.alloc(1024)}
            for tt in range(8):
                P.dma(lambda e, tt=tt: e.dma_start(out=xres3[:, tt, :], in_=xin[tt * 128:(tt + 1) * 128, :]), writes=["xres"])
            for cn in range(2):
                wv, wn = load_w(wkey("w_a_out", 0, 1024, cn * 512, 512), 8, 512)
                gv, gn = load_w(wkey("w_gate", 0, 1024, cn * 512, 512), 8, 512)
                for jj in range(4):
                    j = cn * 4 + jj
                    for hf in range(2):
                        ui = j * 2 + hf
                        pX, pXn = [(pA, ["pA"]), (pB, ["pB"]), (pC, ["pC0", "pC1"])][ui % 3]
                        sg, sgn = [(sgtE[:, 0:512], "sgE0"), (sgtE[:, 512:1024], "sgE1"), (phtF["tmpf"][:, 0:512], "tmpf")][ui % 3]
                        hs_ = slice(hf * 512, (hf + 1) * 512)
                        for kc in range(8):
                            P.pe(lambda e, kc=kc, jj=jj, hs_=hs_, wv=wv, pX=pX: e.matmul(
                                pX[:, 0:512], lhsT=wv[:, kc, jj * 128:(jj + 1) * 128],
                                rhs=ogT3[:, kc, hs_], start=(kc == 0), stop=(kc == 7)),
                                reads=[wn, "ogT"], writes=pXn)
                        for dc in range(8):
                            P.pe(lambda e, dc=dc, jj=jj, hs_=hs_, gv=gv, pX=pX: e.matmul(
                                pX[:, 512:1024], lhsT=gv[:, dc, jj * 128:(jj + 1) * 128],
                                rhs=hT3[:, dc, hs_], start=(dc == 0), stop=(dc == 7)),
                                reads=[gn, "hT"], writes=pXn)
                        P.act(lambda e, pX=pX, sg=sg: e.activation(out=sg, in_=pX[:, 512:1024], func=AF.Sigmoid), reads=pXn, writes=[sgn])
                        P.dve(lambda e, pX=pX, sg=sg, hs_=hs_: e.tensor_tensor(out=tmpeE[:, hs_], in0=pX[:, 0:512], in1=sg, op=ALU.mult),
                              reads=pXn + [sgn], writes=["tmpeE"])
                        P.dve(lambda e, j=j, hs_=hs_: e.tensor_tensor(out=mB3[:, j, hs_], in0=tmpeE[:, hs_], in1=mB3[:, j, hs_], op=ALU.add),
                              reads=["tmpeE", "mB"], writes=["mB"])
            if stop == 'E':
                raise StopBuild()
            wo = [load_w(wkey("w_o", 0, 1024, cn * 512, 512), 8, 512) for cn in range(2)]
            for tt in range(8):
                for cn in range(2):
                    pv, pvn = (pA, "pA") if cn == 0 else (pB, "pB")
                    for dc in range(8):
                        P.pe(lambda e, dc=dc, tt=tt, cn=cn, pv=pv: e.matmul(
                            pv[:, 0:512], lhsT=mB3[:, dc, tt * 128:(tt + 1) * 128], rhs=wo[cn][0][:, dc, :],
                            start=(dc == 0), stop=(dc == 7)), reads=["mB", wo[cn][1]], writes=[pvn])
                    P.dve(lambda e, cn=cn, pv=pv: e.tensor_tensor(out=tmpeE[:, cn * 512:(cn + 1) * 512], in0=pv[:, 0:512],
                                                                  in1=g1bc[:, cn * 512:(cn + 1) * 512], op=ALU.mult),
                          reads=[pvn, "gbc"], writes=["tmpeE"])
                P.dve(lambda e, tt=tt: e.tensor_tensor(out=xres3[:, tt, :], in0=xres3[:, tt, :], in1=tmpeE[:], op=ALU.add),
                       reads=["xres", "tmpeE"], writes=["xres"])
                norm_transpose(xres3[:, tt, :], "xres", hT3[:, :, tt * 128:(tt + 1) * 128], "hT", tt,
                               sc2t[:, job * 8:(job + 1) * 8], 24, phtF, "f", part=1)
                if tt >= 1:
                    norm_transpose(xres3[:, tt - 1, :], "xres", hT3[:, :, (tt - 1) * 128:tt * 128], "hT", tt - 1,
                                   sc2t[:, job * 8:(job + 1) * 8], 24, phtF, "f", part=2)
            norm_transpose(xres3[:, 7, :], "xres", hT3[:, :, 7 * 128:8 * 128], "hT", 7,
                           sc2t[:, job * 8:(job + 1) * 8], 24, phtF, "f", part=2)
            if stop == 'F':
                raise StopBuild()
            P.barrier()
            AN.top = mB_off + 11 * 1024
            sgtG = AN.alloc(1024)
            tmpeG = AN.alloc(1024)
            phtG = {"junk": AN.alloc(1024, BF16)}
            yt = [AN.alloc(1024) for i in range(2)]
            normf_bc = AN.alloc(1024)
            stg = [AN.alloc(8 * 512) for _ in range(2)]
            stgn = [0]

            def load_w_hw(key, kc, ncols):
                if key not in WT:
                    WT[key] = len(WT)
                    assert len(WT) <= NWT
                src3 = wpack[WT[key]][:, 0:kc * ncols].rearrange("p (k n) -> p k n", k=kc)
                k_ = stgn[0] % 2
                stgn[0] += 1
                sv = stg[k_][:, 0:kc * ncols].rearrange("p (k n) -> p k n", k=kc)
                P.dma(lambda e: e.dma_start(out=sv, in_=src3), writes=["stg%d" % k_])
                i = wstate["n"] % NWB
                wstate["n"] += 1
                nm = "wb%d" % i
                view = wbuf[i][:, 0:kc * ncols].rearrange("p (k n) -> p k n", k=kc)
                P.dve(lambda e: e.tensor_copy(out=view, in_=sv), reads=["stg%d" % k_], writes=[nm])
                return view, nm
            P.dma(lambda e: e.dma_start(out=normf_bc[:], in_=norm_f.partition_broadcast(128)), writes=["normf_bc"])
            for j in range(22):
                if j % 4 == 0:
                    ncol = min(512, FF - j * 128)
                    wg = load_w(wkey("w_gu", 0, 1024, j * 128, ncol), 8, ncol)
                    wu = load_w_hw(wkey("w_gu", 0, 1024, FF + j * 128, ncol), 8, ncol)
                jj = j % 4
                for hf in range(2):
                    ui = j * 2 + hf
                    pX, pXn = [(pA, ["pA"]), (pB, ["pB"]), (pC, ["pC0", "pC1"])][ui % 3]
                    sg, sgn = [(sgtG[:, 0:512], "sgS0"), (sgtG[:, 512:1024], "sgS1"), (yt[0][:, 0:512], "yt0")][ui % 3]
                    for dc in range(8):
                        P.pe(lambda e, dc=dc, jj=jj, hf=hf, wg=wg, pX=pX: e.matmul(
                            pX[:, 0:512], lhsT=wg[0][:, dc, jj * 128:(jj + 1) * 128],
                            rhs=hT3[:, dc, hf * 512:(hf + 1) * 512], start=(dc == 0), stop=(dc == 7)),
                            reads=[wg[1], "hT"], writes=pXn)
                    for dc in range(8):
                        P.pe(lambda e, dc=dc, jj=jj, hf=hf, wu=wu, pX=pX: e.matmul(
                            pX[:, 512:1024], lhsT=wu[0][:, dc, jj * 128:(jj + 1) * 128],
                            rhs=hT3[:, dc, hf * 512:(hf + 1) * 512], start=(dc == 0), stop=(dc == 7)),
                            reads=[wu[1], "hT"], writes=pXn)
                    P.act(lambda e, pX=pX, sg=sg: e.activation(out=sg, in_=pX[:, 0:512], func=AF.Silu), reads=pXn, writes=[sgn])
                    P.dve(lambda e, j=j, hf=hf, pX=pX, sg=sg: e.tensor_tensor(out=act3[:, j, hf * 512:(hf + 1) * 512], in0=pX[:, 512:1024], in1=sg, op=ALU.mult),
                          reads=pXn + [sgn], writes=["actb"])
            def final_tile(tt):
                col = stat[:, tt * 3:tt * 3 + 3]
                y = yt[tt % 2]
                yn = "yt%d" % (tt % 2)
                P.act(lambda e, tt=tt, col=col: e.activation(out=phtG["junk"][:], in_=xres3[:, tt, :], func=AF.Square, accum_out=col[:, 0:1]),
                      reads=["xres"], writes=["junk", "stat"])
                P.act(lambda e, col=col: e.activation(out=col[:, 1:2], in_=col[:, 0:1], func=AF.Ln, scale=1.0 / 1024, bias=epsb[:, 0:1]),
                      reads=["stat", "epsb"], writes=["stat"])
                P.act(lambda e, col=col: e.activation(out=col[:, 2:3], in_=col[:, 1:2], func=AF.Exp, scale=-0.5), reads=["stat"], writes=["stat"])
                P.dve(lambda e, tt=tt, col=col, y=y: e.scalar_tensor_tensor(out=y[:], in0=xres3[:, tt, :], scalar=col[:, 2:3], in1=normf_bc[:],
                                                                            op0=ALU.mult, op1=ALU.mult),
                      reads=["xres", "stat", "normf_bc"], writes=[yn])
                P.dma(lambda e, tt=tt, y=y: e.dma_start(out=yout[tt * 128:(tt + 1) * 128, :], in_=y[:]), reads=[yn])

            for cn in range(2):
                wd = [load_w(wkey("w_down", k0 * 128, nk * 128, cn * 512, 512), nk, 512)
                      for (k0, nk) in ((0, 8), (8, 8), (16, 6))]
                for tt in range(8):
                    pv, pvn = (pA, "pA") if tt % 2 == 0 else (pB, "pB")
                    for j in range(22):
                        wdv, wdn = wd[j // 8]
                        P.pe(lambda e, j=j, tt=tt, wdv=wdv, pv=pv: e.matmul(
                            pv[:, 0:512], lhsT=act3[:, j, tt * 128:(tt + 1) * 128], rhs=wdv[:, j % 8, :],
                            start=(j == 0), stop=(j == 21)), reads=["actb", wdn], writes=[pvn])
                    P.dve(lambda e, cn=cn, pv=pv: e.tensor_tensor(out=tmpeG[:, 0:512], in0=pv[:, 0:512],
                                                                  in1=g2bc[:, cn * 512:(cn + 1) * 512], op=ALU.mult),
                          reads=[pvn, "gbc"], writes=["tmpeG"])
                    P.dve(lambda e, tt=tt, cn=cn: e.tensor_tensor(out=xres3[:, tt, cn * 512:(cn + 1) * 512],
                                                                   in0=xres3[:, tt, cn * 512:(cn + 1) * 512], in1=tmpeG[:, 0:512], op=ALU.add),
                           reads=["xres", "tmpeG"], writes=["xres"])
                    if cn == 1:
                        final_tile(tt)
        P.barrier()

    try:
        if stop in ('ada', 'const', 'ada1', 'ada2'):
            raise StopBuild()
        run_job(0)
        if stop == 'P':
            raise StopBuild()
        if enable_S:
            run_job(1)
    except StopBuild:
        pass

    WT_KEYS[:] = sorted(WT, key=WT.get)
    run = P.emit(sems, dma_sems)
    if stats:
        for e_ in ENGINES:
            print(e_, 'ops', len(P.ops[e_]), 'signals', sum(1 for r_ in P.ops[e_] if r_['signal']), 'dmas', P.ndma[e_])
    with nc.Block() as block:
        @block.sync
        def _(eng):
            run("sp", eng)

        @block.tensor
        def _(eng):
            run("pe", eng)

        @block.scalar
        def _(eng):
            run("act", eng)

        @block.vector
        def _(eng):
            run("dve", eng)

        @block.gpsimd
        def _(eng):
            run("pool", eng)
    es.close()
    return nc


_NC_CACHE = {}


def _prep_inputs(inp):
    f = lambda a: np.ascontiguousarray(np.asarray(a, dtype=np.float32))
    x_prompt, x_sample = f(inp["x_prompt"]), f(inp["x_sample"])
    state_delta, c, c_ctx = f(inp["state_delta"]), f(inp["c"]), f(inp["c_ctx"])
    w_in = f(inp["w_in"])[0]
    cols = []
    for h in range(8):
        for base in (0, 1024, 2048, 3072):
            cols.append(np.arange(base + h * 128, base + (h + 1) * 128))
    cols = np.concatenate(cols)
    w_in_h = np.ascontiguousarray(w_in[:, cols])
    w_ab_n = w_in[:, 4096:4128]
    w_ab_sw = w_ab_n.reshape(1024, 2, 2, 8)[:, :, ::-1, :].reshape(1024, 32)
    w_glu = np.ascontiguousarray(w_in[:, 4128:5152])
    w_gate = np.ascontiguousarray(w_in[:, 5152:7200])
    conv_qkv = f(inp["conv_qkv"])[0]
    ccols = []
    for h in range(8):
        for base in (0, 1024, 2048):
            ccols.append(base + h * 128)

    def mk_cw(cq):
        out = np.zeros((128, 24, 5), np.float32)
        for i, c0 in enumerate(ccols):
            out[:, i, :] = cq[:, c0:c0 + 128].T
        return out.reshape(128, 120)
    cw_n, cw_f = mk_cw(conv_qkv), mk_cw(conv_qkv[::-1])
    a_log, dt_bias = f(inp["a_log"])[0], f(inp["dt_bias"])[0]
    gp_n = np.stack([a_log.reshape(16), dt_bias.reshape(16)])
    gp_s = np.stack([a_log[::-1].reshape(16), dt_bias[::-1].reshape(16)])
    conv_dw = f(inp["conv_dw"])[0]

    def mk_cdw(cd):
        return np.ascontiguousarray(cd.T.reshape(4, 128, 31).transpose(1, 0, 2)).reshape(128, 124)
    cdw_n, cdw_f = mk_cdw(conv_dw), mk_cdw(conv_dw[::-1])
    fm = lambda v, k: np.ascontiguousarray(f(v).reshape(k, 128).T)
    cpar = np.concatenate([fm(inp["b_dw"][0], 4), fm(inp["ln_g"][0], 4), fm(inp["ln_b"][0], 4)], axis=1)
    npar = np.concatenate([fm(inp["norm1"][0], 8), fm(inp["norm2"][0], 8)], axis=1)
    common = {
        "w_ada": f(inp["w_ada"])[0], "b_ada_fm": fm(inp["b_ada"][0], 48),
        "cpar": np.ascontiguousarray(cpar), "npar": np.ascontiguousarray(npar),
        "norm_o": f(inp["norm_o"]).reshape(1, 128), "norm_f": f(inp["norm_f"]).reshape(1, 1024),
    }
    if not WT_KEYS:
        _NC_CACHE["nc"] = build_program(True)
    Wd = {"w_in_h": w_in_h, "w_glu": w_glu, "w_gate": w_gate, "w_a_out": f(inp["w_a_out"])[0],
          "w_b_out": f(inp["w_b_out"])[0], "w_o": f(inp["w_o"])[0], "w_gu": f(inp["w_gu"])[0], "w_down": f(inp["w_down"])[0]}
    wpack = np.zeros((NWT, 128, 4096), np.float32)
    for ti, (wn_, r0, nr, c0, ncw) in enumerate(WT_KEYS):
        kc_ = nr // 128
        wpack[ti, :, :kc_ * ncw] = Wd[wn_][r0:r0 + nr, c0:c0 + ncw].reshape(kc_, 128, ncw).transpose(1, 0, 2).reshape(128, kc_ * ncw)
    common["wpack"] = wpack
    ii = np.arange(128)
    mlist = [(ii[:, None] // 8 == ii[None, :] // 8)]
    for s_ in (8, 16, 32, 64):
        mlist.append((ii[:, None] // (2 * s_) == ii[None, :] // (2 * s_)) & (ii[:, None] // s_ != ii[None, :] // s_))
    common["masks"] = np.ascontiguousarray(np.stack(mlist).astype(np.float32))
    in_maps = []
    for core in range(8):
        b, r = core // 2, core % 2
        m = dict(common)
        xsb = x_sample[b]
        m["xs"] = np.ascontiguousarray(xsb if r == 0 else xsb[::-1])
        m["xp"] = np.ascontiguousarray(x_prompt[4 * core:4 * core + 4].reshape(1024, 1024))
        cv = np.stack([c_ctx, c[b]], axis=0)
        m["cT"] = np.ascontiguousarray(cv.reshape(2, 8, 128).transpose(2, 1, 0)).reshape(128, 16)
        m["w_ab"] = np.ascontiguousarray(np.stack([w_ab_n, w_ab_n if r == 0 else w_ab_sw]))
        m["cw"] = np.ascontiguousarray(np.stack([cw_n, cw_n if r == 0 else cw_f]))
        m["gpar"] = np.ascontiguousarray(np.stack([gp_n, gp_n if r == 0 else gp_s]))
        m["cdw"] = np.ascontiguousarray(np.stack([cdw_n, cdw_n if r == 0 else cdw_f]))
        m["s0"] = np.ascontiguousarray(state_delta[b, 0, r])
        sel = np.zeros((128, 2), np.float32)
        sel[:, 1 - r] = 1.0
        m["sel"] = sel
        in_maps.append(m)
    return in_maps


def kernel(**inputs):
    in_maps = _prep_inputs(inputs)
    if "nc" not in _NC_CACHE:
        _NC_CACHE["nc"] = build_program(True)
    res = run_bass_kernel_spmd(_NC_CACHE["nc"], in_maps, core_ids=list(range(8)))
    y_prompt = np.zeros((32, 256, 1024), np.float32)
    y_sample = np.zeros((4, 2048, 1024), np.float32)
    new_state = np.zeros((32, 1, 2, 8, 128, 128), np.float32)
    for core in range(8):
        r_ = res.results[core]
        b, r = core // 2, core % 2
        y_prompt[4 * core:4 * core + 4] = r_["yp"].reshape(4, 256, 1024)
        if r == 0:
            y_sample[b, 0:1024] = r_["ys"]
        else:
            y_sample[b, 1024:2048] = r_["ys"][::-1]
        new_state[4 * core:4 * core + 4, 0] = r_["st"]
    return (y_prompt, y_sample, new_state)
```

```python
import numpy as np
from contextlib import ExitStack
import concourse.bass as bass
import concourse.mybir as mybir
from concourse.bass_utils import run_bass_kernel_spmd

F32 = mybir.dt.float32
BF16 = mybir.dt.bfloat16
ALU = mybir.AluOpType
AF = mybir.ActivationFunctionType
AX = mybir.AxisListType

ENGINES = ("pe", "act", "dve", "pool", "sp")
NDMA_SEMS = 8
EPS = 1e-6
FF = 2816
GROUPS = [[0, 1], [2, 3], [4, 5], [6, 7]]
NWT = 40
WT_KEYS = []


PSUM_NAMES = ("pA", "pB", "pC", "pC0", "pC1", "pS", "pT")


class StopBuild(Exception):
    pass


class Prog:
    def __init__(self, nc):
        self.nc = nc
        self.ops = {e: [] for e in ENGINES}
        self.last_writer = {}
        self.readers = {}
        self.ndma = {e: 0 for e in ENGINES}
        self.nbar = 0

    def op(self, eng, fn, reads=(), writes=(), dma=False, drain=False, selfsig=False):
        deps = set()
        for r in reads:
            lw = self.last_writer.get(r)
            if lw is not None:
                deps.add(lw)
            if r in PSUM_NAMES:
                for rd in self.readers.get(r, ()):
                    if rd[0] != eng:
                        deps.add(rd)
        for w in writes:
            lw = self.last_writer.get(w)
            if lw is not None:
                deps.add(lw)
            for rd in self.readers.get(w, ()):
                deps.add(rd)
        idx = len(self.ops[eng])
        me = (eng, idx)
        deps.discard(me)
        best = {}
        for (pe_, pi_) in deps:
            if self.ops[pe_][pi_]["dma"]:
                continue
            if pi_ > best.get(pe_, -1):
                best[pe_] = pi_
        deps = set(d_ for d_ in deps if self.ops[d_[0]][d_[1]]["dma"] or best[d_[0]] == d_[1])
        rec = dict(fn=fn, deps=deps, dma=dma, signal=selfsig, dma_idx=None, drain=drain,
                   ndma_before=self.ndma[eng], selfsig=selfsig)
        if dma:
            rec["dma_idx"] = self.ndma[eng]
            self.ndma[eng] += 1
        self.ops[eng].append(rec)
        for w in writes:
            self.last_writer[w] = me
            self.readers[w] = []
        for r in reads:
            self.readers.setdefault(r, []).append(me)
        return me

    def pe(self, fn, reads=(), writes=()):
        return self.op("pe", fn, reads, writes)

    def act(self, fn, reads=(), writes=()):
        return self.op("act", fn, reads, writes)

    def dve(self, fn, reads=(), writes=()):
        return self.op("dve", fn, reads, writes)

    def pool(self, fn, reads=(), writes=()):
        return self.op("pool", fn, reads, writes)

    def dma(self, fn, reads=(), writes=(), q="sp"):
        return self.op(q, fn, reads, writes, dma=True)

    def barrier(self):
        self.nbar += 1
        names = []
        for e in ENGINES:
            nm = "__bar%d_%s" % (self.nbar, e)
            last = len(self.ops[e]) - 1
            me = self.op(e, None, writes=[nm], drain=True, selfsig=True)
            if last >= 0:
                self.ops[e][me[1]]["deps"].add((e, last))
                self.ops[e][me[1]]["selfdep"] = True
            names.append(nm)
        for e in ENGINES:
            self.op(e, None, reads=names, selfsig=True)
        self.last_writer = {}
        self.readers = {}

    def emit(self, sems, dma_sems):
        for e in ENGINES:
            for rec in self.ops[e]:
                for (pe_, pi_) in rec["deps"]:
                    p = self.ops[pe_][pi_]
                    if p["dma"]:
                        continue
                    if pe_ == e and e == "pe" and not rec.get("selfdep"):
                        continue
                    p["signal"] = True
        cum = {}
        for e in ENGINES:
            c = 0
            arr = []
            for rec in self.ops[e]:
                if rec["signal"] and not rec["dma"]:
                    c += 1
                arr.append(c)
            cum[e] = arr
        prog = self

        def run_engine(e, eng):
            waited = {}
            nsig = [0]

            def wait(key, sem, val):
                if val <= 0 or waited.get(key, 0) >= val:
                    return
                waited[key] = val
                eng.wait_ge(sem, val)

            def drain_dmas(n):
                for k in range(min(n, NDMA_SEMS)):
                    cnt = (n - 1 - k) // NDMA_SEMS + 1
                    wait((e, k), dma_sems[e][k], 16 * cnt)

            for rec in prog.ops[e]:
                for (pe_, pi_) in sorted(rec["deps"]):
                    p = prog.ops[pe_][pi_]
                    if p["dma"]:
                        di = p["dma_idx"]
                        s = dma_sems[pe_][di % NDMA_SEMS]
                        wait((pe_, di % NDMA_SEMS), s, 16 * (di // NDMA_SEMS + 1))
                    else:
                        if pe_ == e and e == "pe" and not rec.get("selfdep"):
                            continue
                        wait(pe_, sems[pe_], cum[pe_][pi_])
                if e == "pool" and rec["signal"] and not rec["dma"] and nsig[0] > 0:
                    wait(e, sems[e], nsig[0])
                if rec["signal"] and not rec["dma"]:
                    nsig[0] += 1
                if rec["drain"]:
                    drain_dmas(rec["ndma_before"])
                if rec["dma"]:
                    di = rec["dma_idx"]
                    s = dma_sems[e][di % NDMA_SEMS]
                    if di >= NDMA_SEMS:
                        wait((e, di % NDMA_SEMS), s, 16 * (di // NDMA_SEMS))
                    rec["fn"](eng).then_inc(s, 16)
                elif rec["selfsig"]:
                    if rec["fn"] is not None:
                        rec["fn"](eng)
                    eng.nop(nofuse=True).then_inc(sems[e], 1)
                else:
                    ins = rec["fn"](eng)
                    if rec["signal"]:
                        ins.then_inc(sems[e], 1)
            drain_dmas(prog.ndma[e])

        return run_engine


def bc1(ap, n):
    return ap.unsqueeze(2).to_broadcast([ap.shape[0], ap.shape[1], n])


def bcm(ap, n):
    return ap.unsqueeze(1).to_broadcast([ap.shape[0], n, ap.shape[1]])


def v3(ap, a):
    return ap.rearrange("p (a b) -> p a b", a=a)


HEADS = None
SEQ_D4 = False
DORDER = (0, 1)


def build_program(enable_S=True, stop=None, stats=False):
    nc = bass.Bass("TRN2", target_bir_lowering=False)

    def din(name, shape):
        return nc.dram_tensor(name, shape, F32, kind="ExternalInput").ap()

    def dout(name, shape):
        return nc.dram_tensor(name, shape, F32, kind="ExternalOutput").ap()

    xs = din("xs", [2048, 1024])
    xp = din("xp", [1024, 1024])
    cT = din("cT", [128, 16])
    w_ada = din("w_ada", [1024, 6144])
    b_ada_fm = din("b_ada_fm", [128, 48])
    wpack = din("wpack", [NWT, 128, 4096])
    w_ab = din("w_ab", [2, 1024, 32])
    cw = din("cw", [2, 128, 120])
    gpar = din("gpar", [2, 2, 16])
    cdw = din("cdw", [2, 128, 124])
    cpar = din("cpar", [128, 12])
    npar = din("npar", [128, 16])
    norm_o = din("norm_o", [1, 128])
    norm_f = din("norm_f", [1, 1024])
    s0 = din("s0", [8, 128, 128])
    sel = din("sel", [128, 2])
    masks = din("masks", [5, 128, 128])
    yp = dout("yp", [1024, 1024])
    ys = dout("ys", [1024, 1024])
    st = dout("st", [4, 2, 8, 128, 128])
    cc_in = [nc.dram_tensor("cc_in%d" % h, [128, 128], F32).ap() for h in range(8)]
    cc_out = [nc.dram_tensor("cc_out%d" % h, [256, 128], F32).ap() for h in range(8)]

    P = Prog(nc)
    es = ExitStack()

    def sb(name, shape, dt=F32, stack=es):
        return stack.enter_context(nc.sbuf_tensor(name, shape, dt))

    def psum(name, shape, dt=F32):
        return es.enter_context(nc.psum_tensor(name, shape, dt))

    sems = {e: es.enter_context(nc.semaphore("s_" + e)) for e in ("pe", "act", "dve", "pool", "sp")}
    s_cc = es.enter_context(nc.semaphore("s_cc"))
    dma_sems = {"sp": [nc.alloc_semaphore(name="dq%d" % i) for i in range(NDMA_SEMS)],
                "pool": [nc.alloc_semaphore(name="dp%d" % i) for i in range(NDMA_SEMS)]}

    AR = 159 * 256
    arena_t = sb("arena", [128, AR], F32)

    class Arena:
        def __init__(self):
            self.top = 0

        def at(self, off, nelem, dt=F32):
            nfl = nelem if dt == F32 else (nelem + 1) // 2
            assert off + nfl <= AR, (off, nfl, AR)
            ap = arena_t[:, off:off + nfl]
            return ap.bitcast(BF16) if dt == BF16 else ap

        def alloc(self, nelem, dt=F32):
            nfl = nelem if dt == F32 else (nelem + 1) // 2
            off = self.top
            self.top += nfl
            return self.at(off, nelem, dt)

    AN = Arena()

    pA = psum("pA", [128, 1024])
    pB = psum("pB", [128, 1024])
    pC = psum("pC", [128, 1024])
    pS = psum("pS", [128, 512])
    pT = psum("pT", [128, 1024], BF16)

    identb = sb("identb", [128, 128], BF16)
    identf = sb("identf", [128, 128])
    onesf = sb("onesf", [128, 128])
    onesb = sb("onesb", [128, 128], BF16)
    triU = sb("triU", [128, 128])
    triL = sb("triL", [128, 128])
    sL = sb("sL", [128, 128])
    sU = sb("sU", [128, 128])
    epsb = sb("epsb", [128, 1])
    qbias = sb("qbias", [128, 1])
    zero1 = sb("zero1", [128, 1])

    def mk_mask(t, name, op, sgn=1):
        P.pool(lambda e: e.memset(t[:], 1.0), writes=[name])
        P.pool(lambda e: e.affine_select(out=t[:], in_=t[:], pattern=[[-sgn, 128]], compare_op=op, fill=0.0,
                                         base=0, channel_multiplier=sgn), reads=[name], writes=[name])

    mk_mask(identf, "identf", ALU.is_equal)
    mk_mask(triU, "triU", ALU.is_ge, -1)
    mk_mask(triL, "triL", ALU.is_ge, 1)
    mk_mask(sL, "sL", ALU.is_gt, 1)
    mk_mask(sU, "sU", ALU.is_gt, -1)
    P.pool(lambda e: e.memset(onesf[:], 1.0), writes=["onesf"])
    P.pool(lambda e: e.memset(onesb[:], 1.0), writes=["onesb"])
    P.pool(lambda e: e.memset(epsb[:], EPS), writes=["epsb"])
    P.pool(lambda e: e.memset(qbias[:], float(-0.5 * np.log(128.0))), writes=["qbias"])
    P.pool(lambda e: e.memset(zero1[:], 0.0), writes=["zero1"])
    P.pool(lambda e: e.tensor_copy(out=identb[:], in_=identf[:]), reads=["identf"], writes=["identb"])
    TRI = [triU, triL]
    TRIN = ["triU", "triL"]
    STR = [sL, sU]
    STRN = ["sL", "sU"]

    cTt = sb("cTt", [128, 16])
    scT = sb("scT", [128, 16])
    badaf = sb("badaf", [128, 48])
    modT = sb("modT", [128, 96])
    cwt = sb("cwt", [128, 240])
    cdwt = sb("cdwt", [128, 248])
    cpart = sb("cpart", [128, 12])
    npart = sb("npart", [128, 16])
    normo_bc = sb("normo_bc", [128, 128])
    gpt = sb("gpt", [128, 64])
    nea = sb("nea", [128, 32])
    selt = sb("selt", [128, 2])
    mkt = sb("mkt", [128, 5 * 128], BF16)
    mk = [mkt[:, i * 128:(i + 1) * 128] for i in range(5)]
    sc1t = sb("sc1t", [128, 16])
    sc2t = sb("sc2t", [128, 16])
    gbc = sb("gbc", [128, 4096])

    P.dma(lambda e: e.dma_start(out=cTt[:], in_=cT), writes=["cTt"])
    P.dma(lambda e: e.dma_start(out=badaf[:], in_=b_ada_fm), writes=["badaf"])
    P.dma(lambda e: e.dma_start(out=v3(cwt[:], 2), in_=cw.rearrange("j p f -> p j f")), writes=["cwt"])
    P.dma(lambda e: e.dma_start(out=v3(cdwt[:], 2), in_=cdw.rearrange("j p f -> p j f")), writes=["cdwt"])
    P.dma(lambda e: e.dma_start(out=cpart[:], in_=cpar), writes=["cpart"])
    P.dma(lambda e: e.dma_start(out=npart[:], in_=npar), writes=["npart"])
    P.dma(lambda e: e.dma_start(out=normo_bc[:], in_=norm_o.partition_broadcast(128)), writes=["normo_bc"])
    P.dma(lambda e: e.dma_start(out=gpt[:], in_=gpar.rearrange("j a b -> (j a b)").partition_broadcast(128)),
          writes=["gpt"])
    P.dma(lambda e: e.dma_start(out=selt[:], in_=sel), writes=["selt"])
    P.dma(lambda e: e.dma_start(out=v3(mkt[:], 5), in_=masks.rearrange("m p f -> p m f")), writes=["mkt"], q="pool")
    for j in range(2):
        P.act(lambda e, j=j: e.activation(out=nea[:, j * 16:(j + 1) * 16], in_=gpt[:, j * 32:j * 32 + 16], func=AF.Exp),
              reads=["gpt"], writes=["nea"])
    P.dve(lambda e: e.tensor_scalar(out=nea[:], in0=nea[:], scalar1=-1.0, scalar2=None, op0=ALU.mult),
          reads=["nea"], writes=["nea"])

    SKIP_ADA = (stop == 'const')
    NWB = 3
    wbuf = [sb("wb%d" % i, [128, 8 * 512], BF16) for i in range(NWB)]
    wstate = {"n": 0}

    WT = {}

    def wkey(name, r0, nr, c0, ncw):
        return (name, r0, nr, c0, ncw)

    def load_w(key, kc, ncols):
        if key not in WT:
            WT[key] = len(WT)
            assert len(WT) <= NWT
        assert key[2] == kc * 128 and key[4] == ncols
        src3 = wpack[WT[key]][:, 0:kc * ncols].rearrange("p (k n) -> p k n", k=kc)
        i = wstate["n"] % NWB
        wstate["n"] += 1
        nm = "wb%d" % i
        view = wbuf[i][:, 0:kc * ncols].rearrange("p (k n) -> p k n", k=kc)
        P.dma(lambda e: e.dma_start(out=view, in_=src3), writes=[nm], q="pool")
        return view, nm

    def wview(w, r0, nr, c0, ncw):
        return w[r0:r0 + nr, c0:c0 + ncw].rearrange("(k p) n -> p k n", p=128)

    if not SKIP_ADA:
        AN.top = 0
        wa = [AN.alloc(8 * 512) for i in range(2)]
        P.act(lambda e: e.activation(out=scT[:], in_=cTt[:], func=AF.Silu), reads=["cTt"], writes=["scT"])
        scT3 = v3(scT[:], 8)
        for g in range(12):
            wt = wa[g % 2]
            nm = "wa%d" % (g % 2)
            wt3 = wt[:].rearrange("p (k n) -> p k n", k=8)
            P.dma(lambda e, g=g, wt3=wt3: e.dma_start(out=wt3, in_=wview(w_ada, 0, 1024, g * 512, 512)), writes=[nm])
            for jj in range(4):
                j = g * 4 + jj
                for dc in range(8):
                    P.pe(lambda e, j=j, jj=jj, dc=dc, wt3=wt3: e.matmul(
                        pS[:, 2 * j:2 * j + 2], lhsT=wt3[:, dc, jj * 128:(jj + 1) * 128], rhs=scT3[:, dc, :],
                        start=(dc == 0), stop=(dc == 7)), reads=[nm, "scT"], writes=["pS"])
        P.dve(lambda e: e.tensor_tensor(out=v3(modT[:], 48), in0=v3(pS[:, 0:96], 48), in1=bc1(badaf[:], 2), op=ALU.add),
              reads=["pS", "badaf"], writes=["modT"])
        modT3 = v3(modT[:], 48)
        ADA1 = (stop == 'ada1')
        for job in range(0 if ADA1 else 2):
            for (dst, dn, c0, nrow) in ((sc1t, "sc1t", 8, 0), (sc2t, "sc2t", 32, 1)):
                P.dve(lambda e, dst=dst, c0=c0, nrow=nrow, job=job: e.scalar_tensor_tensor(
                    out=dst[:, job * 8:(job + 1) * 8], in0=modT3[:, c0:c0 + 8, job], scalar=1.0,
                    in1=npart[:, nrow * 8:(nrow + 1) * 8], op0=ALU.add, op1=ALU.mult),
                    reads=["modT", "npart"], writes=[dn])
        dg = AN.alloc(1024)
        for job in range(0 if (ADA1 or stop == 'ada2') else 2):
            for which, c0 in ((0, 16), (1, 40)):
                for c in range(8):
                    P.dve(lambda e, c=c, c0=c0, job=job: e.tensor_scalar(
                        out=dg[:, c * 128:(c + 1) * 128], in0=identf[:], scalar1=modT3[:, c0 + c, job:job + 1],
                        scalar2=None, op0=ALU.mult), reads=["identf", "modT"], writes=["dg"])
                for c in range(8):
                    P.pe(lambda e, c=c: e.matmul(pA[:, c * 128:(c + 1) * 128], lhsT=onesf[:], rhs=dg[:, c * 128:(c + 1) * 128],
                                                 start=True, stop=True), reads=["onesf", "dg"], writes=["pA"])
                off = (job * 2 + which) * 1024
                for hf in range(2):
                    P.act(lambda e, off=off, hf=hf: e.copy(out=gbc[:, off + hf * 512:off + (hf + 1) * 512], in_=pA[:, hf * 512:(hf + 1) * 512]), reads=["pA"], writes=["gbc"])
    P.barrier()

    def run_job(job):
        isS = (job == 1)
        xin = xs if isS else xp
        yout = ys if isS else yp
        nseq = 1 if isS else 4
        AN.top = 0
        xres = AN.alloc(8 * 1024)
        hT = AN.alloc(8 * 1024, BF16)
        hT3 = hT[:].rearrange("p (k n) -> p k n", k=8)
        mB_off = AN.top
        mB = AN.alloc(8 * 1024, BF16)
        mB3 = mB[:].rearrange("p (k n) -> p k n", k=8)
        ogT_off = AN.top
        ogT = AN.alloc(8 * 1024, BF16)
        ogT3 = ogT[:].rearrange("p (k n) -> p k n", k=8)
        stat = AN.alloc(64)
        hT2 = AN.alloc(16, BF16)
        T0 = AN.top
        xres3 = xres[:].rearrange("p (t n) -> p t n", t=8)
        g1bc = gbc[:, (job * 2) * 1024:(job * 2 + 1) * 1024]
        g2bc = gbc[:, (job * 2 + 1) * 1024:(job * 2 + 2) * 1024]

        def norm_transpose(src_ap, srcname, dst3, dstname, t, sct, shc0, ph, tag, part=0):
            junk = ph["junk"]
            xn = ph["xn"][t % 2]
            xnn = "xn%d" % (t % 2)
            tmpf = ph["tmpf"]
            col = stat[:, (t % 16) * 3:(t % 16) * 3 + 3]
            if part in (0, 1):
                P.act(lambda e: e.activation(out=junk[:], in_=src_ap, func=AF.Square, accum_out=col[:, 0:1]),
                      reads=[srcname], writes=["junk", "stat"])
                P.act(lambda e: e.activation(out=col[:, 1:2], in_=col[:, 0:1], func=AF.Ln, scale=1.0 / 1024, bias=epsb[:, 0:1]),
                      reads=["stat", "epsb"], writes=["stat"])
                P.act(lambda e: e.activation(out=col[:, 2:3], in_=col[:, 1:2], func=AF.Exp, scale=-0.5),
                      reads=["stat"], writes=["stat"])
                P.dve(lambda e: e.tensor_scalar(out=xn[:], in0=src_ap, scalar1=col[:, 2:3], scalar2=None, op0=ALU.mult),
                      reads=[srcname, "stat"], writes=[xnn])
            if part == 1:
                return
            for dc in range(8):
                P.pe(lambda e, dc=dc: e.transpose(pT[:, dc * 128:(dc + 1) * 128], xn[:, dc * 128:(dc + 1) * 128], identb[:]),
                     reads=[xnn, "identb"], writes=["pT"])
            P.dve(lambda e: e.tensor_tensor(out=v3(tmpf[:], 8), in0=v3(pT[:], 8), in1=bc1(sct, 128), op=ALU.mult),
                  reads=["pT", "sc1t", "sc2t"], writes=["tmpf"])
            (P.dve if tag == "f" else P.pool)(lambda e: e.tensor_tensor(out=dst3, in0=v3(tmpf[:], 8),
                                             in1=bc1(modT3[:, shc0:shc0 + 8, job], 128), op=ALU.add),
                   reads=["tmpf", "modT"], writes=[dstname])

        if True:
            AN.top = T0
            phtA = {"junk": AN.alloc(1024, BF16),
                   "xn": [AN.alloc(1024, BF16) for i in range(2)],
                   "tmpf": AN.alloc(1024)}
            xh = [AN.alloc(1024) for i in range(2)]
            hTh = AN.at(mB_off, 8 * 1024, BF16) if isS else None
            hTh3 = hTh[:].rearrange("p (k n) -> p k n", k=8) if isS else None
            for t in range(8):
                P.dma(lambda e, t=t: e.dma_start(out=xres3[:, t, :], in_=xin[t * 128:(t + 1) * 128, :]),
                      writes=["xres"])
                norm_transpose(xres3[:, t, :], "xres", hT3[:, :, t * 128:(t + 1) * 128], "hT", t,
                               sc1t[:, job * 8:(job + 1) * 8], 0, phtA, "a")
            if isS:
                for t in range(8):
                    xb_ = xh[t % 2]
                    xbn = "xh%d" % (t % 2)
                    P.dma(lambda e, t=t, xb_=xb_: e.dma_start(out=xb_[:], in_=xin[1024 + t * 128:1024 + (t + 1) * 128, :]),
                          writes=[xbn])
                    norm_transpose(xb_[:], xbn, hTh3[:, :, t * 128:(t + 1) * 128], "mB", 8 + t,
                                   sc1t[:, job * 8:(job + 1) * 8], 0, phtA, "h")
                P.pool(lambda e: e.tensor_copy(out=v3(hT2[:], 8), in_=hTh3[:, :, 0:2]), reads=["mB"], writes=["hT2"])
            else:
                P.pool(lambda e: e.memset(hT2[:], 0.0), writes=["hT2"])
            if stop == 'A':
                raise StopBuild()


            upads = [AN.alloc(2944), AN.alloc(2944)]
            cvo = AN.alloc(4 * 1024)
            cvo3 = cvo[:].rearrange("p (c n) -> p c n", c=4)
            ub = AN.at(ogT_off, 4 * 1024, BF16)
            ub3 = ub[:].rearrange("p (c n) -> p c n", c=4)
            sgtC = AN.alloc(1024)
            lnt = [AN.alloc(512) for i in range(4)]
            wa_v, wa_n = load_w(wkey("w_glu", 0, 1024, 0, 512), 8, 512)
            wb_v, wb_n = load_w(wkey("w_glu", 0, 1024, 512, 512), 8, 512)
            cdw3 = cdwt[:, job * 124:(job + 1) * 124].rearrange("p (c k) -> p c k", c=4)
            def gen_glu(cc):
                up = upads[cc % 2]
                upn = "upad%d" % (cc % 2)
                P.pool(lambda e, up=up: e.memset(up[:], 0.0), writes=[upn])
                passes = [(hT3, "hT", hf, False) for hf in range(2)]
                if isS and cc >= 2:
                    passes += [(hTh3, "mB", hf, True) for hf in range(2)]
                for (src3, srcn, hf, halo) in passes:
                    for dc in range(8):
                        P.pe(lambda e, dc=dc, src3=src3, hf=hf, cc=cc: e.matmul(
                            pA[:, 0:512], lhsT=wa_v[:, dc, cc * 128:(cc + 1) * 128], rhs=src3[:, dc, hf * 512:(hf + 1) * 512],
                            start=(dc == 0), stop=(dc == 7)), reads=[wa_n, srcn], writes=["pA"])
                    for dc in range(8):
                        P.pe(lambda e, dc=dc, src3=src3, hf=hf, cc=cc: e.matmul(
                            pB[:, 0:512], lhsT=wb_v[:, dc, cc * 128:(cc + 1) * 128], rhs=src3[:, dc, hf * 512:(hf + 1) * 512],
                            start=(dc == 0), stop=(dc == 7)), reads=[wb_n, srcn], writes=["pB"])
                    P.act(lambda e: e.activation(out=sgtC[:, 0:512], in_=pB[:, 0:512], func=AF.Sigmoid),
                          reads=["pB"], writes=["sgtC"])
                    if not isS:
                        dstv = up[:, 0:4 * 286].rearrange("p (s w) -> p s w", s=4)[:, 2 * hf:2 * hf + 2, 15:271]
                        inA = v3(pA[:, 0:512], 2)
                        inS = v3(sgtC[:, 0:512], 2)
                    elif cc < 2:
                        dstv = up[:, 0:16 * 94].rearrange("p (s w) -> p s w", s=16)[:, 8 * hf:8 * hf + 8, 15:79]
                        inA = v3(pA[:, 0:512], 8)
                        inS = v3(sgtC[:, 0:512], 8)
                    else:
                        r0 = (31 + 8 * hf) if halo else (15 + 8 * hf)
                        n = 448 if (halo and hf == 1) else 512
                        dstv = up[:, r0 * 64:r0 * 64 + n]
                        inA = pA[:, 0:n]
                        inS = sgtC[:, 0:n]
                    P.dve(lambda e, dstv=dstv, inA=inA, inS=inS: e.tensor_tensor(out=dstv, in0=inA, in1=inS, op=ALU.mult),
                          reads=["pA", "sgtC"], writes=[upn])
                    yield

            def gen_conv(cc):
                up = upads[cc % 2]
                upn = "upad%d" % (cc % 2)
                if not isS:
                    def iv(j, up=up):
                        return up[:, 0:4 * 286].rearrange("p (s w) -> p s w", s=4)[:, :, j:j + 256]
                    ov = v3(cvo3[:, cc, :], 4)
                elif cc < 2:
                    def iv(j, up=up):
                        return up[:, 0:16 * 94].rearrange("p (s w) -> p s w", s=16)[:, :, j:j + 64]
                    ov = v3(cvo3[:, cc, :], 16)
                else:
                    def iv(j, up=up):
                        return up[:, j * 64:j * 64 + 1024]
                    ov = cvo3[:, cc, :]
                P.dve(lambda e, ov=ov, iv=iv, cc=cc: e.tensor_scalar(
                    out=ov, in0=iv(0), scalar1=cdw3[:, cc, 0:1], scalar2=cpart[:, cc:cc + 1], op0=ALU.mult, op1=ALU.add),
                    reads=[upn, "cdwt", "cpart"], writes=["cvo"])
                for j in range(1, 31):
                    P.dve(lambda e, ov=ov, iv=iv, cc=cc, j=j: e.scalar_tensor_tensor(
                        out=ov, in0=iv(j), scalar=cdw3[:, cc, j:j + 1], in1=ov, op0=ALU.mult, op1=ALU.add),
                        reads=[upn, "cdwt", "cvo"], writes=["cvo"])
                    if j % 6 == 0:
                        yield

                yield

            def run_gens_c(gs):
                alive = [True] * len(gs)
                while any(alive):
                    for i_ in range(len(gs)):
                        if alive[i_]:
                            try:
                                next(gs[i_])
                            except StopIteration:
                                alive[i_] = False
            run_gens_c([gen_glu(0)])
            for cc in range(4):
                run_gens_c([gen_conv(cc)] + ([gen_glu(cc + 1)] if cc < 3 else []))
            for hf in range(2):
                sl = slice(hf * 512, (hf + 1) * 512)
                for cc in range(4):
                    P.pe(lambda e, cc=cc, sl=sl: e.matmul(pA[:, 0:512], lhsT=onesf[:], rhs=cvo3[:, cc, sl],
                                                          start=(cc == 0), stop=(cc == 3)), reads=["onesf", "cvo"], writes=["pA"])
                for cc in range(4):
                    P.pool(lambda e, cc=cc, sl=sl: e.tensor_tensor(out=lnt[cc % 2][:], in0=cvo3[:, cc, sl], in1=cvo3[:, cc, sl],
                                                                   op=ALU.mult), reads=["cvo"], writes=["lnt%d" % (cc % 2)])
                    P.pe(lambda e, cc=cc: e.matmul(pB[:, 0:512], lhsT=onesf[:], rhs=lnt[cc % 2][:],
                                                   start=(cc == 0), stop=(cc == 3)), reads=["onesf", "lnt%d" % (cc % 2)], writes=["pB"])
                mean, msq, var = lnt[2], lnt[3], lnt[0]
                P.dve(lambda e: e.tensor_scalar(out=mean[:], in0=pA[:, 0:512], scalar1=1.0 / 512, scalar2=None, op0=ALU.mult),
                      reads=["pA"], writes=["lnt2"])
                P.pool(lambda e: e.tensor_tensor(out=msq[:], in0=mean[:], in1=mean[:], op=ALU.mult), reads=["lnt2"], writes=["lnt3"])
                P.dve(lambda e: e.scalar_tensor_tensor(out=var[:], in0=pB[:, 0:512], scalar=1.0 / 512, in1=msq[:],
                                                       op0=ALU.mult, op1=ALU.subtract), reads=["pB", "lnt3"], writes=["lnt0"])
                P.act(lambda e: e.activation(out=var[:], in_=var[:], func=AF.Ln, bias=epsb[:, 0:1]), reads=["lnt0", "epsb"], writes=["lnt0"])
                P.act(lambda e: e.activation(out=var[:], in_=var[:], func=AF.Exp, scale=-0.5), reads=["lnt0"], writes=["lnt0"])
                for cc in range(4):
                    P.pool(lambda e, cc=cc, sl=sl: e.tensor_tensor(out=lnt[1][:], in0=cvo3[:, cc, sl], in1=mean[:], op=ALU.subtract),
                           reads=["cvo", "lnt2"], writes=["lnt1"])
                    P.pool(lambda e: e.tensor_tensor(out=lnt[1][:], in0=lnt[1][:], in1=var[:], op=ALU.mult),
                           reads=["lnt1", "lnt0"], writes=["lnt1"])
                    P.act(lambda e, cc=cc, sl=sl: e.activation(out=ub3[:, cc, sl], in_=lnt[1][:], func=AF.Silu,
                                                               scale=cpart[:, 4 + cc:5 + cc], bias=cpart[:, 8 + cc:9 + cc]),
                          reads=["lnt1", "cpart"], writes=["ogT"])
            for cn in range(2):
                wv, wn = load_w(wkey("w_b_out", 0, 512, cn * 512, 512), 4, 512)
                gv, gn = load_w(wkey("w_gate", 0, 1024, 1024 + cn * 512, 512), 8, 512)
                for jj in range(4):
                    j = cn * 4 + jj
                    for hf in range(2):
                        ui = j * 2 + hf
                        pX, pXn = [(pA, ["pA"]), (pB, ["pB"]), (pC, ["pC0", "pC1"])][ui % 3]
                        sg, sgn = [(sgtC[:, 0:512], "sgC0"), (sgtC[:, 512:1024], "sgC1"), (lnt[0][:], "lnt0")][ui % 3]
                        hs_ = slice(hf * 512, (hf + 1) * 512)
                        for kc in range(4):
                            P.pe(lambda e, kc=kc, jj=jj, hs_=hs_, wv=wv, pX=pX: e.matmul(
                                pX[:, 0:512], lhsT=wv[:, kc, jj * 128:(jj + 1) * 128],
                                rhs=ub3[:, kc, hs_], start=(kc == 0), stop=(kc == 3)),
                                reads=[wn, "ogT"], writes=pXn)
                        for dc in range(8):
                            P.pe(lambda e, dc=dc, jj=jj, hs_=hs_, gv=gv, pX=pX: e.matmul(
                                pX[:, 512:1024], lhsT=gv[:, dc, jj * 128:(jj + 1) * 128],
                                rhs=hT3[:, dc, hs_], start=(dc == 0), stop=(dc == 7)),
                                reads=[gn, "hT"], writes=pXn)
                        P.act(lambda e, pX=pX, sg=sg: e.activation(out=sg, in_=pX[:, 512:1024], func=AF.Sigmoid), reads=pXn, writes=[sgn])
                        P.dve(lambda e, j=j, hs_=hs_, pX=pX, sg=sg: e.tensor_tensor(out=mB3[:, j, hs_], in0=pX[:, 0:512], in1=sg, op=ALU.mult),
                              reads=pXn + [sgn], writes=["mB"])
        P.barrier()

        if stop == 'C':
            raise StopBuild()
        if True:
            AN.top = T0

            def t(name, shape, dt=F32):
                return AN.alloc(shape[1], dt)
            abt = t("abt", [128, 8 * 32])
            abt3 = v3(abt[:], 8)
            gt = t("gt", [128, 8 * 16])
            bt = t("bt", [128, 8 * 16])
            gc = t("gc", [128, 8 * 16])
            gtot = t("gtot", [128, 8 * 16])
            egc = t("egc", [128, 8 * 16])
            ekd = t("ekd", [128, 8 * 16])
            egt = t("egt", [128, 8 * 16])
            bgc = t("bgc", [128, 8 * 16])
            gt3, bt3, gc3, gtot3 = v3(gt[:], 8), v3(bt[:], 8), v3(gc[:], 8), v3(gtot[:], 8)
            egc3, ekd3, egt3, bgc3 = v3(egc[:], 8), v3(ekd[:], 8), v3(egt[:], 8), v3(bgc[:], 8)
            wabt = t("wabt", [128, 8 * 32], BF16)
            wab3 = v3(wabt[:], 8)
            P.dma(lambda e: e.dma_start(out=wab3, in_=w_ab[job].rearrange("(k p) n -> p k n", p=128)),
                  writes=["wabt"], q="pool")
            for tt in range(8):
                for dc in range(8):
                    P.pe(lambda e, tt=tt, dc=dc: e.matmul(pS[:, tt * 32:(tt + 1) * 32], lhsT=hT3[:, dc, tt * 128:(tt + 1) * 128],
                                                          rhs=wab3[:, dc, :], start=(dc == 0), stop=(dc == 7)),
                         reads=["hT", "wabt"], writes=["pS"])
            P.act(lambda e: e.copy(out=abt[:], in_=pS[:, 0:256]), reads=["pS"], writes=["abt"])
            gp3 = gpt[:, job * 32:(job + 1) * 32]
            P.dve(lambda e: e.tensor_tensor(out=gt3, in0=abt3[:, :, 0:16], in1=bcm(gp3[:, 16:32], 8), op=ALU.add),
                  reads=["abt", "gpt"], writes=["gt"])
            P.act(lambda e: e.activation(out=gt[:], in_=gt[:], func=AF.Exp), reads=["gt"], writes=["gt"])
            P.act(lambda e: e.activation(out=gt[:], in_=gt[:], func=AF.Ln, bias=onesf[:, 0:1]), reads=["gt", "onesf"], writes=["gt"])
            P.dve(lambda e: e.tensor_tensor(out=gt3, in0=gt3, in1=bcm(nea[:, job * 16:(job + 1) * 16], 8), op=ALU.mult),
                  reads=["gt", "nea"], writes=["gt"])
            P.act(lambda e: e.activation(out=bt3, in_=abt3[:, :, 16:32], func=AF.Sigmoid), reads=["abt"], writes=["bt"])
            for tt in range(8):
                for d in range(2):
                    P.pe(lambda e, tt=tt, d=d: e.matmul(pS[:, tt * 16 + d * 8:tt * 16 + d * 8 + 8], lhsT=TRI[d][:],
                                                        rhs=gt3[:, tt, d * 8:d * 8 + 8], start=True, stop=True),
                         reads=[TRIN[d], "gt"], writes=["pS"])
                P.pe(lambda e, tt=tt: e.matmul(pS[:, 128 + tt * 16:128 + (tt + 1) * 16], lhsT=onesf[:], rhs=gt3[:, tt, :],
                                               start=True, stop=True), reads=["onesf", "gt"], writes=["pS"])
            P.dve(lambda e: e.tensor_copy(out=gc[:], in_=pS[:, 0:128]), reads=["pS"], writes=["gc"])
            P.dve(lambda e: e.tensor_copy(out=gtot[:], in_=pS[:, 128:256]), reads=["pS"], writes=["gtot"])
            P.act(lambda e: e.activation(out=egc[:], in_=gc[:], func=AF.Exp), reads=["gc"], writes=["egc"])
            P.act(lambda e: e.activation(out=egt[:], in_=gtot[:], func=AF.Exp), reads=["gtot"], writes=["egt"])
            P.dve(lambda e: e.tensor_tensor(out=ekd[:], in0=gtot[:], in1=gc[:], op=ALU.subtract), reads=["gtot", "gc"], writes=["ekd"])
            P.act(lambda e: e.activation(out=ekd[:], in_=ekd[:], func=AF.Exp), reads=["ekd"], writes=["ekd"])
            P.dve(lambda e: e.tensor_tensor(out=bgc[:], in0=bt[:], in1=egc[:], op=ALU.mult), reads=["bt", "egc"], writes=["bgc"])

            AN.topA = 0

            def t2(n, dt=F32):
                nfl = n if dt == F32 else (n + 1) // 2
                if AN.topA + nfl <= 8192:
                    off = AN.topA
                    AN.topA += nfl
                    return AN.at(off, n, dt)
                return AN.alloc(n, dt)

            class BDir:
                pass
            BD = []
            for d_ in range(2):
                b_ = BDir()
                b_.X = [t2(1040 if (i_ == 0 or (d_ == 0 and i_ == 3)) else 1024) for i_ in range(4)]
                b_.Lb = [t2(1024, BF16) for _ in range(2)]
                b_.Nb = [t2(1024, BF16) for _ in range(2)]
                b_.Nfull = t2(1024, BF16)
                b_.Ttb, b_.attT, b_.qdT, b_.vbt, b_.kdt, b_.nwT = (t2(1024, BF16) for _ in range(6))
                b_.Sf = [t2(128) for _ in range(4)]
                b_.Sb = [t2(128, BF16) for _ in range(4)]
                b_.vnb = [t2(128, BF16) for _ in range(4)]
                BD.append(b_)

            def R0(nm):
                return nm + "_0"
            X1, X2, X3, X4 = BD[0].X
            raw, cvq, rq = X1, X2[:, 0:1024], X3[:, 0:1024]
            osq = X4[:, 0:1024]
            sq = t2(1024, BF16)
            ogt = t2(1024, BF16)
            fmT = [t2(1024, BF16) for i in range(3)]
            ktm = t2(1024, BF16)
            vtm = t2(1024, BF16)
            szt = t2(1024, BF16)
            osum = t2(1024)
            cw3 = cwt[:, job * 120:(job + 1) * 120].rearrange("p (c k) -> p c k", c=24)
            if stop == 'Dg':
                raise StopBuild()


            def d1_gen(h, wv, wn):
                def gen_ci(ci):
                    c24 = h * 3 + ci
                    pI, pIn = [(pA, ["pA"]), (pB, ["pB"]), (pC, ["pC0", "pC1"])][ci]
                    raw_, rawn = [(BD[0].X[0], "X1_0"), (BD[1].X[0], "X1_1"), (BD[0].X[3], "X4_0")][ci]
                    cvq_, cvqn = [(BD[0].X[1], "X2_0"), (BD[1].X[1], "X2_1"), (BD[1].X[3], "X4_1")][ci]
                    cvq_ = cvq_[:, 0:1024]
                    rq_, rqn = [(BD[0].X[2], "X3_0"), (BD[1].X[2], "X3_1"), (None, None)][ci]
                    if rq_ is not None:
                        rq_ = rq_[:, 0:1024]
                    for hf in range(2):
                        for dc in range(8):
                            P.pe(lambda e, dc=dc, hf=hf: e.matmul(
                                pI[:, hf * 512:(hf + 1) * 512], lhsT=wv[:, dc, ci * 128:(ci + 1) * 128],
                                rhs=hT3[:, dc, hf * 512:(hf + 1) * 512], start=(dc == 0), stop=(dc == 7)),
                                reads=[wn, "hT"], writes=pIn)
                    P.pool(lambda e: e.memset(raw_[:], 0.0), writes=[rawn])
                    if isS:
                        hv = v3(hT2[:], 8)
                        for dc in range(8):
                            P.pe(lambda e, dc=dc: e.matmul(
                                pS[:, 0:2], lhsT=wv[:, dc, ci * 128:(ci + 1) * 128], rhs=hv[:, dc, :],
                                start=(dc == 0), stop=(dc == 7)), reads=[wn, "hT2"], writes=["pS"])
                        P.act(lambda e: e.copy(out=raw_[:, 1026:1028], in_=pS[:, 0:2]), reads=["pS"], writes=[rawn])
                        P.act(lambda e: e.copy(out=raw_[:, 2:1026], in_=pI[:]), reads=pIn, writes=[rawn])

                        def iv(j):
                            return raw_[:, j:j + 1024]
                        ov = cvq_
                    else:
                        P.act(lambda e: e.copy(out=v3(raw_[:, 0:1040], 4)[:, :, 2:258], in_=v3(pI[:], 4)), reads=pIn, writes=[rawn])

                        def iv(j):
                            return v3(raw_[:, 0:1040], 4)[:, :, j:j + 256]
                        ov = v3(cvq_, 4)
                    yield
                    P.dve(lambda e: e.tensor_scalar(
                        out=ov, in0=iv(0), scalar1=cw3[:, c24, 0:1], scalar2=None, op0=ALU.mult),
                        reads=[rawn, "cwt"], writes=[cvqn])
                    for j in range(1, 5):
                        P.dve(lambda e, j=j: e.scalar_tensor_tensor(
                            out=ov, in0=iv(j), scalar=cw3[:, c24, j:j + 1], in1=ov, op0=ALU.mult, op1=ALU.add),
                            reads=[rawn, "cwt", cvqn], writes=[cvqn])
                    yield
                    if ci == 2:
                        P.act(lambda e: e.activation(out=fmT[2][:], in_=cvq_, func=AF.Silu), reads=[cvqn], writes=["fmT2"])
                    else:
                        P.act(lambda e: e.activation(out=cvq_, in_=cvq_, func=AF.Silu), reads=[cvqn], writes=[cvqn])
                        yield
                        P.act(lambda e: e.activation(out=sq[:], in_=cvq_, func=AF.Square), reads=[cvqn], writes=["sq"])
                        for hf in range(2):
                            P.pe(lambda e, hf=hf: e.matmul(pI[:, hf * 512:(hf + 1) * 512], lhsT=onesb[:], rhs=sq[:, hf * 512:(hf + 1) * 512],
                                                           start=True, stop=True), reads=["onesb", "sq"], writes=pIn)
                        P.act(lambda e: e.activation(out=rq_, in_=pI[:], func=AF.Ln, bias=epsb[:, 0:1]), reads=pIn + ["epsb"], writes=[rqn])
                        yield
                        P.act(lambda e: e.activation(out=rq_, in_=rq_, func=AF.Exp, scale=-0.5,
                                                     bias=(qbias[:, 0:1] if ci == 0 else zero1[:, 0:1])),
                              reads=[rqn, "qbias", "zero1"], writes=[rqn])
                        P.dve(lambda e: e.tensor_tensor(out=fmT[ci][:], in0=cvq_, in1=rq_, op=ALU.mult),
                              reads=[cvqn, rqn], writes=["fmT%d" % ci])
                    yield

                gl = [gen_ci(0), gen_ci(1), gen_ci(2)]
                alive = [True, True, True]
                while any(alive):
                    for i_ in range(3):
                        if alive[i_]:
                            try:
                                next(gl[i_])
                            except StopIteration:
                                alive[i_] = False
                    yield
                for (src, srcn, dst, dstn) in ((fmT[1], "fmT1", ktm, "ktm"), (fmT[2], "fmT2", vtm, "vtm")):
                    for tt in range(8):
                        P.pe(lambda e, tt=tt, src=src: e.transpose(pT[:, tt * 128:(tt + 1) * 128], src[:, tt * 128:(tt + 1) * 128], identb[:]),
                             reads=[srcn, "identb"], writes=["pT"])
                    P.act(lambda e, dst=dst: e.copy(out=dst[:], in_=pT[:]), reads=["pT"], writes=[dstn])
                    yield
                for tt in range(8):
                    for dc in range(8):
                        P.pe(lambda e, tt=tt, dc=dc: e.matmul(
                            pC[:, tt * 128:(tt + 1) * 128], lhsT=hT3[:, dc, tt * 128:(tt + 1) * 128], rhs=wv[:, dc, 384:512],
                            start=(dc == 0), stop=(dc == 7)), reads=[wn, "hT"], writes=["pC0", "pC1"])
                P.act(lambda e: e.activation(out=szt[:], in_=pC[:], func=AF.Silu), reads=["pC0", "pC1"], writes=["szt"])
                yield

            def d23(h):
                P.pool(lambda e: e.memset(osum[:], 0.0), writes=["osum"])
                def gen_dir(d):
                    bd = BD[d]

                    def R(nm):
                        return nm + "_%d" % d
                    X1, X2, X3, X4 = bd.X
                    gtri, dd, esym, eL = X1[:, 0:1024], X2[:, 0:1024], X3[:, 0:1024], X4[:, 0:1024]
                    ebc, Ttf, eA = X1[:, 0:1024], X2[:, 0:1024], X4[:, 0:1024]
                    cct = X3[:, 0:256]
                    Lb, Nb, Nfull = bd.Lb, bd.Nb, bd.Nfull
                    kbg = Lb[0]
                    Ttb, attT, qdT, vbt, kdt, nwT = bd.Ttb, bd.attT, bd.qdT, bd.vbt, bd.kdt, bd.nwT
                    Sf, Sb, vnb = bd.Sf, bd.Sb, bd.vnb

                    def init_state(ci_, d_):
                        if isS and d_ == 0:
                            P.dma(lambda e: e.dma_start(out=Sf[ci_][:], in_=s0[h]), writes=[R("Sf%d" % ci_)])
                        elif isS:
                            P.dma(lambda e: e.dma_start(out=v3(cct, 2), in_=cc_out[h].rearrange("(r p) f -> p r f", p=128)),
                                  reads=["cc_out%d" % h], writes=[R("X3")])
                            P.dve(lambda e: e.tensor_scalar(out=Sf[ci_][:], in0=cct[:, 0:128], scalar1=selt[:, 0:1], scalar2=None, op0=ALU.mult),
                                  reads=[R("X3"), "selt"], writes=[R("Sf%d" % ci_)])
                            P.dve(lambda e: e.scalar_tensor_tensor(out=Sf[ci_][:], in0=cct[:, 128:256], scalar=selt[:, 1:2], in1=Sf[ci_][:],
                                                                   op0=ALU.mult, op1=ALU.add), reads=[R("X3"), "selt", R("Sf%d" % ci_)], writes=[R("Sf%d" % ci_)])
                        else:
                            P.pool(lambda e: e.memset(Sf[ci_][:], 0.0), writes=[R("Sf%d" % ci_)])
                        P.act(lambda e: e.copy(out=Sb[ci_][:], in_=Sf[ci_][:]), reads=[R("Sf%d" % ci_)], writes=[R("Sb%d" % ci_)])

                    def run_chains(chs, d_):
                        col = d_ * 8 + h
                        nst = len(chs[0][1])
                        for ci_, (s_, order) in enumerate(chs):
                            init_state(ci_, d_)
                        for step in range(nst):
                            for ci_, (s_, order) in enumerate(chs):
                                c = order[step]
                                cs = slice(c * 128, (c + 1) * 128)
                                slot = ci_ % 2
                                pv = pC[:, slot * 512:(slot + 1) * 512]
                                pvn = "pC%d" % slot
                                Sfn, Sbn, vnn = R("Sf%d" % ci_), R("Sb%d" % ci_), R("vnb%d" % ci_)
                                P.pe(lambda e, cs=cs, pv=pv: e.matmul(pv[:, 0:128], lhsT=Ttb[:, cs], rhs=vbt[:, cs], start=True, stop=False),
                                     reads=[R("Ttb"), R("vbt")], writes=[pvn])
                                P.pe(lambda e, cs=cs, pv=pv, ci_=ci_: e.matmul(pv[:, 0:128], lhsT=nwT[:, cs], rhs=Sb[ci_][:], start=False, stop=True),
                                     reads=[R("nwT"), Sbn], writes=[pvn])
                                P.act(lambda e, pv=pv, ci_=ci_: e.copy(out=vnb[ci_][:], in_=pv[:, 0:128]), reads=[pvn], writes=[vnn])
                                P.pe(lambda e, cs=cs, pv=pv, ci_=ci_: e.matmul(pv[:, 128:256], lhsT=qdT[:, cs], rhs=Sb[ci_][:], start=True, stop=False),
                                     reads=[R("qdT"), Sbn], writes=[pvn])
                                P.pe(lambda e, cs=cs, pv=pv, ci_=ci_: e.matmul(pv[:, 128:256], lhsT=attT[:, cs], rhs=vnb[ci_][:], start=False, stop=True),
                                     reads=[R("attT"), vnn], writes=[pvn])
                                P.pe(lambda e, cs=cs, pv=pv, ci_=ci_: e.matmul(pv[:, 256:384], lhsT=kdt[:, cs], rhs=vnb[ci_][:], start=True, stop=True),
                                     reads=[R("kdt"), vnn], writes=[pvn])
                                P.dve(lambda e, cs=cs, pv=pv: e.tensor_tensor(out=osum[:, cs], in0=osum[:, cs], in1=pv[:, 128:256], op=ALU.add),
                                      reads=["osum", pvn], writes=["osum"])
                                P.dve(lambda e, pv=pv, ci_=ci_, c=c: e.scalar_tensor_tensor(
                                    out=Sf[ci_][:], in0=Sf[ci_][:], scalar=egt3[:, c, col:col + 1], in1=pv[:, 256:384], op0=ALU.mult, op1=ALU.add),
                                    reads=[Sfn, "egt", pvn], writes=[Sfn])
                                P.act(lambda e, ci_=ci_: e.copy(out=Sb[ci_][:], in_=Sf[ci_][:]), reads=[Sfn], writes=[Sbn])
                                yield

                    if stop == 'H1':
                        raise StopBuild()

                    pG, pGn = (pA, "pA") if d == 0 else (pB, "pB")
                    col = d * 8 + h
                    gcol = gt3[:, :, col]
                    P.dve(lambda e: e.tensor_tensor(out=v3(gtri, 8), in0=bcm(TRI[d][:], 8), in1=bc1(gcol, 128), op=ALU.mult),
                          reads=[TRIN[d], "gt"], writes=[R("X1")])
                    yield
                    for hf in range(2):
                        P.pe(lambda e, hf=hf: e.matmul(pG[:, hf * 512:(hf + 1) * 512], lhsT=onesf[:], rhs=gtri[:, hf * 512:(hf + 1) * 512],
                                                       start=True, stop=True), reads=["onesf", R("X1")], writes=[pGn])
                    P.dve(lambda e: e.tensor_tensor(out=v3(dd, 8), in0=v3(pG[:], 8), in1=bc1(gc3[:, :, col], 128), op=ALU.subtract),
                          reads=[pGn, "gc"], writes=[R("X2")])
                    P.act(lambda e: e.activation(out=ebc, in_=pG[:], func=AF.Exp), reads=[pGn], writes=[R("X1")])
                    P.dve(lambda e: e.scalar_tensor_tensor(out=dd, in0=dd, scalar=-1.0, in1=dd, op0=ALU.mult, op1=ALU.min), reads=[R("X2")], writes=[R("X2")])
                    P.act(lambda e: e.activation(out=esym, in_=dd, func=AF.Exp), reads=[R("X2")], writes=[R("X3")])
                    P.dve(lambda e: e.tensor_tensor(out=v3(eL, 8), in0=v3(esym, 8), in1=bcm(STR[d][:], 8), op=ALU.mult),
                           reads=[R("X3"), STRN[d]], writes=[R("X4")])
                    P.dve(lambda e: e.tensor_tensor(out=v3(eL, 8), in0=v3(eL, 8), in1=bc1(bt3[:, :, col], 128), op=ALU.mult),
                          reads=[R("X4"), "bt"], writes=[R("X4")])
                    P.dve(lambda e: e.tensor_tensor(out=qdT[:], in0=fmT[0][:], in1=ebc, op=ALU.mult),
                          reads=["fmT0", R("X1")], writes=[R("qdT")])
                    yield
                    for c in range(8):
                        cs = slice(c * 128, (c + 1) * 128)
                        P.pe(lambda e, cs=cs: e.matmul(pG[:, cs], lhsT=fmT[1][:, cs], rhs=fmT[1][:, cs], start=True, stop=True),
                             reads=["fmT1"], writes=[pGn])
                    P.dve(lambda e: e.tensor_tensor(out=Lb[0][:], in0=pG[:], in1=eL, op=ALU.mult), reads=[pGn, R("X4")], writes=[R("Lb0")])
                    P.pool(lambda e: e.tensor_tensor(out=v3(eA, 8), in0=v3(esym, 8), in1=bcm(TRI[d][:], 8), op=ALU.mult),
                           reads=[R("X3"), TRIN[d]], writes=[R("X4")])
                    yield
                    for c in range(8):
                        cs = slice(c * 128, (c + 1) * 128)
                        P.pe(lambda e, cs=cs: e.matmul(pG[:, cs], lhsT=fmT[1][:, cs], rhs=fmT[0][:, cs], start=True, stop=True),
                             reads=["fmT1", "fmT0"], writes=[pGn])
                    P.dve(lambda e: e.tensor_tensor(out=attT[:], in0=pG[:], in1=eA, op=ALU.mult),
                          reads=[pGn, R("X4")], writes=[R("attT")])
                    yield
                    for c in range(8):
                        cs = slice(c * 128, (c + 1) * 128)
                        P.pe(lambda e, cs=cs: e.transpose(pT[:, cs], Lb[0][:, cs], identb[:]), reads=[R("Lb0"), "identb"], writes=["pT"])
                    P.act(lambda e: e.copy(out=Nfull[:], in_=pT[:]), reads=["pT"], writes=[R("Nfull")])
                    P.dve(lambda e: e.tensor_tensor(out=v3(Lb[1][:], 8), in0=v3(Lb[0][:], 8), in1=bcm(mk[0], 8), op=ALU.mult),
                           reads=[R("Lb0"), "mkt"], writes=[R("Lb1")])
                    P.pool(lambda e: e.tensor_tensor(out=v3(Nb[1][:], 8), in0=v3(Nfull[:], 8), in1=bcm(mk[0], 8), op=ALU.mult),
                           reads=[R("Nfull"), "mkt"], writes=[R("Nb1")])
                    P.dve(lambda e: e.tensor_tensor(out=v3(Ttf, 8), in0=bcm(identf[:], 8), in1=v3(Nb[1][:], 8), op=ALU.subtract),
                          reads=["identf", R("Nb1")], writes=[R("X2")])
                    P.act(lambda e: e.copy(out=Ttb[:], in_=Ttf), reads=[R("X2")], writes=[R("Ttb")])
                    yield
                    cur = 1
                    for k in range(1, 3):
                        nxt = 1 - cur
                        for c in range(8):
                            cs = slice(c * 128, (c + 1) * 128)
                            P.pe(lambda e, cs=cs, cur=cur: e.matmul(pG[:, cs], lhsT=Nb[cur][:, cs], rhs=Lb[cur][:, cs], start=True, stop=True),
                                 reads=[R("Nb%d" % cur), R("Lb%d" % cur)], writes=[pGn])
                        P.act(lambda e, nxt=nxt: e.copy(out=Lb[nxt][:], in_=pG[:]), reads=[pGn], writes=[R("Lb%d" % nxt)])
                        yield
                        if k < 2:
                            for c in range(8):
                                cs = slice(c * 128, (c + 1) * 128)
                                P.pe(lambda e, cs=cs, cur=cur: e.matmul(pG[:, cs], lhsT=Lb[cur][:, cs], rhs=Nb[cur][:, cs], start=True, stop=True),
                                     reads=[R("Nb%d" % cur), R("Lb%d" % cur)], writes=[pGn])
                            P.dve(lambda e, nxt=nxt: e.tensor_copy(out=Nb[nxt][:], in_=pG[:]), reads=[pGn], writes=[R("Nb%d" % nxt)])
                            yield
                        for c in range(8):
                            cs = slice(c * 128, (c + 1) * 128)
                            P.pe(lambda e, cs=cs, nxt=nxt: e.matmul(pG[:, cs], lhsT=Lb[nxt][:, cs], rhs=Ttb[:, cs], start=True, stop=True),
                                 reads=[R("Lb%d" % nxt), R("Ttb")], writes=[pGn])
                        P.dve(lambda e: e.tensor_tensor(out=Ttb[:], in0=Ttf, in1=pG[:], op=ALU.add), reads=[R("X2"), pGn], writes=[R("Ttb")])
                        P.dve(lambda e: e.tensor_tensor(out=Ttf, in0=Ttf, in1=pG[:], op=ALU.add), reads=[R("X2"), pGn], writes=[R("X2")])
                        yield
                        cur = nxt
                    Tf, Tb, Ub, NOb = Ttf, Lb[1], Nb[0], Nb[1]
                    for c in range(8):
                        cs = slice(c * 128, (c + 1) * 128)
                        P.pe(lambda e, cs=cs: e.transpose(pT[:, cs], Ttb[:, cs], identb[:]), reads=[R("Ttb"), "identb"], writes=["pT"])
                    P.act(lambda e: e.copy(out=Tb[:], in_=pT[:]), reads=["pT"], writes=[R("Lb1")])
                    P.dve(lambda e: e.tensor_copy(out=Tf, in_=Tb[:]), reads=[R("Lb1")], writes=[R("X2")])
                    yield
                    NOs = [(Nb[1], R("Nb1")), (Lb[0], R("Lb0"))]

                    def mkmask(lvl):
                        nb_, nn_ = NOs[lvl % 2]
                        P.pool(lambda e: e.tensor_tensor(out=v3(nb_[:], 8), in0=v3(Nfull[:], 8), in1=bcm(mk[lvl], 8), op=ALU.mult),
                               reads=[R("Nfull"), "mkt"], writes=[nn_])
                    mkmask(1)
                    for lvl in range(1, 5):
                        NOb, NOn = NOs[lvl % 2]
                        for c in range(8):
                            cs = slice(c * 128, (c + 1) * 128)
                            P.pe(lambda e, cs=cs, NOb=NOb: e.matmul(pG[:, cs], lhsT=NOb[:, cs], rhs=Tb[:, cs], start=True, stop=True),
                                 reads=[NOn, R("Lb1")], writes=[pGn])
                        P.act(lambda e: e.copy(out=Ub[:], in_=pG[:]), reads=[pGn], writes=[R("Nb0")])
                        if lvl < 4:
                            mkmask(lvl + 1)
                        yield
                        for c in range(8):
                            cs = slice(c * 128, (c + 1) * 128)
                            P.pe(lambda e, cs=cs: e.matmul(pG[:, cs], lhsT=Ttb[:, cs], rhs=Ub[:, cs], start=True, stop=True),
                                 reads=[R("Ttb"), R("Nb0")], writes=[pGn])
                        P.dve(lambda e: e.tensor_tensor(out=Tb[:], in0=Tf, in1=pG[:], op=ALU.subtract), reads=[R("X2"), pGn], writes=[R("Lb1")])
                        if lvl < 4:
                            P.dve(lambda e: e.tensor_tensor(out=Tf, in0=Tf, in1=pG[:], op=ALU.subtract), reads=[R("X2"), pGn], writes=[R("X2")])
                        yield
                        for c in range(8):
                            cs = slice(c * 128, (c + 1) * 128)
                            P.pe(lambda e, cs=cs: e.transpose(pT[:, cs], Tb[:, cs], identb[:]), reads=[R("Lb1"), "identb"], writes=["pT"])
                        P.act(lambda e: e.copy(out=Ttb[:], in_=pT[:]), reads=["pT"], writes=[R("Ttb")])
                        yield
                    P.dve(lambda e: e.tensor_tensor(out=v3(vbt[:], 8), in0=v3(vtm[:], 8), in1=bc1(bt3[:, :, col], 128), op=ALU.mult),
                          reads=["vtm", "bt"], writes=[R("vbt")])
                    P.dve(lambda e: e.tensor_tensor(out=v3(kbg[:], 8), in0=v3(ktm[:], 8), in1=bc1(bgc3[:, :, col], 128), op=ALU.mult),
                          reads=["ktm", "bgc"], writes=[R("Lb0")])
                    P.pool(lambda e: e.tensor_tensor(out=v3(kdt[:], 8), in0=v3(ktm[:], 8), in1=bc1(ekd3[:, :, col], 128), op=ALU.mult),
                           reads=["ktm", "ekd"], writes=[R("kdt")])
                    yield
                    for c in range(8):
                        cs = slice(c * 128, (c + 1) * 128)
                        P.pe(lambda e, cs=cs: e.matmul(pG[:, cs], lhsT=kbg[:, cs], rhs=Ttb[:, cs], start=True, stop=True),
                             reads=[R("Lb0"), R("Ttb")], writes=[pGn])
                    P.act(lambda e: e.mul(out=nwT[:], in_=pG[:], mul=-1.0), reads=[pGn], writes=[R("nwT")])
                    yield

                    if isS:
                        if d == 1:
                            yield "WAIT0"
                            P.dma(lambda e: e.dma_start(out=cc_in[h], in_=BD[0].Sf[0][:]), reads=["Sf0_0"], writes=["cc_in%d" % h], q="pool")

                            def ccfn(e):
                                e.collective_compute("AllGather", ALU.bypass, replica_groups=GROUPS,
                                                     ins=[cc_in[h]], outs=[cc_out[h]]).then_inc(s_cc, 1)
                                e.wait_ge(s_cc, h + 1)
                            P.op("pool", ccfn, reads=["cc_in%d" % h], writes=["cc_out%d" % h], selfsig=True)
                        yield from run_chains([(0, list(range(8)) if d == 0 else list(range(7, -1, -1)))], d)
                    else:
                        chs = [(s_, ([2 * s_, 2 * s_ + 1] if d == 0 else [2 * s_ + 1, 2 * s_])) for s_ in range(4)]
                        yield from run_chains(chs, d)
                        for ci_, (s_, order) in enumerate(chs):
                            P.dma(lambda e, ci_=ci_, s_=s_, d=d: e.dma_start(out=st[s_, d, h], in_=Sf[ci_][:]), reads=[R("Sf%d" % ci_)])

                    if stop == 'X%d_%d' % (h, d):
                        raise StopBuild()


                gens = [gen_dir(0), gen_dir(1)]
                done = [False, False]

                def step(i_):
                    if done[i_]:
                        return None
                    try:
                        return next(gens[i_])
                    except StopIteration:
                        done[i_] = True
                        return None
                while not (done[0] and done[1]):
                    step(0)
                    r_ = step(1)
                    if r_ == "WAIT0":
                        while not done[0]:
                            step(0)

            def d4_gen(h):
                for tt in range(8):
                    P.act(lambda e, tt=tt: e.activation(out=ogt[:, tt * 128:(tt + 1) * 128], in_=osum[:, tt * 128:(tt + 1) * 128],
                                                        func=AF.Square, accum_out=stat[:, 48 + tt:49 + tt]),
                          reads=["osum"], writes=["ogt", "stat"])
                P.act(lambda e: e.activation(out=stat[:, 48:56], in_=stat[:, 48:56], func=AF.Ln, scale=1.0 / 128, bias=epsb[:, 0:1]),
                      reads=["stat", "epsb"], writes=["stat"])
                P.act(lambda e: e.activation(out=stat[:, 48:56], in_=stat[:, 48:56], func=AF.Exp, scale=-0.5), reads=["stat"], writes=["stat"])
                yield
                P.dve(lambda e: e.tensor_tensor(out=v3(osum[:], 8), in0=v3(osum[:], 8), in1=bc1(stat[:, 48:56], 128), op=ALU.mult),
                      reads=["osum", "stat"], writes=["osum"])
                P.dve(lambda e: e.tensor_tensor(out=v3(osum[:], 8), in0=v3(osum[:], 8), in1=bcm(normo_bc[:], 8), op=ALU.mult),
                      reads=["osum", "normo_bc"], writes=["osum"])
                yield
                P.dve(lambda e: e.tensor_tensor(out=ogt[:], in0=osum[:], in1=szt[:], op=ALU.mult), reads=["osum", "szt"], writes=["ogt"])
                yield
                for tt in range(8):
                    P.pe(lambda e, tt=tt: e.transpose(pT[:, tt * 128:(tt + 1) * 128], ogt[:, tt * 128:(tt + 1) * 128], identb[:]),
                         reads=["ogt", "identb"], writes=["pT"])
                P.act(lambda e: e.copy(out=ogT3[:, h, :], in_=pT[:]), reads=["pT"], writes=["ogT"])
                yield

            def run_gens(gs):
                alive = [True] * len(gs)
                while any(alive):
                    for i_ in range(len(gs)):
                        if alive[i_]:
                            try:
                                next(gs[i_])
                            except StopIteration:
                                alive[i_] = False

            heads = list(HEADS if HEADS else range(8))
            wl = {heads[0]: load_w(wkey("w_in_h", 0, 1024, heads[0] * 512, 512), 8, 512)}
            run_gens([d1_gen(heads[0], *wl[heads[0]])])
            for hi, h in enumerate(heads):
                nh = heads[hi + 1] if hi + 1 < len(heads) else None
                if nh is not None:
                    wl[nh] = load_w(wkey("w_in_h", 0, 1024, nh * 512, 512), 8, 512)
                if nh is None:
                    pre_E = [load_w(wkey("w_a_out", 0, 1024, 0, 512), 8, 512), load_w(wkey("w_gate", 0, 1024, 0, 512), 8, 512)]
                d23(h)
                if SEQ_D4:
                    run_gens([d4_gen(h)])
                    if nh is not None:
                        run_gens([d1_gen(nh, *wl[nh])])
                else:
                    run_gens([d4_gen(h)] + ([d1_gen(nh, *wl[nh])] if nh is not None else []))
                if stop == 'D%d' % h:
                    raise StopBuild()
        P.barrier()

        if stop == 'D':
            raise StopBuild()
        if True:
            AN.top = T0
            sgtE = AN.alloc(1024)
            tmpeE = AN.alloc(1024)
            actb = AN.at(mB_off, 22 * 1024, BF16)
            act3 = actb[:].rearrange("p (k n) -> p k n", k=22)
            phtF = {"junk": AN.alloc(1024, BF16),
                   "xn": [AN.alloc(1024, BF16) for i in range(2)],
                   "tmpf": AN.alloc(1024)}
            for tt in range(8):
                P.dma(lambda e, tt=tt: e.dma_start(out=xres3[:, tt, :], in_=xin[tt * 128:(tt + 1) * 128, :]), writes=["xres"])
            for cn in range(2):
                if cn == 0:
                    (wv, wn), (gv, gn) = pre_E
                else:
                    wv, wn = load_w(wkey("w_a_out", 0, 1024, cn * 512, 512), 8, 512)
                    gv, gn = load_w(wkey("w_gate", 0, 1024, cn * 512, 512), 8, 512)
                for jj in range(4):
                    j = cn * 4 + jj
                    for hf in range(2):
                        ui = j * 2 + hf
                        pX, pXn = [(pA, ["pA"]), (pB, ["pB"]), (pC, ["pC0", "pC1"])][ui % 3]
                        sg, sgn = [(sgtE[:, 0:512], "sgE0"), (sgtE[:, 512:1024], "sgE1"), (phtF["tmpf"][:, 0:512], "tmpf")][ui % 3]
                        hs_ = slice(hf * 512, (hf + 1) * 512)
                        for kc in range(8):
                            P.pe(lambda e, kc=kc, jj=jj, hs_=hs_, wv=wv, pX=pX: e.matmul(
                                pX[:, 0:512], lhsT=wv[:, kc, jj * 128:(jj + 1) * 128],
                                rhs=ogT3[:, kc, hs_], start=(kc == 0), stop=(kc == 7)),
                                reads=[wn, "ogT"], writes=pXn)
                        for dc in range(8):
                            P.pe(lambda e, dc=dc, jj=jj, hs_=hs_, gv=gv, pX=pX: e.matmul(
                                pX[:, 512:1024], lhsT=gv[:, dc, jj * 128:(jj + 1) * 128],
                                rhs=hT3[:, dc, hs_], start=(dc == 0), stop=(dc == 7)),
                                reads=[gn, "hT"], writes=pXn)
                        P.act(lambda e, pX=pX, sg=sg: e.activation(out=sg, in_=pX[:, 512:1024], func=AF.Sigmoid), reads=pXn, writes=[sgn])
                        P.dve(lambda e, pX=pX, sg=sg, hs_=hs_: e.tensor_tensor(out=tmpeE[:, hs_], in0=pX[:, 0:512], in1=sg, op=ALU.mult),
                              reads=pXn + [sgn], writes=["tmpeE"])
                        P.dve(lambda e, j=j, hs_=hs_: e.tensor_tensor(out=mB3[:, j, hs_], in0=tmpeE[:, hs_], in1=mB3[:, j, hs_], op=ALU.add),
                              reads=["tmpeE", "mB"], writes=["mB"])
            if stop == 'E':
                raise StopBuild()
            wo = [load_w(wkey("w_o", 0, 1024, cn * 512, 512), 8, 512) for cn in range(2)]
            for tt in range(8):
                for cn in range(2):
                    pv, pvn = (pA, "pA") if cn == 0 else (pB, "pB")
                    for dc in range(8):
                        P.pe(lambda e, dc=dc, tt=tt, cn=cn, pv=pv: e.matmul(
                            pv[:, 0:512], lhsT=mB3[:, dc, tt * 128:(tt + 1) * 128], rhs=wo[cn][0][:, dc, :],
                            start=(dc == 0), stop=(dc == 7)), reads=["mB", wo[cn][1]], writes=[pvn])
                    P.dve(lambda e, cn=cn, pv=pv: e.tensor_tensor(out=tmpeE[:, cn * 512:(cn + 1) * 512], in0=pv[:, 0:512],
                                                                  in1=g1bc[:, cn * 512:(cn + 1) * 512], op=ALU.mult),
                          reads=[pvn, "gbc"], writes=["tmpeE"])
                P.dve(lambda e, tt=tt: e.tensor_tensor(out=xres3[:, tt, :], in0=xres3[:, tt, :], in1=tmpeE[:], op=ALU.add),
                       reads=["xres", "tmpeE"], writes=["xres"])
                norm_transpose(xres3[:, tt, :], "xres", hT3[:, :, tt * 128:(tt + 1) * 128], "hT", tt,
                               sc2t[:, job * 8:(job + 1) * 8], 24, phtF, "f", part=1)
                if tt >= 1:
                    norm_transpose(xres3[:, tt - 1, :], "xres", hT3[:, :, (tt - 1) * 128:tt * 128], "hT", tt - 1,
                                   sc2t[:, job * 8:(job + 1) * 8], 24, phtF, "f", part=2)
            norm_transpose(xres3[:, 7, :], "xres", hT3[:, :, 7 * 128:8 * 128], "hT", 7,
                           sc2t[:, job * 8:(job + 1) * 8], 24, phtF, "f", part=2)
            if stop == 'F':
                raise StopBuild()
            P.barrier()
            AN.top = mB_off + 11 * 1024
            sgtG = AN.alloc(1024)
            tmpeG = AN.alloc(1024)
            phtG = {"junk": AN.alloc(1024, BF16)}
            yt = [AN.alloc(1024) for i in range(2)]
            normf_bc = AN.alloc(1024)
            stg = [AN.alloc(8 * 512) for _ in range(2)]
            stgn = [0]

            def load_w_hw(key, kc, ncols):
                if key not in WT:
                    WT[key] = len(WT)
                    assert len(WT) <= NWT
                src3 = wpack[WT[key]][:, 0:kc * ncols].rearrange("p (k n) -> p k n", k=kc)
                k_ = stgn[0] % 2
                stgn[0] += 1
                sv = stg[k_][:, 0:kc * ncols].rearrange("p (k n) -> p k n", k=kc)
                P.dma(lambda e: e.dma_start(out=sv, in_=src3), writes=["stg%d" % k_])
                i = wstate["n"] % NWB
                wstate["n"] += 1
                nm = "wb%d" % i
                view = wbuf[i][:, 0:kc * ncols].rearrange("p (k n) -> p k n", k=kc)
                P.dve(lambda e: e.tensor_copy(out=view, in_=sv), reads=["stg%d" % k_], writes=[nm])
                return view, nm
            P.dma(lambda e: e.dma_start(out=normf_bc[:], in_=norm_f.partition_broadcast(128)), writes=["normf_bc"])
            for j in range(22):
                if j % 4 == 0:
                    ncol = min(512, FF - j * 128)
                    wg = load_w(wkey("w_gu", 0, 1024, j * 128, ncol), 8, ncol)
                    wu = load_w_hw(wkey("w_gu", 0, 1024, FF + j * 128, ncol), 8, ncol)
                jj = j % 4
                for hf in range(2):
                    ui = j * 2 + hf
                    pX, pXn = [(pA, ["pA"]), (pB, ["pB"]), (pC, ["pC0", "pC1"])][ui % 3]
                    sg, sgn = [(sgtG[:, 0:512], "sgS0"), (sgtG[:, 512:1024], "sgS1"), (yt[0][:, 0:512], "yt0")][ui % 3]
                    for dc in range(8):
                        P.pe(lambda e, dc=dc, jj=jj, hf=hf, wg=wg, pX=pX: e.matmul(
                            pX[:, 0:512], lhsT=wg[0][:, dc, jj * 128:(jj + 1) * 128],
                            rhs=hT3[:, dc, hf * 512:(hf + 1) * 512], start=(dc == 0), stop=(dc == 7)),
                            reads=[wg[1], "hT"], writes=pXn)
                    for dc in range(8):
                        P.pe(lambda e, dc=dc, jj=jj, hf=hf, wu=wu, pX=pX: e.matmul(
                            pX[:, 512:1024], lhsT=wu[0][:, dc, jj * 128:(jj + 1) * 128],
                            rhs=hT3[:, dc, hf * 512:(hf + 1) * 512], start=(dc == 0), stop=(dc == 7)),
                            reads=[wu[1], "hT"], writes=pXn)
                    P.act(lambda e, pX=pX, sg=sg: e.activation(out=sg, in_=pX[:, 0:512], func=AF.Silu), reads=pXn, writes=[sgn])
                    P.dve(lambda e, j=j, hf=hf, pX=pX, sg=sg: e.tensor_tensor(out=act3[:, j, hf * 512:(hf + 1) * 512], in0=pX[:, 512:1024], in1=sg, op=ALU.mult),
                          reads=pXn + [sgn], writes=["actb"])
            def final_tile(tt):
                col = stat[:, tt * 3:tt * 3 + 3]
                y = yt[tt % 2]
                yn = "yt%d" % (tt % 2)
                P.act(lambda e, tt=tt, col=col: e.activation(out=phtG["junk"][:], in_=xres3[:, tt, :], func=AF.Square, accum_out=col[:, 0:1]),
                      reads=["xres"], writes=["junk", "stat"])
                P.act(lambda e, col=col: e.activation(out=col[:, 1:2], in_=col[:, 0:1], func=AF.Ln, scale=1.0 / 1024, bias=epsb[:, 0:1]),
                      reads=["stat", "epsb"], writes=["stat"])
                P.act(lambda e, col=col: e.activation(out=col[:, 2:3], in_=col[:, 1:2], func=AF.Exp, scale=-0.5), reads=["stat"], writes=["stat"])
                P.dve(lambda e, tt=tt, col=col, y=y: e.scalar_tensor_tensor(out=y[:], in0=xres3[:, tt, :], scalar=col[:, 2:3], in1=normf_bc[:],
                                                                            op0=ALU.mult, op1=ALU.mult),
                      reads=["xres", "stat", "normf_bc"], writes=[yn])
                P.dma(lambda e, tt=tt, y=y: e.dma_start(out=yout[tt * 128:(tt + 1) * 128, :], in_=y[:]), reads=[yn])

            for cn in range(2):
                wd = [load_w(wkey("w_down", k0 * 128, nk * 128, cn * 512, 512), nk, 512)
                      for (k0, nk) in ((0, 8), (8, 8), (16, 6))]
                for tt in range(8):
                    pv, pvn = (pA, "pA") if tt % 2 == 0 else (pB, "pB")
                    for j in range(22):
                        wdv, wdn = wd[j // 8]
                        P.pe(lambda e, j=j, tt=tt, wdv=wdv, pv=pv: e.matmul(
                            pv[:, 0:512], lhsT=act3[:, j, tt * 128:(tt + 1) * 128], rhs=wdv[:, j % 8, :],
                            start=(j == 0), stop=(j == 21)), reads=["actb", wdn], writes=[pvn])
                    P.dve(lambda e, cn=cn, pv=pv: e.tensor_tensor(out=tmpeG[:, 0:512], in0=pv[:, 0:512],
                                                                  in1=g2bc[:, cn * 512:(cn + 1) * 512], op=ALU.mult),
                          reads=[pvn, "gbc"], writes=["tmpeG"])
                    P.dve(lambda e, tt=tt, cn=cn: e.tensor_tensor(out=xres3[:, tt, cn * 512:(cn + 1) * 512],
                                                                   in0=xres3[:, tt, cn * 512:(cn + 1) * 512], in1=tmpeG[:, 0:512], op=ALU.add),
                           reads=["xres", "tmpeG"], writes=["xres"])
                    if cn == 1:
                        final_tile(tt)
        P.barrier()

    try:
        if stop in ('ada', 'const', 'ada1', 'ada2'):
            raise StopBuild()
        run_job(0)
        if stop == 'P':
            raise StopBuild()
        if enable_S:
            run_job(1)
    except StopBuild:
        pass

    WT_KEYS[:] = sorted(WT, key=WT.get)
    run = P.emit(sems, dma_sems)
    if stats:
        for e_ in ENGINES:
            print(e_, 'ops', len(P.ops[e_]), 'signals', sum(1 for r_ in P.ops[e_] if r_['signal']), 'dmas', P.ndma[e_])
    with nc.Block() as block:
        @block.sync
        def _(eng):
            run("sp", eng)

        @block.tensor
        def _(eng):
            run("pe", eng)

        @block.scalar
        def _(eng):
            run("act", eng)

        @block.vector
        def _(eng):
            run("dve", eng)

        @block.gpsimd
        def _(eng):
            run("pool", eng)
    es.close()
    return nc


_NC_CACHE = {}


def _prep_inputs(inp):
    f = lambda a: np.ascontiguousarray(np.asarray(a, dtype=np.float32))
    x_prompt, x_sample = f(inp["x_prompt"]), f(inp["x_sample"])
    state_delta, c, c_ctx = f(inp["state_delta"]), f(inp["c"]), f(inp["c_ctx"])
    w_in = f(inp["w_in"])[0]
    cols = []
    for h in range(8):
        for base in (0, 1024, 2048, 3072):
            cols.append(np.arange(base + h * 128, base + (h + 1) * 128))
    cols = np.concatenate(cols)
    w_in_h = np.ascontiguousarray(w_in[:, cols])
    w_ab_n = w_in[:, 4096:4128]
    w_ab_sw = w_ab_n.reshape(1024, 2, 2, 8)[:, :, ::-1, :].reshape(1024, 32)
    w_glu = np.ascontiguousarray(w_in[:, 4128:5152])
    w_gate = np.ascontiguousarray(w_in[:, 5152:7200])
    conv_qkv = f(inp["conv_qkv"])[0]
    ccols = []
    for h in range(8):
        for base in (0, 1024, 2048):
            ccols.append(base + h * 128)

    def mk_cw(cq):
        out = np.zeros((128, 24, 5), np.float32)
        for i, c0 in enumerate(ccols):
            out[:, i, :] = cq[:, c0:c0 + 128].T
        return out.reshape(128, 120)
    cw_n, cw_f = mk_cw(conv_qkv), mk_cw(conv_qkv[::-1])
    a_log, dt_bias = f(inp["a_log"])[0], f(inp["dt_bias"])[0]
    gp_n = np.stack([a_log.reshape(16), dt_bias.reshape(16)])
    gp_s = np.stack([a_log[::-1].reshape(16), dt_bias[::-1].reshape(16)])
    conv_dw = f(inp["conv_dw"])[0]

    def mk_cdw(cd):
        return np.ascontiguousarray(cd.T.reshape(4, 128, 31).transpose(1, 0, 2)).reshape(128, 124)
    cdw_n, cdw_f = mk_cdw(conv_dw), mk_cdw(conv_dw[::-1])
    fm = lambda v, k: np.ascontiguousarray(f(v).reshape(k, 128).T)
    cpar = np.concatenate([fm(inp["b_dw"][0], 4), fm(inp["ln_g"][0], 4), fm(inp["ln_b"][0], 4)], axis=1)
    npar = np.concatenate([fm(inp["norm1"][0], 8), fm(inp["norm2"][0], 8)], axis=1)
    common = {
        "w_ada": f(inp["w_ada"])[0], "b_ada_fm": fm(inp["b_ada"][0], 48),
        "cpar": np.ascontiguousarray(cpar), "npar": np.ascontiguousarray(npar),
        "norm_o": f(inp["norm_o"]).reshape(1, 128), "norm_f": f(inp["norm_f"]).reshape(1, 1024),
    }
    if not WT_KEYS:
        _NC_CACHE["nc"] = build_program(True)
    Wd = {"w_in_h": w_in_h, "w_glu": w_glu, "w_gate": w_gate, "w_a_out": f(inp["w_a_out"])[0],
          "w_b_out": f(inp["w_b_out"])[0], "w_o": f(inp["w_o"])[0], "w_gu": f(inp["w_gu"])[0], "w_down": f(inp["w_down"])[0]}
    wpack = np.zeros((NWT, 128, 4096), np.float32)
    for ti, (wn_, r0, nr, c0, ncw) in enumerate(WT_KEYS):
        kc_ = nr // 128
        wpack[ti, :, :kc_ * ncw] = Wd[wn_][r0:r0 + nr, c0:c0 + ncw].reshape(kc_, 128, ncw).transpose(1, 0, 2).reshape(128, kc_ * ncw)
    common["wpack"] = wpack
    ii = np.arange(128)
    mlist = [(ii[:, None] // 8 == ii[None, :] // 8)]
    for s_ in (8, 16, 32, 64):
        mlist.append((ii[:, None] // (2 * s_) == ii[None, :] // (2 * s_)) & (ii[:, None] // s_ != ii[None, :] // s_))
    common["masks"] = np.ascontiguousarray(np.stack(mlist).astype(np.float32))
    in_maps = []
    for core in range(8):
        b, r = core // 2, core % 2
        m = dict(common)
        xsb = x_sample[b]
        m["xs"] = np.ascontiguousarray(xsb if r == 0 else xsb[::-1])
        m["xp"] = np.ascontiguousarray(x_prompt[4 * core:4 * core + 4].reshape(1024, 1024))
        cv = np.stack([c_ctx, c[b]], axis=0)
        m["cT"] = np.ascontiguousarray(cv.reshape(2, 8, 128).transpose(2, 1, 0)).reshape(128, 16)
        m["w_ab"] = np.ascontiguousarray(np.stack([w_ab_n, w_ab_n if r == 0 else w_ab_sw]))
        m["cw"] = np.ascontiguousarray(np.stack([cw_n, cw_n if r == 0 else cw_f]))
        m["gpar"] = np.ascontiguousarray(np.stack([gp_n, gp_n if r == 0 else gp_s]))
        m["cdw"] = np.ascontiguousarray(np.stack([cdw_n, cdw_n if r == 0 else cdw_f]))
        m["s0"] = np.ascontiguousarray(state_delta[b, 0, r])
        sel = np.zeros((128, 2), np.float32)
        sel[:, 1 - r] = 1.0
        m["sel"] = sel
        in_maps.append(m)
    return in_maps


def kernel(**inputs):
    in_maps = _prep_inputs(inputs)
    if "nc" not in _NC_CACHE:
        _NC_CACHE["nc"] = build_program(True)
    res = run_bass_kernel_spmd(_NC_CACHE["nc"], in_maps, core_ids=list(range(8)))
    y_prompt = np.zeros((32, 256, 1024), np.float32)
    y_sample = np.zeros((4, 2048, 1024), np.float32)
    new_state = np.zeros((32, 1, 2, 8, 128, 128), np.float32)
    for core in range(8):
        r_ = res.results[core]
        b, r = core // 2, core % 2
        y_prompt[4 * core:4 * core + 4] = r_["yp"].reshape(4, 256, 1024)
        if r == 0:
            y_sample[b, 0:1024] = r_["ys"]
        else:
            y_sample[b, 1024:2048] = r_["ys"][::-1]
        new_state[4 * core:4 * core + 4, 0] = r_["st"]
    return (y_prompt, y_sample, new_state)
```
